# Optimizing a Trainium2 kernel written in Bass

```python
import math
import jax, jax.numpy as jnp
from jax import lax
import numpy as np

D_MODEL = 4096
BATCH = 4
SEQ = 2048
DEPTH = 2
DEC_BATCH = 128
DEC_SEQ = 1
PAST_LEN = 16384
PAGE_SIZE = 128

N_MIXERS = 2
N_CONF = (DEPTH + 1) // 2
N_GDN = DEPTH // 2
PLE_DIM = 256
D_FF = -(-8 * D_MODEL // (3 * 256)) * 256
CONF_WIDTH = 31
D_CONV = D_MODEL
GDN_HEADS = D_MODEL // 128
GDN_DK = 128
GDN_DV = 128
SHORT_CONV = 4
CHUNK = 64
QKV_DIM = GDN_HEADS * (2 * GDN_DK + GDN_DV)
GDN_PROJ = QKV_DIM + GDN_HEADS * GDN_DV + 2 * GDN_HEADS
RMS_EPS = 1e-6
LN_EPS = 1e-5
L2_EPS = 1e-6

kernel_name = 'hybrid_conformer_gdn_decoder_step'


def rmsnorm(x, g, eps=RMS_EPS):
    xf = x.astype(jnp.float32)
    y = xf * lax.rsqrt(jnp.mean(xf * xf, axis=-1, keepdims=True) + eps)
    return (y * g.astype(jnp.float32)).astype(x.dtype)


def layernorm(x, g, b, eps=LN_EPS):
    xf = x.astype(jnp.float32)
    mu = jnp.mean(xf, axis=-1, keepdims=True)
    xc = xf - mu
    y = xc * lax.rsqrt(jnp.mean(xc * xc, axis=-1, keepdims=True) + eps)
    return (y * g.astype(jnp.float32) + b.astype(jnp.float32)).astype(x.dtype)


def l2norm(x, eps=L2_EPS):
    return x * lax.rsqrt(jnp.sum(x * x, axis=-1, keepdims=True) + eps)


def causal_depthwise_conv(x, buf, w):
    x_ext = jnp.concatenate([buf.astype(x.dtype), x], axis=1)
    y = lax.conv_general_dilated(x_ext, w[:, None, :].astype(x.dtype), (1,), 'VALID',
                                 dimension_numbers=('NWC', 'WIO', 'NWC'),
                                 feature_group_count=x.shape[-1])
    return y, x_ext[:, x_ext.shape[1] - (w.shape[0] - 1):]


def conformer_conv(u, buf, w_pw1, w_dw, b_dw, ln_g, ln_b, w_pw2):
    a, gt = jnp.split(u @ w_pw1, 2, axis=-1)
    glu = a * jax.nn.sigmoid(gt)
    c, new_buf = causal_depthwise_conv(glu, buf, w_dw)
    c = layernorm(c + b_dw, ln_g, ln_b)
    return jax.nn.silu(c) @ w_pw2, new_buf


def gated_delta_rule_chunked(q, k, v, g, beta, S0):
    B_, T, H, _ = q.shape
    DVv = v.shape[-1]
    C = min(CHUNK, T)
    pad = (-T) % C
    if pad:
        padt = lambda a: jnp.pad(a, [(0, 0), (0, pad)] + [(0, 0)] * (a.ndim - 2))
        q, k, v, g, beta = padt(q), padt(k), padt(v), padt(g), padt(beta)
    N = (T + pad) // C

    def to_chunks(a):
        a = a.reshape((B_, N, C) + a.shape[2:])
        return jnp.moveaxis(a, (1, 3), (0, 2))

    qc, kc, vc, gc, bc = to_chunks(q), to_chunks(k), to_chunks(v), to_chunks(g), to_chunks(beta)
    G = jnp.cumsum(gc, axis=-1)
    causal = jnp.tril(jnp.ones((C, C), dtype=bool))
    strict = jnp.tril(jnp.ones((C, C), dtype=bool), -1)
    diff = G[..., :, None] - G[..., None, :]
    decay = jnp.where(causal, jnp.exp(jnp.where(causal, diff, 0.0)), 0.0)
    kb = kc * bc[..., None]
    Bm = jnp.where(strict, jnp.einsum('nbhik,nbhjk->nbhij', kb, kc) * decay, 0.0)
    rhs = jnp.concatenate([vc * bc[..., None], kb * jnp.exp(G)[..., None]], axis=-1)
    sol = lax.linalg.triangular_solve(Bm, rhs, left_side=True, lower=True, unit_diagonal=True)
    value, kcd = sol[..., :DVv], sol[..., DVv:]
    attn = jnp.einsum('nbhik,nbhjk->nbhij', qc, kc) * decay
    q_dec = qc * jnp.exp(G)[..., None]
    k_dec = kc * jnp.exp(G[..., -1:] - G)[..., None]
    a_last = jnp.exp(G[..., -1])

    def step(S, xs):
        value_n, kcd_n, attn_n, qd_n, kd_n, al_n = xs
        u = value_n - jnp.einsum('bhck,bhkv->bhcv', kcd_n, S)
        o = jnp.einsum('bhck,bhkv->bhcv', qd_n, S) + jnp.einsum('bhij,bhjv->bhiv', attn_n, u)
        S = S * al_n[..., None, None] + jnp.einsum('bhck,bhcv->bhkv', kd_n, u)
        return S, o

    S_fin, o = lax.scan(step, S0, (value, kcd, attn, q_dec, k_dec, a_last))
    o = jnp.moveaxis(o, (0, 2), (1, 3)).reshape(B_, N * C, H, DVv)[:, :T]
    return o, S_fin


def gated_deltanet(u, conv_buf, S0, w_in, w_conv, a_log, dt_bias, g_onorm, w_out):
    B_, T, _ = u.shape
    H, DK, DV = GDN_HEADS, GDN_DK, GDN_DV
    proj = u @ w_in
    qkv, z, b_logit, a_logit = jnp.split(proj, [QKV_DIM, QKV_DIM + H * DV, QKV_DIM + H * DV + H], axis=-1)
    qkv_c, new_buf = causal_depthwise_conv(qkv, conv_buf, w_conv)
    qkv_c = jax.nn.silu(qkv_c).astype(jnp.float32)
    q, k, v = jnp.split(qkv_c, [H * DK, 2 * H * DK], axis=-1)
    q = l2norm(q.reshape(B_, T, H, DK)) * (DK ** -0.5)
    k = l2norm(k.reshape(B_, T, H, DK))
    v = v.reshape(B_, T, H, DV)
    beta = jax.nn.sigmoid(b_logit.astype(jnp.float32))
    g = -jnp.exp(a_log.astype(jnp.float32)) * jax.nn.softplus(a_logit.astype(jnp.float32) + dt_bias.astype(jnp.float32))
    o, S_new = gated_delta_rule_chunked(q, k, v, g, beta, S0.astype(jnp.float32))
    o = rmsnorm(o, g_onorm) * jax.nn.silu(z.reshape(B_, T, H, DV).astype(jnp.float32))
    out = o.reshape(B_, T, H * DV).astype(u.dtype) @ w_out
    return out, new_buf, S_new.astype(S0.dtype)


def swiglu(u, w_gate, w_up, w_down):
    return (jax.nn.silu(u @ w_gate) * (u @ w_up)) @ w_down


def trunk(x, p, conf_buf, qkv_buf, gdn_state, prm):
    h = x
    conf_new, qkv_new, gdn_new = [], [], []
    for i in range(DEPTH):
        j = i // N_MIXERS
        u = rmsnorm(h, prm['g_mix'][i])
        if i % N_MIXERS == 0:
            out, nb = conformer_conv(u, conf_buf[j], prm['conf_w_pw1'][j], prm['conf_w_dw'][j], prm['conf_b_dw'][j],
                                     prm['conf_ln_g'][j], prm['conf_ln_b'][j], prm['conf_w_pw2'][j])
            conf_new.append(nb)
        else:
            out, nb, ns = gated_deltanet(u, qkv_buf[j], gdn_state[j], prm['gdn_w_in'][j], prm['gdn_w_conv'][j],
                                         prm['gdn_a_log'][j], prm['gdn_dt_bias'][j], prm['gdn_g_onorm'][j], prm['gdn_w_out'][j])
            qkv_new.append(nb)
            gdn_new.append(ns)
        h = h + out
        h = h + swiglu(rmsnorm(h, prm['g_ffn'][i]), prm['ffn_w_gate'][i], prm['ffn_w_up'][i], prm['ffn_w_down'][i])
        gate = jax.nn.sigmoid(rmsnorm(h, prm['g_ple'][i]) @ prm['ple_w_gate'][i])
        h = h + gate * (p[i].astype(h.dtype) @ prm['ple_w_proj'][i])
    return rmsnorm(h, prm['g_final']), jnp.stack(conf_new), jnp.stack(qkv_new), jnp.stack(gdn_new)


def _normal(key, shape, scale):
    return jax.random.normal(key, shape, jnp.float32) * scale


def setup_inputs(seed: int = 0) -> dict:
    key = jax.random.key(seed)
    k = jax.random.split(key, 32)
    d, f, h = D_MODEL, D_FF, GDN_HEADS
    gain = lambda kk, shape: 1.0 + _normal(kk, shape, 0.02)
    x_prompt = _normal(k[0], (BATCH, SEQ, d), 1.0)
    x_sample = _normal(k[1], (DEC_BATCH, DEC_SEQ, d), 1.0)
    p_prompt = _normal(k[2], (DEPTH, BATCH, SEQ, PLE_DIM), 1.0)
    p_sample = _normal(k[3], (DEPTH, DEC_BATCH, DEC_SEQ, PLE_DIM), 1.0)
    state_conv_conformer = _normal(k[4], (N_CONF, DEC_BATCH, CONF_WIDTH - 1, D_CONV), 0.5)
    state_conv_qkv = _normal(k[5], (N_GDN, DEC_BATCH, SHORT_CONV - 1, QKV_DIM), 1.0)
    state_gdn = _normal(k[6], (N_GDN, DEC_BATCH, h, GDN_DK, GDN_DV), GDN_DK ** -0.5)
    g_mix = gain(k[7], (DEPTH, d))
    g_ffn = gain(k[8], (DEPTH, d))
    g_ple = gain(k[9], (DEPTH, d))
    g_final = gain(k[10], (d,))
    conf_w_pw1 = _normal(k[11], (N_CONF, d, 2 * D_CONV), d ** -0.5)
    conf_w_dw = _normal(k[12], (N_CONF, CONF_WIDTH, D_CONV), CONF_WIDTH ** -0.5)
    conf_b_dw = _normal(k[13], (N_CONF, D_CONV), 0.02)
    conf_ln_g = gain(k[14], (N_CONF, D_CONV))
    conf_ln_b = _normal(k[15], (N_CONF, D_CONV), 0.02)
    conf_w_pw2 = _normal(k[16], (N_CONF, D_CONV, d), D_CONV ** -0.5)
    gdn_w_in = _normal(k[17], (N_GDN, d, GDN_PROJ), d ** -0.5)
    gdn_w_conv = _normal(k[18], (N_GDN, SHORT_CONV, QKV_DIM), SHORT_CONV ** -0.5)
    gdn_a_log = jnp.log(jax.random.uniform(k[19], (N_GDN, h), jnp.float32, 1.0, 16.0))
    dt = jnp.exp(jax.random.uniform(k[20], (N_GDN, h), jnp.float32, math.log(1e-3), math.log(1e-1)))
    gdn_dt_bias = dt + jnp.log(-jnp.expm1(-dt))
    gdn_g_onorm = gain(k[21], (N_GDN, GDN_DV))
    gdn_w_out = _normal(k[22], (N_GDN, h * GDN_DV, d), (h * GDN_DV) ** -0.5)
    ffn_w_gate = _normal(k[23], (DEPTH, d, f), d ** -0.5)
    ffn_w_up = _normal(k[24], (DEPTH, d, f), d ** -0.5)
    ffn_w_down = _normal(k[25], (DEPTH, f, d), f ** -0.5)
    ple_w_gate = _normal(k[26], (DEPTH, d, d), d ** -0.5)
    ple_w_proj = _normal(k[27], (DEPTH, PLE_DIM, d), PLE_DIM ** -0.5)
    return {'x_prompt': x_prompt, 'x_sample': x_sample, 'p_prompt': p_prompt, 'p_sample': p_sample,
            'state_conv_conformer': state_conv_conformer, 'state_conv_qkv': state_conv_qkv, 'state_gdn': state_gdn,
            'g_mix': g_mix, 'g_ffn': g_ffn, 'g_ple': g_ple, 'g_final': g_final,
            'conf_w_pw1': conf_w_pw1, 'conf_w_dw': conf_w_dw, 'conf_b_dw': conf_b_dw,
            'conf_ln_g': conf_ln_g, 'conf_ln_b': conf_ln_b, 'conf_w_pw2': conf_w_pw2,
            'gdn_w_in': gdn_w_in, 'gdn_w_conv': gdn_w_conv, 'gdn_a_log': gdn_a_log, 'gdn_dt_bias': gdn_dt_bias,
            'gdn_g_onorm': gdn_g_onorm, 'gdn_w_out': gdn_w_out,
            'ffn_w_gate': ffn_w_gate, 'ffn_w_up': ffn_w_up, 'ffn_w_down': ffn_w_down,
            'ple_w_gate': ple_w_gate, 'ple_w_proj': ple_w_proj}


def reference(x_prompt, x_sample, p_prompt, p_sample, state_conv_conformer, state_conv_qkv, state_gdn,
              g_mix, g_ffn, g_ple, g_final, conf_w_pw1, conf_w_dw, conf_b_dw, conf_ln_g, conf_ln_b, conf_w_pw2,
              gdn_w_in, gdn_w_conv, gdn_a_log, gdn_dt_bias, gdn_g_onorm, gdn_w_out,
              ffn_w_gate, ffn_w_up, ffn_w_down, ple_w_gate, ple_w_proj):
    prm = dict(g_mix=g_mix, g_ffn=g_ffn, g_ple=g_ple, g_final=g_final,
               conf_w_pw1=conf_w_pw1, conf_w_dw=conf_w_dw, conf_b_dw=conf_b_dw,
               conf_ln_g=conf_ln_g, conf_ln_b=conf_ln_b, conf_w_pw2=conf_w_pw2,
               gdn_w_in=gdn_w_in, gdn_w_conv=gdn_w_conv, gdn_a_log=gdn_a_log, gdn_dt_bias=gdn_dt_bias,
               gdn_g_onorm=gdn_g_onorm, gdn_w_out=gdn_w_out,
               ffn_w_gate=ffn_w_gate, ffn_w_up=ffn_w_up, ffn_w_down=ffn_w_down,
               ple_w_gate=ple_w_gate, ple_w_proj=ple_w_proj)
    bp = x_prompt.shape[0]
    conf0 = jnp.zeros((N_CONF, bp, CONF_WIDTH - 1, D_CONV), x_prompt.dtype)
    qkv0 = jnp.zeros((N_GDN, bp, SHORT_CONV - 1, QKV_DIM), x_prompt.dtype)
    gdn0 = jnp.zeros((N_GDN, bp, GDN_HEADS, GDN_DK, GDN_DV), state_gdn.dtype)
    y_prompt, conf_p, qkv_p, gdn_p = trunk(x_prompt, p_prompt, conf0, qkv0, gdn0, prm)
    y_sample, conf_s, qkv_s, gdn_s = trunk(x_sample, p_sample, state_conv_conformer, state_conv_qkv, state_gdn, prm)
    return (y_prompt, y_sample, conf_p, qkv_p, gdn_p, conf_s, qkv_s, gdn_s)
```

```python
import contextlib
import numpy as np
import concourse.bass as bass
import concourse.mybir as mybir
from concourse.bass_utils import run_bass_kernel_spmd

F32 = mybir.dt.float32
BF16 = mybir.dt.bfloat16
AF = mybir.ActivationFunctionType
ALU = mybir.AluOpType
AX = mybir.AxisListType

RMS_EPS = 1e-6
LN_EPS = 1e-5
L2_EPS = 1e-6
CW = 31
SC = 4
C64 = 64
NEG = 30000.0


class Cfg:
    def __init__(self, D=4096, F=11008, H=32, PLE=256, SEQ=2048, NS=16, TT=256, NCORES=8, NWB=3):
        self.D, self.F, self.H, self.PLE, self.SEQ, self.NS, self.TT = D, F, H, PLE, SEQ, NS, TT
        self.NCORES, self.NWB = NCORES, NWB
        self.KC = D // 128
        self.FC = F // 128
        self.QC = 3 * H
        self.QKV = H * 384
        self.PROJ = self.QKV + H * 128 + 2 * H
        self.PC = PLE // 128
        self.NT = SEQ // TT
        self.NCH = TT // C64
        ng = -(-self.FC // 32)
        base = self.FC // ng
        self.FG = []
        s = 0
        for i in range(ng):
            n = base + (1 if i < self.FC - base * ng else 0)
            self.FG.append((s, n))
            s += n
        self.FGMAX = max(n for _, n in self.FG)


class Sched:
    ENG = ('pe', 'act', 'dve', 'pool', 'sp')

    def __init__(self, nc, stack):
        self.nc = nc
        self.stack = stack
        self.eng = dict(pe=nc.tensor, act=nc.scalar, dve=nc.vector, pool=nc.gpsimd, sp=nc.sync)
        self.prog = {e: [] for e in self.ENG}
        self.psem = {e: stack.enter_context(nc.semaphore(f"prog_{e}")) for e in self.ENG if e != 'sp'}
        self.pcnt = {e: 0 for e in self.ENG}
        self.seen = {e: {} for e in self.ENG}
        self.last_w = {}
        self.readers = {}
        self.dsem = {}
        self.dcnt = {}
        self.dry = False
        self.ns = None
        self.cap = None

    def _deps(self, e, reads, writes):
        best = {}
        for k in reads:
            t = self.last_w.get(k)
            if t is not None and best.get(t[0], 0) < t[1]:
                best[t[0]] = t[1]
        for k in writes:
            t = self.last_w.get(k)
            if t is not None and best.get(t[0], 0) < t[1]:
                best[t[0]] = t[1]
            for t in self.readers.get(k, ()):
                if best.get(t[0], 0) < t[1]:
                    best[t[0]] = t[1]
        seen = self.seen[e]
        for s, v in best.items():
            if seen.get(s, 0) < v:
                seen[s] = v
                self.prog[e].append(('w', s, v))

    def _commit(self, tok, reads, writes):
        for k in writes:
            self.last_w[k] = tok
            self.readers[k] = []
        for k in reads:
            if k in writes:
                continue
            self.readers.setdefault(k, []).append(tok)

    PRIV = ('knb', 'qnb', 'qrb', 'Sbf')

    def _nsk(self, k):
        if isinstance(k, str):
            return (k, 'ns', self.ns) if (k.startswith('t_') or k in self.PRIV) else k
        if isinstance(k, tuple) and isinstance(k[0], str) and k[0].startswith('t_'):
            return k + ('ns', self.ns)
        return k

    def op(self, e, fn, reads=(), writes=()):
        if self.dry:
            return
        if self.ns is not None:
            reads = [self._nsk(k) for k in reads]
            writes = [self._nsk(k) for k in writes]
        if self.cap is not None:
            self.cap.append((e, fn, reads, writes))
            return
        px = [k for k in reads if isinstance(k, tuple) and k[0] == 'ps']
        if px:
            reads = [k for k in reads if not (isinstance(k, tuple) and k[0] == 'ps')]
            writes = list(writes) + [k for k in px if k not in writes]
        self._deps(e, reads, writes)
        self.pcnt[e] += 1
        tok = (('p', e), self.pcnt[e])
        self.prog[e].append(('o', fn, ('p', e)))
        self._commit(tok, reads, writes)

    def dma(self, e, sem_name, fn, reads=(), writes=()):
        if self.dry:
            return
        if sem_name not in self.dsem:
            self.dsem[sem_name] = self.stack.enter_context(self.nc.semaphore(f"dma_{sem_name}"))
            self.dcnt[sem_name] = 0
        self._deps(e, reads, writes)
        self.dcnt[sem_name] += 16
        tok = (('d', sem_name), self.dcnt[sem_name])
        self.prog[e].append(('d', fn, ('d', sem_name)))
        self._commit(tok, reads, writes)

    def wait_all(self, e):
        allt = [(('p', x), self.pcnt[x]) for x in self.psem] + [(('d', n), c) for n, c in self.dcnt.items()]
        for s, c in allt:
            if c > 0 and self.seen[e].get(s, 0) < c:
                self.seen[e][s] = c
                self.prog[e].append(('w', s, c))

    def _sem(self, s):
        return self.psem[s[1]] if s[0] == 'p' else self.dsem[s[1]]

    def emit(self):
        nc = self.nc

        def run(e):
            eng = self.eng[e]
            for it in self.prog[e]:
                if it[0] == 'w':
                    eng.wait_ge(self._sem(it[1]), it[2])
                elif it[0] == 'o':
                    it[1](eng).then_inc(self._sem(it[2]), 1)
                else:
                    it[1](eng).then_inc(self._sem(it[2]), 16)

        with nc.Block() as block:
            @block.tensor
            def _(eng):
                run('pe')

            @block.scalar
            def _(eng):
                run('act')

            @block.vector
            def _(eng):
                run('dve')

            @block.gpsimd
            def _(eng):
                run('pool')

            @block.sync
            def _(eng):
                run('sp')
        self.prog = {e: [] for e in self.ENG}


class Prog:
    def __init__(self, cfg):
        self.c = cfg
        self.nc = bass.Bass("TRN2", target_bir_lowering=False)
        self.uid = 0
        self.aux_set = None

    def sb(self, name, shape, dt=F32):
        return self.st.enter_context(self.nc.sbuf_tensor("sb_" + name, list(shape), dt))

    def din(self, name, shape, dt=F32):
        return self.nc.dram_tensor(name, list(shape), dt, kind="ExternalInput").ap()

    def dout(self, name, shape, dt=F32):
        return self.nc.dram_tensor(name, list(shape), dt, kind="ExternalOutput").ap()

    def ps(self, kind):
        if kind == 'lin':
            r = self.ps_lin_i % 4
            self.ps_lin_i += 1
            return self.psb[r][:, 0:256], ('ps', r)
        if self.aux_set is not None:
            st_ = self.aux_set
            r = st_[self.ps_aux_i % len(st_)]
        else:
            r = self.ps_aux_i % 4
        self.ps_aux_i += 1
        return self.psb[4 + r][:, 0:256], ('ps', 4 + r)

    def tmp(self, kind):
        n = len(self.tmps)
        i = self.tmp_i % n
        self.tmp_i += 1
        return self.tmps[i], ('tmp', i)

    def wget(self, src, nk, ncols, tag, hold=0):
        c = self.c
        if self.S.dry:
            self.plan.append((src, nk, ncols, tag))
            return None, None
        i = self.wpos
        self.wpos += 1
        assert self.plan[i % len(self.plan)][3] == tag, (self.plan[i % len(self.plan)][3], tag)
        self._wissue(min(i - hold + c.NWB - 1, self.wtotal - 1))
        slot = i % c.NWB
        return self.wb[slot], ('wb', slot)

    def _wissue(self, upto):
        c = self.c
        S = self.S
        NP = len(self.plan)
        while self.wissued <= upto:
            g = self.wissued
            self.wissued += 1
            src, nk, ncols, tag = self.plan[g % NP]
            slot = g % c.NWB
            pid = g % NP
            dst = self.wb[slot][:, 0:nk, 0:ncols]
            cache = self.wcache[pid]
            cview = cache[:, 0:nk * ncols].rearrange("p (k n) -> p k n", k=nk)
            if g < NP:
                srcv = src.rearrange("(kc p) n -> p kc n", p=128)
                S.dma('pool', f'wl{slot}', (lambda e, dst=dst, srcv=srcv: e.dma_start(out=dst, in_=srcv)),
                      writes=[('wb', slot)])
                if self.wtotal > NP:
                    S.dma('sp', f'wc{slot}', (lambda e, dst=dst, cview=cview: e.dma_start(out=cview, in_=dst)),
                          reads=[('wb', slot)], writes=[('wcache', pid)])
            else:
                S.dma('sp', f'wh{slot}', (lambda e, dst=dst, cview=cview: e.dma_start(out=dst, in_=cview)),
                      reads=[('wcache', pid)], writes=[('wb', slot)])

    def mm_acc(self, psap, pskey, parts, W, extra_reads=()):
        S = self.S
        if S.dry:
            return
        seq = []
        reads = list(extra_reads)
        for (wt, wk, kcs, moff, msz, it, ikcs, ikeys) in parts:
            for a, b in zip(kcs, ikcs):
                seq.append((wt[:, a, moff:moff + msz], it[:, b, 0:W]))
            reads.append(wk)
            reads.extend(ikeys)
        n = len(seq)

        def fn(e, seq=seq, psap=psap, W=W, n=n):
            ins = None
            for i, (l, r) in enumerate(seq):
                ins = e.matmul(psap[0:l.shape[-1], 0:W], lhsT=l, rhs=r, start=(i == 0), stop=(i == n - 1))
            return ins
        S.op('pe', fn, reads=reads, writes=[pskey])

    def linear(self, Wsrc, K_chunks, col0, ncols_total, in_tile, in_key_fn, W, cb, tag, k0=0):
        npan = -(-ncols_total // 256)
        for pi in range(npan):
            c0 = col0 + pi * 256
            ncol = min(256, col0 + ncols_total - c0)
            subs = []
            kk = 0
            while kk < K_chunks:
                nk = min(32, K_chunks - kk)
                src = Wsrc[(k0 + kk) * 128:(k0 + kk + nk) * 128, c0:c0 + ncol]
                wt, wk = self.wget(src, nk, ncol, (tag, pi, kk), hold=len(subs))
                subs.append((wt, wk, kk, nk))
                kk += nk
            for mi in range(ncol // 128):
                if self.S.dry:
                    continue
                psap, pskey = self.ps('lin')
                parts = []
                for (wt, wk, kk, nk) in subs:
                    parts.append((wt, wk, list(range(nk)), mi * 128, 128, in_tile,
                                  list(range(kk, kk + nk)), [in_key_fn(k) for k in range(kk, kk + nk)]))
                self.mm_acc(psap, pskey, parts, W)
                cb(pi * 2 + mi, psap, pskey)

    def colsum_sq(self, src_fn, nchunks, W, scale_inv, eps, out_rstd, out_key):
        S = self.S
        if S.dry:
            return
        psap, pskey = self.ps('aux')
        for k in range(nchunks):
            sap, skey = src_fn(k)
            t, tk = self.tmp('sq')
            S.op('act', (lambda e, t=t, sap=sap, W=W: e.activation(out=t[:, 0:W], in_=sap, func=AF.Square)),
                 reads=[skey], writes=[tk])
            S.op('pe', (lambda e, t=t, psap=psap, W=W, k=k, n=nchunks: e.matmul(
                psap[:, 0:W], lhsT=self.ones_f[:, :], rhs=t[:, 0:W], start=(k == 0), stop=(k == n - 1))),
                 reads=[tk, 'const'], writes=[pskey])
        S.op('dve', (lambda e, psap=psap, W=W: e.tensor_scalar(out=out_rstd[:, 0:W], in0=psap[:, 0:W], scalar1=scale_inv,
                                                           scalar2=eps, op0=ALU.mult, op1=ALU.add)),
             reads=[pskey], writes=[out_key])
        self.rsqrt_(out_rstd[:, 0:W], out_key)

    def rsqrt_(self, ap, key):
        S = self.S
        S.op('act', (lambda e, ap=ap: e.activation(out=ap, in_=ap, func=AF.Sqrt)), reads=[key], writes=[key])
        S.op('dve', (lambda e, ap=ap: e.reciprocal(out=ap, in_=ap)), reads=[key], writes=[key])

    def rmsnorm(self, gtile, gl, W):
        c = self.c
        S = self.S
        if S.dry:
            return
        h, u = self.h, self.u
        self.colsum_sq(lambda k: (h[:, k, 0:W], ('h', k)), c.KC, W, 1.0 / c.D, RMS_EPS, self.rstd, 'rstd')
        for k in range(c.KC):
            S.op('dve', (lambda e, k=k, W=W: e.scalar_tensor_tensor(
                out=u[:, k, 0:W], in0=h[:, k, 0:W], scalar=gtile[:, gl, k:k + 1], in1=self.rstd[:, 0:W],
                op0=ALU.mult, op1=ALU.mult)), reads=[('h', k), 'rstd', 'const'], writes=[('u', k)])

    def conformer(self, W, sample, t0):
        c = self.c
        S = self.S
        KC = c.KC
        h, u = self.h, self.u
        Wp1 = self.w['conf_w_pw1'][0]
        Wp2 = self.w['conf_w_pw2'][0]
        cbuf = self.scr
        glub = self.scr_glu
        ps_mean = ps_var = None
        if not S.dry:
            ps_mean, km = self.ps('aux')
            ps_var, kv = self.ps('aux')
        if sample and not S.dry:
            pass
        for j in range(KC // 2):
            a_src = Wp1[:, j * 256:(j + 1) * 256]
            g_src = Wp1[:, c.D + j * 256:c.D + (j + 1) * 256]
            wa, wak = self.wget(a_src, KC, 256, ('pw1a', j))
            wg, wgk = self.wget(g_src, KC, 256, ('pw1g', j), hold=1)
            if S.dry:
                continue
            for mi in range(2):
                ch = 2 * j + mi
                gi = ch % 4
                pa, pak = self.ps('lin')
                pg, pgk = self.ps('lin')
                ukeys = [('u', k) for k in range(KC)]
                self.mm_acc(pa, pak, [(wa, wak, list(range(KC)), mi * 128, 128, u, list(range(KC)), ukeys)], W)
                self.mm_acc(pg, pgk, [(wg, wgk, list(range(KC)), mi * 128, 128, u, list(range(KC)), ukeys)], W)
                sg, sgk = self.tmp('sig')
                S.op('act', (lambda e, sg=sg, pg=pg, W=W: e.activation(out=sg[:, 0:W], in_=pg[:, 0:W], func=AF.Sigmoid)),
                     reads=[pgk], writes=[sgk])
                acc, acck = self.tmp('acc')
                if not sample:
                    gk = ('glu', gi)
                    S.op('act', (lambda e, gi=gi, ch=ch: e.activation(out=glub[:, gi, 0:30], in_=self.chalo[:, ch, :],
                                                                     func=AF.Copy)),
                         reads=[('chalo', ch)], writes=[gk])
                    S.op('dve', (lambda e, gi=gi, pa=pa, sg=sg, W=W: e.tensor_tensor(
                        out=glub[:, gi, 30:30 + W], in0=pa[:, 0:W], in1=sg[:, 0:W], op=ALU.mult)),
                         reads=[pak, sgk, gk], writes=[gk])
                    S.op('act', (lambda e, gi=gi, ch=ch, W=W: e.activation(out=self.chalo[:, ch, :],
                                                                          in_=glub[:, gi, W:W + 30], func=AF.Copy)),
                         reads=[gk], writes=[('chalo', ch)])
                    for w in range(CW):
                        if w == 0:
                            S.op('dve', (lambda e, acc=acc, gi=gi, ch=ch, W=W: e.tensor_scalar(
                                out=acc[:, 0:W], in0=glub[:, gi, 0:W], scalar1=self.cwdw[:, ch, 0:1],
                                scalar2=self.cbdw[:, ch:ch + 1], op0=ALU.mult, op1=ALU.add)),
                                 reads=[gk, 'const'], writes=[acck])
                        else:
                            S.op('dve', (lambda e, acc=acc, gi=gi, ch=ch, W=W, w=w: e.scalar_tensor_tensor(
                                out=acc[:, 0:W], in0=glub[:, gi, w:w + W], scalar=self.cwdw[:, ch, w:w + 1],
                                in1=acc[:, 0:W], op0=ALU.mult, op1=ALU.add)),
                                 reads=[gk, acck, 'const'], writes=[acck])
                else:
                    NS = c.NS
                    st, stk = self.cs_st[ch % 2], ('csst', ch % 2)
                    nst, nstk = self.cs_new[ch % 2], ('csnew', ch % 2)
                    S.dma('pool', f'csin{ch % 2}', (lambda e, st=st, ch=ch: e.dma_start(out=st[:, :, :], in_=self.d_cs_in[:, ch, :, :])),
                          writes=[stk])
                    gl, glk = self.tmp('glu_s')
                    S.op('dve', (lambda e, gl=gl, pa=pa, sg=sg, W=W: e.tensor_tensor(
                        out=gl[:, 0:W], in0=pa[:, 0:W], in1=sg[:, 0:W], op=ALU.mult)),
                         reads=[pak, sgk], writes=[glk])
                    S.op('act', (lambda e, st=st, nst=nst: e.activation(out=nst[:, :, 0:29], in_=st[:, :, 1:30], func=AF.Copy)),
                         reads=[stk], writes=[nstk])
                    S.op('act', (lambda e, gl=gl, nst=nst, W=W: e.activation(out=nst[:, :, 29], in_=gl[:, 0:W], func=AF.Copy)),
                         reads=[glk, nstk], writes=[nstk])
                    S.dma('pool', f'csout{ch % 2}', (lambda e, nst=nst, ch=ch: e.dma_start(out=self.d_conf_s[:, ch, :, :], in_=nst[:, :, :])),
                          reads=[nstk], writes=[('d_conf_s', ch)])
                    pr, prk = self.cs_prod, 'csprod'
                    S.op('dve', (lambda e, st=st, ch=ch, pr=pr, NS=NS: e.tensor_tensor(
                        out=pr[:, :, :], in0=st[:, :, :], in1=self.cwdw[:, ch, 0:30].unsqueeze(1).to_broadcast([128, NS, 30]),
                        op=ALU.mult)), reads=[stk, 'const'], writes=[prk])
                    S.op('dve', (lambda e, acc=acc, pr=pr, W=W: e.tensor_reduce(out=acc[:, 0:W], in_=pr[:, :, :], axis=AX.X, op=ALU.add)),
                         reads=[prk], writes=[acck])
                    S.op('dve', (lambda e, acc=acc, gl=gl, ch=ch, W=W: e.scalar_tensor_tensor(
                        out=acc[:, 0:W], in0=gl[:, 0:W], scalar=self.cwdw[:, ch, 30:31], in1=acc[:, 0:W],
                        op0=ALU.mult, op1=ALU.add)), reads=[glk, acck, 'const'], writes=[acck])
                    S.op('dve', (lambda e, acc=acc, ch=ch, W=W: e.tensor_scalar(
                        out=acc[:, 0:W], in0=acc[:, 0:W], scalar1=self.cbdw[:, ch:ch + 1], scalar2=None, op0=ALU.add)),
                         reads=[acck, 'const'], writes=[acck])
                S.op('pe', (lambda e, acc=acc, W=W, ch=ch: e.matmul(ps_mean[:, 0:W], lhsT=self.ones_f[:, :], rhs=acc[:, 0:W],
                                                               start=(ch == 0), stop=(ch == KC - 1))),
                     reads=[acck, 'const'], writes=[km])
                sq, sqk = self.tmp('sq')
                S.op('act', (lambda e, sq=sq, acc=acc, W=W: e.activation(out=sq[:, 0:W], in_=acc[:, 0:W], func=AF.Square)),
                     reads=[acck], writes=[sqk])
                S.op('pe', (lambda e, sq=sq, W=W, ch=ch: e.matmul(ps_var[:, 0:W], lhsT=self.ones_f[:, :], rhs=sq[:, 0:W],
                                                             start=(ch == 0), stop=(ch == KC - 1))),
                     reads=[sqk, 'const'], writes=[kv])
                S.op('act', (lambda e, acc=acc, W=W, ch=ch: e.activation(out=cbuf[:, ch, 0:W], in_=acc[:, 0:W], func=AF.Copy)),
                     reads=[acck], writes=[('scr', ch)])
        if not S.dry:
            mean, var = self.mean, self.rstd
            invD = 1.0 / c.D
            S.op('dve', (lambda e, W=W: e.tensor_scalar(out=mean[:, 0:W], in0=ps_mean[:, 0:W], scalar1=invD, scalar2=None,
                                                      op0=ALU.mult)), reads=[km], writes=['mean'])
            msq, msqk = self.tmp('msq')
            S.op('dve', (lambda e, W=W, msq=msq: e.tensor_tensor(out=msq[:, 0:W], in0=mean[:, 0:W], in1=mean[:, 0:W], op=ALU.mult)),
                 reads=['mean'], writes=[msqk])
            S.op('dve', (lambda e, W=W, msq=msq: e.scalar_tensor_tensor(out=var[:, 0:W], in0=ps_var[:, 0:W], scalar=invD,
                                                                      in1=msq[:, 0:W], op0=ALU.mult, op1=ALU.subtract)),
                 reads=[kv, msqk], writes=['rstd'])
            S.op('dve', (lambda e, W=W: e.tensor_scalar(out=var[:, 0:W], in0=var[:, 0:W], scalar1=LN_EPS, scalar2=None,
                                                      op0=ALU.add)), reads=['rstd'], writes=['rstd'])
            self.rsqrt_(var[:, 0:W], 'rstd')
            for ch in range(KC):
                t1, t1k = self.tmp('ln1')
                S.op('dve', (lambda e, t1=t1, ch=ch, W=W: e.tensor_tensor(out=t1[:, 0:W], in0=cbuf[:, ch, 0:W], in1=mean[:, 0:W],
                                                                        op=ALU.subtract)),
                     reads=[('scr', ch), 'mean'], writes=[t1k])
                S.op('dve', (lambda e, t1=t1, W=W: e.tensor_tensor(out=t1[:, 0:W], in0=t1[:, 0:W], in1=var[:, 0:W], op=ALU.mult)),
                     reads=[t1k, 'rstd'], writes=[t1k])
                S.op('act', (lambda e, t1=t1, ch=ch, W=W: e.activation(out=u[:, ch, 0:W], in_=t1[:, 0:W], func=AF.Silu,
                                                                     bias=self.clnb[:, ch:ch + 1], scale=self.clng[:, ch:ch + 1])),
                     reads=[t1k, 'const'], writes=[('u', ch)])

        def cb(mi, psap, pskey):
            S.op('dve', (lambda e, mi=mi, psap=psap, W=W: e.tensor_tensor(out=h[:, mi, 0:W], in0=h[:, mi, 0:W], in1=psap[:, 0:W],
                                                                        op=ALU.add)),
                 reads=[pskey, ('h', mi)], writes=[('h', mi)])
        self.linear(Wp2, KC, 0, c.D, u, lambda k: ('u', k), W, cb, 'pw2')

    def ffn(self, layer, W):
        c = self.c
        S = self.S
        KC = c.KC
        h, u = self.h, self.u
        Wg = self.w['ffn_w_gate'][layer]
        Wu = self.w['ffn_w_up'][layer]
        Wd = self.w['ffn_w_down'][layer]
        hid = self.scr
        ukeys = [('u', k) for k in range(KC)]
        for (f0, fn_) in c.FG:
            j = 0
            while j < fn_:
                ncol = min(2, fn_ - j) * 128
                c0 = (f0 + j) * 128
                wg, wgk = self.wget(Wg[:, c0:c0 + ncol], KC, ncol, ('ffg', layer, f0 + j))
                wu, wuk = self.wget(Wu[:, c0:c0 + ncol], KC, ncol, ('ffu', layer, f0 + j), hold=1)
                if not S.dry:
                    for mi in range(ncol // 128):
                        jj = j + mi
                        pg, pgk = self.ps('lin')
                        pu, puk = self.ps('lin')
                        self.mm_acc(pg, pgk, [(wg, wgk, list(range(KC)), mi * 128, 128, u, list(range(KC)), ukeys)], W)
                        self.mm_acc(pu, puk, [(wu, wuk, list(range(KC)), mi * 128, 128, u, list(range(KC)), ukeys)], W)
                        sg, sgk = self.tmp('silu')
                        S.op('act', (lambda e, sg=sg, pg=pg, W=W: e.activation(out=sg[:, 0:W], in_=pg[:, 0:W], func=AF.Silu)),
                             reads=[pgk], writes=[sgk])
                        S.op('dve', (lambda e, sg=sg, pu=pu, jj=jj, W=W: e.tensor_tensor(out=hid[:, jj, 0:W], in0=pu[:, 0:W],
                                                                                     in1=sg[:, 0:W], op=ALU.mult)),
                             reads=[puk, sgk], writes=[('scr', jj)])
                j += 2

            def cb(mi, psap, pskey):
                S.op('dve', (lambda e, mi=mi, psap=psap, W=W: e.tensor_tensor(out=h[:, mi, 0:W], in0=h[:, mi, 0:W],
                                                                            in1=psap[:, 0:W], op=ALU.add)),
                     reads=[pskey, ('h', mi)], writes=[('h', mi)])
            self.linear(Wd, fn_, 0, c.D, hid, lambda k: ('scr', k), W, cb, ('ffd', layer, f0), k0=f0)

    def ple(self, layer, W, sample, t0):
        c = self.c
        S = self.S
        h, u = self.h, self.u
        Wpg = self.w['ple_w_gate'][layer]
        Wpp = self.w['ple_w_proj'][layer]
        pT = self.pT
        if not S.dry:
            src = (self.d_psT[layer] if sample else self.d_ppT[layer][:, :, t0:t0 + W])
            S.dma('pool', 'pin', (lambda e, src=src, W=W: e.dma_start(out=pT[:, :, 0:W], in_=src)), writes=['pT'])
        pkeys = ['pT'] * c.PC
        for j in range(c.KC // 2):
            wg, wgk = self.wget(Wpg[:, j * 256:(j + 1) * 256], c.KC, 256, ('pleg', layer, j))
            wp, wpk = self.wget(Wpp[:, j * 256:(j + 1) * 256], c.PC, 256, ('plep', layer, j), hold=1)
            if S.dry:
                continue
            for mi in range(2):
                ch = 2 * j + mi
                pg, pgk = self.ps('lin')
                pp, ppk = self.ps('lin')
                self.mm_acc(pg, pgk, [(wg, wgk, list(range(c.KC)), mi * 128, 128, u, list(range(c.KC)),
                                       [('u', k) for k in range(c.KC)])], W)
                self.mm_acc(pp, ppk, [(wp, wpk, list(range(c.PC)), mi * 128, 128, pT, list(range(c.PC)), pkeys)], W)
                sg, sgk = self.tmp('sig')
                S.op('act', (lambda e, sg=sg, pg=pg, W=W: e.activation(out=sg[:, 0:W], in_=pg[:, 0:W], func=AF.Sigmoid)),
                     reads=[pgk], writes=[sgk])
                S.op('dve', (lambda e, sg=sg, pp=pp, W=W: e.tensor_tensor(out=sg[:, 0:W], in0=pp[:, 0:W], in1=sg[:, 0:W],
                                                                        op=ALU.mult)), reads=[ppk, sgk], writes=[sgk])
                S.op('dve', (lambda e, sg=sg, ch=ch, W=W: e.tensor_tensor(out=h[:, ch, 0:W], in0=h[:, ch, 0:W], in1=sg[:, 0:W],
                                                                        op=ALU.add)),
                     reads=[sgk, ('h', ch)], writes=[('h', ch)])

    def gdn_gates(self, W, ntok_blocks):
        c = self.c
        S = self.S
        H = c.H
        Win = self.w['gdn_w_in'][0]
        cb0 = c.QKV + H * 128
        if not S.dry:
            S.dma('pool', 'wba', (lambda e: e.dma_start(out=self.wba[:, :, :],
                                                       in_=Win[:, cb0:cb0 + 2 * H].rearrange("(kc p) n -> p kc n", p=128))),
                  writes=['wba'])
        if S.dry:
            return
        tb = min(W, C64)
        for blk in range(ntok_blocks):
            psap, pskey = self.ps('aux')
            u = self.u

            def fn(e, blk=blk, psap=psap, tb=tb):
                ins = None
                for k in range(c.KC):
                    ins = e.matmul(psap[0:tb, 0:2 * H], lhsT=u[:, k, blk * tb:(blk + 1) * tb], rhs=self.wba[:, k, :],
                                   start=(k == 0), stop=(k == c.KC - 1))
                return ins
            S.op('pe', fn, reads=['wba'] + [('u', k) for k in range(c.KC)], writes=[pskey])
            S.op('act', (lambda e, blk=blk, psap=psap, tb=tb: e.activation(out=self.g_beta[0:tb, blk, :], in_=psap[0:tb, 0:H],
                                                                         func=AF.Sigmoid)),
                 reads=[pskey], writes=['g_beta'])
            S.op('dve', (lambda e, blk=blk, psap=psap, tb=tb: e.tensor_tensor(out=self.g_g[0:tb, blk, :], in0=psap[0:tb, H:2 * H],
                                                                            in1=self.dtb[0:tb, :], op=ALU.add)),
                 reads=[pskey, 'const'], writes=['g_g'])
            S.op('act', (lambda e, blk=blk, tb=tb: e.activation(out=self.g_g[0:tb, blk, :], in_=self.g_g[0:tb, blk, :], func=AF.Exp)),
                 reads=['g_g'], writes=['g_g'])
            S.op('dve', (lambda e, blk=blk, tb=tb: e.tensor_scalar(out=self.g_g[0:tb, blk, :], in0=self.g_g[0:tb, blk, :],
                                                                 scalar1=1.0, scalar2=None, op0=ALU.add)),
                 reads=['g_g'], writes=['g_g'])
            S.op('act', (lambda e, blk=blk, tb=tb: e.activation(out=self.g_g[0:tb, blk, :], in_=self.g_g[0:tb, blk, :], func=AF.Ln)),
                 reads=['g_g'], writes=['g_g'])
            S.op('dve', (lambda e, blk=blk, tb=tb: e.tensor_tensor(out=self.g_g[0:tb, blk, :], in0=self.g_g[0:tb, blk, :],
                                                                 in1=self.nA[0:tb, :], op=ALU.mult)),
                 reads=['g_g', 'const'], writes=['g_g'])

    def gdn_proj_head(self, hd, W, which, sample):
        c = self.c
        S = self.S
        H = c.H
        KC = c.KC
        j = hd // 2
        Win = self.w['gdn_w_in'][0]
        nh = min(2, H - 2 * j)
        dsts = {'q': self.hq, 'k': self.hk, 'v': self.hv}
        par = j % 2
        qst = qnew = None
        if sample and not S.dry:
            qst, qnew = self.qs_st[par], self.qs_new[par]
            for idx in range(3):
                ch0 = idx * H + 2 * j
                S.dma('pool', f'qsin{par}', (lambda e, qst=qst, idx=idx, ch0=ch0: e.dma_start(
                    out=qst[:, idx, 0:nh, :, :], in_=self.d_qs_in[:, ch0:ch0 + nh, :, :])), writes=[('qs_st', par)])
            S.op('act', (lambda e, qst=qst, qnew=qnew: e.activation(out=qnew[:, :, :, :, 0:2], in_=qst[:, :, :, :, 1:3],
                                                                    func=AF.Copy)),
                 reads=[('qs_st', par)], writes=[('qs_new', par)])
        for idx, nm in enumerate(('q', 'k', 'v', 'z')):
            colbase = (idx * H * 128 if nm != 'z' else c.QKV) + j * 256
            wt, wk = self.wget(Win[:, colbase:colbase + nh * 128], KC, nh * 128, ('gin', nm, j))
            if S.dry:
                continue
            for hh in range(nh):
                head = 2 * j + hh
                psap, pskey = self.ps('lin')
                self.mm_acc(psap, pskey, [(wt, wk, list(range(KC)), hh * 128, 128, self.u, list(range(KC)),
                                           [('u', k) for k in range(KC)])], W)
                if nm == 'z':
                    S.op('act', (lambda e, hh=hh, psap=psap, W=W: e.activation(out=self.hz[hh][:, 0:W], in_=psap[:, 0:W],
                                                                             func=AF.Silu)),
                         reads=[pskey], writes=[('hz', hh)])
                    continue
                qch = idx * H + head
                dst = dsts[nm][hh]
                dkey = ('h' + nm, hh)
                if not sample:
                    cv, cvk = self.cvb[idx % 2][hh], ('cvb', idx % 2, hh)
                    S.op('act', (lambda e, cv=cv, qch=qch: e.activation(out=cv[:, 0:3], in_=self.qhalo[:, qch, :], func=AF.Copy)),
                         reads=[('qhalo', qch)], writes=[cvk])
                    S.op('act', (lambda e, cv=cv, psap=psap, W=W: e.activation(out=cv[:, 3:3 + W], in_=psap[:, 0:W], func=AF.Copy)),
                         reads=[pskey, cvk], writes=[cvk])
                    S.op('act', (lambda e, cv=cv, qch=qch, W=W: e.activation(out=self.qhalo[:, qch, :], in_=cv[:, W:W + 3],
                                                                           func=AF.Copy)),
                         reads=[cvk], writes=[('qhalo', qch)])
                    for w in range(SC):
                        if w == 0:
                            S.op('dve', (lambda e, cv=cv, dst=dst, qch=qch, W=W: e.tensor_scalar(
                                out=dst[:, 0:W], in0=cv[:, 0:W], scalar1=self.gwc[:, qch, 0:1], scalar2=None, op0=ALU.mult)),
                                 reads=[cvk, 'const'], writes=[dkey])
                        else:
                            S.op('dve', (lambda e, cv=cv, dst=dst, qch=qch, W=W, w=w: e.scalar_tensor_tensor(
                                out=dst[:, 0:W], in0=cv[:, w:w + W], scalar=self.gwc[:, qch, w:w + 1], in1=dst[:, 0:W],
                                op0=ALU.mult, op1=ALU.add)), reads=[cvk, dkey, 'const'], writes=[dkey])
                else:
                    NS = c.NS
                    S.op('act', (lambda e, qnew=qnew, idx=idx, hh=hh, psap=psap, W=W: e.activation(
                        out=qnew[:, idx, hh, :, 2], in_=psap[:, 0:W], func=AF.Copy)),
                         reads=[pskey, ('qs_new', par)], writes=[('qs_new', par)])
                    pr, prk = self.qs_prod, 'qsprod'
                    S.op('dve', (lambda e, qst=qst, idx=idx, hh=hh, qch=qch, pr=pr, NS=NS: e.tensor_tensor(
                        out=pr[:, :, :], in0=qst[:, idx, hh, :, :],
                        in1=self.gwc[:, qch, 0:3].unsqueeze(1).to_broadcast([128, NS, 3]), op=ALU.mult)),
                         reads=[('qs_st', par), 'const'], writes=[prk])
                    S.op('dve', (lambda e, dst=dst, pr=pr, W=W: e.tensor_reduce(out=dst[:, 0:W], in_=pr[:, :, :], axis=AX.X,
                                                                               op=ALU.add)),
                         reads=[prk], writes=[dkey])
                    S.op('dve', (lambda e, dst=dst, psap=psap, qch=qch, W=W: e.scalar_tensor_tensor(
                        out=dst[:, 0:W], in0=psap[:, 0:W], scalar=self.gwc[:, qch, 3:4], in1=dst[:, 0:W],
                        op0=ALU.mult, op1=ALU.add)), reads=[pskey, dkey, 'const'], writes=[dkey])
                S.op('act', (lambda e, dst=dst, W=W: e.activation(out=dst[:, 0:W], in_=dst[:, 0:W], func=AF.Silu)),
                     reads=[dkey], writes=[dkey])
                if nm in ('q', 'k'):
                    rn, rnk = self.tmp('rn')
                    self.colsum_sq(lambda k, dst=dst, dkey=dkey, W=W: (dst[:, 0:W], dkey), 1, W, 1.0, L2_EPS, rn, rnk)
                    sc = (128.0 ** -0.5) if nm == 'q' else 1.0
                    S.op('dve', (lambda e, dst=dst, rn=rn, W=W, sc=sc: e.scalar_tensor_tensor(
                        out=dst[:, 0:W], in0=dst[:, 0:W], scalar=sc, in1=rn[:, 0:W], op0=ALU.mult, op1=ALU.mult)),
                         reads=[dkey, rnk], writes=[dkey])

        if sample and not S.dry:
            for idx in range(3):
                ch0 = idx * H + 2 * j
                S.dma('pool', f'qsout{par}', (lambda e, qnew=qnew, idx=idx, ch0=ch0: e.dma_start(
                    out=self.d_qkv_s[:, ch0:ch0 + nh, :, :], in_=qnew[:, idx, 0:nh, :, :])),
                      reads=[('qs_new', par)], writes=[('d_qkv_s', idx, j)])

    def gdn_prompt_head(self, hd, hh, W):
        c = self.c
        S = self.S
        H = c.H
        NCH = W // C64
        NW = NCH * C64
        T = self.Tb[hh]
        qn, kn, vv = self.hq[hh], self.hk[hh], self.hv[hh]
        kq, kk, kv_ = ('hq', hh), ('hk', hh), ('hv', hh)
        ident = self.ident_f
        knb, qnb = T.knb, T.qnb
        S.op('act', (lambda e: e.activation(out=knb[:, 0:NW], in_=kn[:, 0:NW], func=AF.Copy)), reads=[kk], writes=['knb'])
        gcol = self.g_g[0:C64, 0:NCH, hd]
        bcol = self.g_beta[0:C64, 0:NCH, hd]
        pG, pGk = self.ps('aux')
        S.op('pe', (lambda e: e.matmul(pG[0:C64, 0:NCH], lhsT=self.tri_f[0:C64, 0:C64], rhs=gcol, start=True, stop=True)),
             reads=['g_g', 'const'], writes=[pGk])
        S.op('pe', (lambda e: e.matmul(pG[:, 64:64 + NCH], lhsT=self.ones_f[0:C64, :], rhs=gcol, start=True, stop=True)),
             reads=['g_g', 'const', pGk], writes=[pGk])
        Gc = T.t_G
        S.op('dve', (lambda e: e.tensor_copy(out=Gc[:, 0:NCH], in_=pG[0:C64, 0:NCH])), reads=[pGk], writes=['t_G'])
        S.op('act', (lambda e: e.activation(out=T.t_alast[:, 0:NCH], in_=pG[:, 64:64 + NCH], func=AF.Exp)),
             reads=[pGk], writes=['t_alast'])
        S.op('dve', (lambda e: e.tensor_tensor(out=T.t_kdsc[:, 0:NCH], in0=pG[0:C64, 64:64 + NCH], in1=Gc[:, 0:NCH],
                                               op=ALU.subtract)), reads=[pGk, 't_G'], writes=['t_kdsc'])
        S.op('act', (lambda e: e.activation(out=T.t_kdsc[:, 0:NCH], in_=T.t_kdsc[:, 0:NCH], func=AF.Exp)),
             reads=['t_kdsc'], writes=['t_kdsc'])
        S.op('act', (lambda e: e.activation(out=T.t_eG[:, 0:NCH], in_=Gc[:, 0:NCH], func=AF.Exp)), reads=['t_G'],
             writes=['t_eG'])
        S.op('dve', (lambda e: e.scalar_tensor_tensor(out=T.t_nbe[:, 0:NCH], in0=T.t_eG[:, 0:NCH], scalar=-1.0, in1=bcol,
                                                      op0=ALU.mult, op1=ALU.mult)), reads=['t_eG', 'g_beta'], writes=['t_nbe'])
        gt, gtk = T.t_X, 't_X'
        S.op('dve', (lambda e: e.tensor_tensor(out=gt[:, 0:NCH, :], in0=gcol.unsqueeze(2).to_broadcast([C64, NCH, C64]),
                                               in1=self.tri_f[0:C64, 0:C64].unsqueeze(1).to_broadcast([C64, NCH, C64]),
                                               op=ALU.mult)), reads=['g_g', 'const'], writes=[gtk])
        pR, pRk = self.ps('aux')
        S.op('pe', (lambda e: e.matmul(pR[:, 0:NW], lhsT=self.ones_f[0:C64, :], rhs=gt[:, 0:NCH, :].rearrange("p c x -> p (c x)"),
                                       start=True, stop=True)), reads=[gtk, 'const'], writes=[pRk])
        S.op('act', (lambda e: e.activation(out=T.t_eGrow[:, 0:NW], in_=pR[:, 0:NW], func=AF.Exp)), reads=[pRk],
             writes=['t_eGrow'])
        S.op('dve', (lambda e: e.tensor_tensor(out=qnb[:, 0:NW], in0=qn[:, 0:NW], in1=T.t_eGrow[:, 0:NW], op=ALU.mult)),
             reads=[kq, 't_eGrow'], writes=['qnb'])
        X, Xk = T.t_X, 't_X'
        S.op('dve', (lambda e: e.tensor_tensor(out=X[:, 0:NCH, :], in0=pR[0:C64, 0:NW].rearrange("p (c x) -> p c x", c=NCH),
                                               in1=Gc[:, 0:NCH].unsqueeze(2).to_broadcast([C64, NCH, C64]), op=ALU.subtract)),
             reads=[pRk, 't_G'], writes=[Xk])
        D, DT = T.t_D, T.t_DT
        S.op('dve', (lambda e: e.scalar_tensor_tensor(out=D[:, 0:NCH, :], in0=X[:, 0:NCH, :], scalar=-1.0,
                                                      in1=self.m_up.unsqueeze(1).to_broadcast([C64, NCH, C64]),
                                                      op0=ALU.mult, op1=ALU.subtract)), reads=[Xk, 'const'], writes=['t_D'])
        S.op('act', (lambda e: e.activation(out=D[:, 0:NCH, :], in_=D[:, 0:NCH, :], func=AF.Exp)), reads=['t_D'], writes=['t_D'])
        S.op('dve', (lambda e: e.tensor_tensor(out=DT[:, 0:NCH, :], in0=X[:, 0:NCH, :],
                                               in1=self.m_lo.unsqueeze(1).to_broadcast([C64, NCH, C64]), op=ALU.subtract)),
             reads=[Xk, 'const'], writes=['t_DT'])
        S.op('act', (lambda e: e.activation(out=DT[:, 0:NCH, :], in_=DT[:, 0:NCH, :], func=AF.Exp)), reads=['t_DT'],
             writes=['t_DT'])
        S.op('dve', (lambda e: e.tensor_tensor(out=D[:, 0:NCH, :], in0=D[:, 0:NCH, :],
                                               in1=self.m_strict.unsqueeze(1).to_broadcast([C64, NCH, C64]), op=ALU.mult)),
             reads=['t_D', 'const'], writes=['t_D'])
        S.op('dve', (lambda e: e.scalar_tensor_tensor(out=D[:, 0:NCH, :], in0=D[:, 0:NCH, :], scalar=-1.0,
                                                      in1=bcol.unsqueeze(2).to_broadcast([C64, NCH, C64]),
                                                      op0=ALU.mult, op1=ALU.mult)), reads=['t_D', 'g_beta'], writes=['t_D'])
        pA, pAk = self.ps('aux')
        pQ, pQk = self.ps('aux')

        def fa(e):
            ins = None
            for ci in range(NCH):
                sl = slice(ci * C64, (ci + 1) * C64)
                ins = e.matmul(pA[0:C64, sl], lhsT=knb[:, sl], rhs=knb[:, sl], start=True, stop=True)
            return ins
        S.op('pe', fa, reads=['knb'], writes=[pAk])
        S.op('act', (lambda e: e.activation(out=T.qrb[:, 0:NW], in_=qn[:, 0:NW], func=AF.Copy)), reads=[kq], writes=['qrb'])

        def fq(e):
            ins = None
            for ci in range(NCH):
                sl = slice(ci * C64, (ci + 1) * C64)
                ins = e.matmul(pQ[0:C64, sl], lhsT=knb[:, sl], rhs=T.qrb[:, sl], start=True, stop=True)
            return ins
        S.op('pe', fq, reads=['knb', 'qrb'], writes=[pQk])
        Nm, NmT = T.t_N, T.t_NT
        S.op('dve', (lambda e: e.tensor_tensor(out=Nm[0][:, 0:NCH, :], in0=pA[0:C64, 0:NW].rearrange("p (c x) -> p c x", c=NCH),
                                               in1=D[:, 0:NCH, :], op=ALU.mult)), reads=[pAk, 't_D'], writes=[('t_N', 0)])
        S.op('dve', (lambda e: e.tensor_tensor(out=T.t_attT[:, 0:NCH, :],
                                               in0=pQ[0:C64, 0:NW].rearrange("p (c x) -> p c x", c=NCH),
                                               in1=DT[:, 0:NCH, :], op=ALU.mult)), reads=[pQk, 't_DT'], writes=['t_attT'])
        pT_, pTk = self.ps('aux')

        def ft(e):
            ins = None
            for ci in range(NCH):
                sl = slice(ci * C64, (ci + 1) * C64)
                ins = e.transpose(pT_[0:C64, sl], Nm[0][:, ci, :], ident[0:C64, 0:C64])
            return ins
        S.op('pe', ft, reads=[('t_N', 0), 'const'], writes=[pTk])
        S.op('act', (lambda e: e.activation(out=NmT[0][:, 0:NCH, :], in_=pT_[0:C64, 0:NW].rearrange("p (c x) -> p c x", c=NCH),
                                            func=AF.Copy)), reads=[pTk], writes=[('t_NT', 0)])
        PT = T.t_PT
        S.op('dve', (lambda e: e.tensor_tensor(out=PT[0][:, 0:NCH, :], in0=NmT[0][:, 0:NCH, :],
                                               in1=ident[0:C64, 0:C64].unsqueeze(1).to_broadcast([C64, NCH, C64]), op=ALU.add)),
             reads=[('t_NT', 0), 'const'], writes=[('t_PT', 0)])
        cur = 0
        for lvl in range(1, 6):
            nxt = 1 - cur
            pM, pMk = self.ps('aux')

            def fm(e, cur=cur, pM=pM):
                ins = None
                for ci in range(NCH):
                    sl = slice(ci * C64, (ci + 1) * C64)
                    ins = e.matmul(pM[0:C64, sl], lhsT=NmT[cur][:, ci, :], rhs=Nm[cur][:, ci, :], start=True, stop=True)
                return ins
            S.op('pe', fm, reads=[('t_N', cur), ('t_NT', cur)], writes=[pMk])
            S.op('act', (lambda e, nxt=nxt, pM=pM: e.activation(out=Nm[nxt][:, 0:NCH, :],
                                                               in_=pM[0:C64, 0:NW].rearrange("p (c x) -> p c x", c=NCH),
                                                               func=AF.Copy)), reads=[pMk], writes=[('t_N', nxt)])
            if lvl < 5:
                pMT, pMTk = self.ps('aux')

                def fmt(e, cur=cur, pMT=pMT):
                    ins = None
                    for ci in range(NCH):
                        sl = slice(ci * C64, (ci + 1) * C64)
                        ins = e.matmul(pMT[0:C64, sl], lhsT=Nm[cur][:, ci, :], rhs=NmT[cur][:, ci, :], start=True, stop=True)
                    return ins
                S.op('pe', fmt, reads=[('t_N', cur), ('t_NT', cur)], writes=[pMTk])
                S.op('dve', (lambda e, nxt=nxt, pMT=pMT: e.tensor_copy(out=NmT[nxt][:, 0:NCH, :],
                                                                      in_=pMT[0:C64, 0:NW].rearrange("p (c x) -> p c x", c=NCH))),
                     reads=[pMTk], writes=[('t_NT', nxt)])
            pP, pPk = self.ps('aux')

            def fp(e, nxt=nxt, cur=cur, pP=pP):
                ins = None
                for ci in range(NCH):
                    sl = slice(ci * C64, (ci + 1) * C64)
                    ins = e.matmul(pP[0:C64, sl], lhsT=Nm[nxt][:, ci, :], rhs=PT[0][:, ci, :], start=True, stop=True)
                return ins
            S.op('pe', fp, reads=[('t_N', nxt), ('t_PT', 0)], writes=[pPk])
            S.op('dve', (lambda e, nxt=nxt, cur=cur, pP=pP: e.tensor_tensor(
                out=PT[0][:, 0:NCH, :], in0=pP[0:C64, 0:NW].rearrange("p (c x) -> p c x", c=NCH), in1=PT[0][:, 0:NCH, :],
                op=ALU.add)), reads=[pPk, ('t_PT', 0)], writes=[('t_PT', 0)])
            cur = nxt
        PTf, PTk = PT[0], ('t_PT', 0)
        for ci in range(NCH):
            sl = slice(ci * C64, (ci + 1) * C64)
            pk, pkk = self.ps('aux')
            S.op('pe', (lambda e, pk=pk, sl=sl: e.transpose(pk[0:C64, 0:128], kn[:, sl], ident[:, :])), reads=[kk, 'const'],
                 writes=[pkk])
            S.op('pe', (lambda e, pk=pk, sl=sl: e.transpose(pk[0:C64, 128:256], vv[:, sl], ident[:, :])),
                 reads=[kv_, 'const', pkk], writes=[pkk])
            S.op('dve', (lambda e, pk=pk, ci=ci: e.tensor_scalar(out=T.t_kd[:, ci, :], in0=pk[0:C64, 0:128],
                                                               scalar1=T.t_kdsc[:, ci:ci + 1], scalar2=None, op0=ALU.mult)),
                 reads=[pkk, 't_kdsc'], writes=[('t_kd', ci)])
            S.op('dve', (lambda e, pk=pk, ci=ci: e.tensor_scalar(out=T.t_bv[:, ci, :], in0=pk[0:C64, 128:256],
                                                               scalar1=self.g_beta[0:C64, ci, hd:hd + 1], scalar2=None,
                                                               op0=ALU.mult)),
                 reads=[pkk, 'g_beta'], writes=[('t_bv', ci)])
        Sf = self.Sst[:, hd, :]
        Sk = ('S', hd)
        Sb = T.Sbf
        S.op('act', (lambda e: e.activation(out=Sb[:, :], in_=Sf, func=AF.Copy)), reads=[Sk], writes=['Sbf'])
        for ci in range(NCH):
            sl = slice(ci * C64, (ci + 1) * C64)
            p1, p1k = self.ps('aux')
            S.op('pe', (lambda e, p1=p1, sl=sl: e.matmul(p1[0:C64, 0:128], lhsT=knb[:, sl], rhs=Sb[:, :], start=True, stop=True)),
                 reads=['knb', 'Sbf'], writes=[p1k])
            S.op('dve', (lambda e, p1=p1, ci=ci: e.scalar_tensor_tensor(out=T.t_R[:, :], in0=p1[0:C64, 0:128],
                                                                      scalar=T.t_nbe[:, ci:ci + 1], in1=T.t_bv[:, ci, :],
                                                                      op0=ALU.mult, op1=ALU.add)),
                 reads=[p1k, 't_nbe', ('t_bv', ci)], writes=['t_R'])
            S.op('pe', (lambda e, p1=p1, ci=ci: e.matmul(p1[0:C64, 128:256], lhsT=PTf[:, ci, :], rhs=T.t_R[:, :],
                                                       start=True, stop=True)), reads=[PTk, 't_R', p1k], writes=[p1k])
            S.op('act', (lambda e, p1=p1: e.activation(out=T.t_ub[:, :], in_=p1[0:C64, 128:256], func=AF.Copy)),
                 reads=[p1k], writes=['t_ub'])
            p2, p2k = self.ps('aux')

            def fo(e, p2=p2, sl=sl, ci=ci):
                e.matmul(p2[0:C64, 0:128], lhsT=qnb[:, sl], rhs=Sb[:, :], start=True, stop=False)
                return e.matmul(p2[0:C64, 0:128], lhsT=T.t_attT[:, ci, :], rhs=T.t_ub[:, :], start=False, stop=True)
            S.op('pe', fo, reads=['qnb', 'Sbf', 't_attT', 't_ub'], writes=[p2k])
            S.op('act', (lambda e, p2=p2, ci=ci: e.activation(out=T.t_o[:, ci, :], in_=p2[0:C64, 0:128], func=AF.Copy)),
                 reads=[p2k], writes=[('t_o', ci)])
            p3, p3k = self.ps('aux')
            S.op('pe', (lambda e, p3=p3, ci=ci: e.matmul(p3[:, 0:128], lhsT=T.t_kd[:, ci, :], rhs=T.t_ub[:, :],
                                                       start=True, stop=True)), reads=[('t_kd', ci), 't_ub'], writes=[p3k])
            S.op('dve', (lambda e, p3=p3, ci=ci: e.scalar_tensor_tensor(out=Sf, in0=Sf, scalar=T.t_alast[:, ci:ci + 1],
                                                                      in1=p3[:, 0:128], op0=ALU.mult, op1=ALU.add)),
                 reads=[p3k, 't_alast', Sk], writes=[Sk])
            if ci < NCH - 1:
                S.op('act', (lambda e: e.activation(out=Sb[:, :], in_=Sf, func=AF.Copy)), reads=[Sk], writes=['Sbf'])
        sq = T.t_bv
        sqkeys = [('t_bv', ci) for ci in range(NCH)]
        S.op('dve', (lambda e: e.tensor_tensor(out=sq[:, 0:NCH, :], in0=T.t_o[:, 0:NCH, :], in1=T.t_o[:, 0:NCH, :],
                                               op=ALU.mult)), reads=[('t_o', ci) for ci in range(NCH)], writes=sqkeys)
        S.op('dve', (lambda e: e.tensor_reduce(out=T.t_oss[:, 0:NCH], in_=sq[:, 0:NCH, :], axis=AX.X, op=ALU.add)),
             reads=sqkeys, writes=['t_oss'])
        S.op('dve', (lambda e: e.tensor_scalar(out=T.t_oss[:, 0:NCH], in0=T.t_oss[:, 0:NCH], scalar1=1.0 / 128,
                                               scalar2=RMS_EPS, op0=ALU.mult, op1=ALU.add)), reads=['t_oss'], writes=['t_oss'])
        self.rsqrt_(T.t_oss[:, 0:NCH], 't_oss')
        S.op('dve', (lambda e: e.tensor_tensor(out=sq[:, 0:NCH, :], in0=T.t_o[:, 0:NCH, :],
                                               in1=T.t_oss[:, 0:NCH].unsqueeze(2).to_broadcast([C64, NCH, 128]), op=ALU.mult)),
             reads=[('t_o', ci) for ci in range(NCH)] + ['t_oss'] + sqkeys, writes=sqkeys)
        pO, pOk = self.ps('aux')

        def fto(e):
            ins = None
            for ci in range(NCH):
                ins = e.transpose(pO[:, ci * C64:(ci + 1) * C64], sq[:, ci, :], ident[0:C64, 0:C64])
            return ins
        S.op('pe', fto, reads=sqkeys + ['const'], writes=[pOk])
        S.op('dve', (lambda e: e.scalar_tensor_tensor(out=self.scr[:, hd, 0:NW], in0=pO[:, 0:NW], scalar=self.gon[:, 0:1],
                                                      in1=self.hz[hh][:, 0:NW], op0=ALU.mult, op1=ALU.mult)),
             reads=[pOk, ('hz', hh), 'const'], writes=[('scr', hd)])

    def gdn_sample_states(self, W):
        c = self.c
        S = self.S
        H, NS = c.H, c.NS
        ident = self.ident_f
        bd, bdk = self.s_bd, 's_bd'
        for nm, dst in (('a', self.s_abc), ('b', self.s_bbc)):
            if nm == 'a':
                S.op('act', (lambda e: e.activation(out=self.s_a[0:NS, :], in_=self.g_g[0:NS, 0, :], func=AF.Exp)),
                     reads=['g_g'], writes=['s_a'])
                src, srck = self.s_a[0:NS, :], 's_a'
            else:
                src, srck = self.g_beta[0:NS, 0, :], 'g_beta'
            S.op('dve', (lambda e, src=src: e.tensor_tensor(out=bd[0:NS, :, :], in0=src.unsqueeze(1).to_broadcast([NS, NS, H]),
                                                            in1=ident[0:NS, 0:NS].unsqueeze(2).to_broadcast([NS, NS, H]),
                                                            op=ALU.mult)), reads=[srck, 'const'], writes=[bdk])
            nbmax = max(1, 256 // H)
            for b0 in range(0, NS, nbmax):
                nb = min(nbmax, NS - b0)
                pb, pbk = self.ps('aux')
                S.op('pe', (lambda e, pb=pb, b0=b0, nb=nb: e.matmul(pb[:, 0:nb * H], lhsT=self.ones_f[0:NS, :],
                                                                   rhs=bd[0:NS, b0:b0 + nb, :].rearrange("p b h -> p (b h)"),
                                                                   start=True, stop=True)), reads=[bdk, 'const'], writes=[pbk])
                S.op('act', (lambda e, pb=pb, b0=b0, nb=nb, dst=dst: e.activation(
                    out=dst[:, b0:b0 + nb, :], in_=pb[:, 0:nb * H].rearrange("p (b h) -> p b h", b=nb), func=AF.Copy)),
                     reads=[pbk], writes=['s_' + nm + 'bc'])
        for b in range(NS):
            Sb_, Sbk = self.s_S, 's_S'
            S.dma('pool', 'sin', (lambda e, Sb_=Sb_, b=b: e.dma_start(out=Sb_[:, :, :], in_=self.d_gs_in[b])), writes=[Sbk])
            pk, pkk = self.ps('aux')

            def fks(e, Sb_=Sb_, pk=pk, b=b):
                ins = None
                for hd in range(H):
                    ins = e.matmul(pk[:, hd:hd + 1], lhsT=Sb_[:, hd, :], rhs=self.s_k[:, hd, b:b + 1], start=True, stop=True)
                return ins
            S.op('pe', fks, reads=[Sbk, 's_k'], writes=[pkk])
            r, rk = self.s_r, 's_r'
            S.op('dve', (lambda e, pk=pk, b=b: e.tensor_tensor(out=r[:, :], in0=pk[:, 0:H], in1=self.s_abc[:, b, :], op=ALU.mult)),
                 reads=[pkk, 's_abc'], writes=[rk])
            S.op('dve', (lambda e, b=b: e.tensor_tensor(out=r[:, :], in0=self.s_v[:, :, b], in1=r[:, :], op=ALU.subtract)),
                 reads=[rk, 's_v'], writes=[rk])
            S.op('dve', (lambda e, b=b: e.tensor_tensor(out=r[:, :], in0=r[:, :], in1=self.s_bbc[:, b, :], op=ALU.mult)),
                 reads=[rk, 's_bbc'], writes=[rk])
            pt, ptk = self.ps('aux')
            S.op('pe', (lambda e, pt=pt, b=b: e.transpose(pt[0:H, 0:128], self.s_k[:, :, b], ident[:, :])), reads=['s_k', 'const'],
                 writes=[ptk])
            S.op('pe', (lambda e, pt=pt: e.transpose(pt[0:H, 128:256], r[:, :], ident[:, :])), reads=[rk, 'const', ptk],
                 writes=[ptk])
            S.op('act', (lambda e, pt=pt: e.activation(out=self.s_rows[0:H, :], in_=pt[0:H, 0:256], func=AF.Copy)), reads=[ptk],
                 writes=['s_rows'])
            for g0 in range(0, H, 2):
                ng = min(2, H - g0)
                po, pok = self.ps('aux')
                rb, rbk = self.s_rbd[(g0 // 2) % 2], ('s_rbd', (g0 // 2) % 2)
                S.op('dve', (lambda e, rb=rb, g0=g0, ng=ng: e.tensor_tensor(
                    out=rb[0:H, 0:ng, :], in0=self.s_rows[0:H, 128:256].unsqueeze(1).to_broadcast([H, ng, 128]),
                    in1=ident[0:H, g0:g0 + ng].unsqueeze(2).to_broadcast([H, ng, 128]), op=ALU.mult)),
                     reads=['s_rows', 'const'], writes=[rbk])
                S.op('pe', (lambda e, po=po, rb=rb, ng=ng: e.matmul(
                    po[:, 0:ng * 128], lhsT=self.s_rows[0:H, 0:128],
                    rhs=rb[0:H, 0:ng, :].rearrange("p h d -> p (h d)"), start=True, stop=True)),
                     reads=['s_rows', rbk], writes=[pok])
                S.op('dve', (lambda e, Sb_=Sb_, b=b, g0=g0, ng=ng: e.tensor_tensor(
                    out=Sb_[:, g0:g0 + ng, :], in0=Sb_[:, g0:g0 + ng, :],
                    in1=self.s_abc[:, b, g0:g0 + ng].unsqueeze(2).to_broadcast([128, ng, 128]), op=ALU.mult)),
                     reads=[Sbk, 's_abc'], writes=[Sbk])
                S.op('dve', (lambda e, Sb_=Sb_, po=po, g0=g0, ng=ng: e.tensor_tensor(
                    out=Sb_[:, g0:g0 + ng, :], in0=Sb_[:, g0:g0 + ng, :],
                    in1=po[:, 0:ng * 128].rearrange("p (h d) -> p h d", h=ng), op=ALU.add)),
                     reads=[Sbk, pok], writes=[Sbk])
            pq, pqk = self.ps('aux')

            def foq(e, Sb_=Sb_, pq=pq, b=b):
                ins = None
                for hd in range(H):
                    ins = e.matmul(pq[:, hd:hd + 1], lhsT=Sb_[:, hd, :], rhs=self.s_q[:, hd, b:b + 1], start=True, stop=True)
                return ins
            S.op('pe', foq, reads=[Sbk, 's_q'], writes=[pqk])
            S.op('act', (lambda e, pq=pq, b=b: e.activation(out=self.s_o[:, :, b], in_=pq[:, 0:H], func=AF.Copy)), reads=[pqk],
                 writes=['s_o'])
            S.dma('pool', 'sout', (lambda e, Sb_=Sb_, b=b: e.dma_start(out=self.d_gdn_s[b], in_=Sb_[:, :, :])),
                  reads=[Sbk], writes=[('d_gdn_s', b)])
        HB = H * NS
        of = self.s_o[:, :, :].rearrange("p h b -> p (h b)")
        hstep = max(1, 256 // NS)
        for h0 in range(0, H, hstep):
            nh = min(hstep, H - h0)
            o0, n = h0 * NS, nh * NS
            rn, rnk = self.tmp('rn')
            self.colsum_sq(lambda k, o0=o0, n=n: (of[:, o0:o0 + n], 's_o'), 1, n, 1.0 / 128, RMS_EPS, rn, rnk)
            t, tk = self.tmp('on')
            S.op('dve', (lambda e, t=t, rn=rn, o0=o0, n=n: e.scalar_tensor_tensor(out=t[:, 0:n], in0=of[:, o0:o0 + n],
                                                                               scalar=self.gon[:, 0:1], in1=rn[:, 0:n],
                                                                               op0=ALU.mult, op1=ALU.mult)),
                 reads=['s_o', rnk, 'const'], writes=[tk])
            S.op('dve', (lambda e, t=t, n=n, h0=h0, nh=nh: e.tensor_tensor(
                out=self.scr[:, h0:h0 + nh, 0:NS], in0=t[:, 0:n].rearrange("p (h b) -> p h b", h=nh),
                in1=self.s_z[:, h0:h0 + nh, :], op=ALU.mult)),
                 reads=[tk, 's_z'], writes=[('scr', hx) for hx in range(h0, h0 + nh)])

    def gdn(self, W, sample, t0):
        c = self.c
        S = self.S
        H, NS = c.H, c.NS
        h = self.h
        self.gdn_gates(W, 1 if sample else W // C64)
        for j in range((H + 1) // 2):
            self.gdn_proj_head(2 * j, W, None, sample)
            if S.dry:
                continue
            if not sample:
                caps = []
                for hh in range(min(2, H - 2 * j)):
                    S.cap = []
                    S.ns = hh
                    self.aux_set = (2 * hh, 2 * hh + 1)
                    self.gdn_prompt_head(2 * j + hh, hh, W)
                    caps.append(S.cap)
                    S.cap = None
                    S.ns = None
                    self.aux_set = None
                for i in range(max(len(x) for x in caps)):
                    for x in caps:
                        if i < len(x):
                            S.op(*x[i])
                continue
            for hh in range(min(2, H - 2 * j)):
                hd = 2 * j + hh
                if True:
                    for nm, src, dst in (('q', self.hq, self.s_q), ('k', self.hk, self.s_k), ('v', self.hv, self.s_v),
                                         ('z', self.hz, self.s_z)):
                        S.op('act', (lambda e, src=src, dst=dst, hh=hh, hd=hd: e.activation(out=dst[:, hd, :], in_=src[hh][:, 0:NS],
                                                                                            func=AF.Copy)),
                             reads=[('h' + nm, hh)], writes=['s_' + nm])
        if sample and not S.dry:
            self.gdn_sample_states(W)

        def cb(mi, psap, pskey):
            S.op('dve', (lambda e, mi=mi, psap=psap, W=W: e.tensor_tensor(out=h[:, mi, 0:W], in0=h[:, mi, 0:W], in1=psap[:, 0:W],
                                                                        op=ALU.add)),
                 reads=[pskey, ('h', mi)], writes=[('h', mi)])
        self.linear(self.w['gdn_w_out'][0], c.KC, 0, c.D, self.scr, lambda k: ('scr', k), W, cb, 'gout')

    def tile(self, ti, sample):
        c = self.c
        S = self.S
        W = c.NS if sample else c.TT
        t0 = 0 if sample else ti * c.TT
        h = self.h
        if not S.dry:
            src = self.d_xsT if sample else self.d_xpT[:, :, t0:t0 + W]
            S.dma('pool', 'xin', (lambda e, src=src, W=W: e.dma_start(out=h[:, :, 0:W], in_=src)),
                  writes=[('h', k) for k in range(c.KC)])
        for layer in range(2):
            self.rmsnorm(self.gmix, layer, W)
            if layer == 0:
                self.conformer(W, sample, t0)
            else:
                self.gdn(W, sample, t0)
            self.rmsnorm(self.gffn, layer, W)
            self.ffn(layer, W)
            self.rmsnorm(self.gple, layer, W)
            self.ple(layer, W, sample, t0)
        if S.dry:
            return
        self.colsum_sq(lambda k: (h[:, k, 0:W], ('h', k)), c.KC, W, 1.0 / c.D, RMS_EPS, self.rstd, 'rstd')
        for k in range(c.KC):
            t, tk = self.tmp('y')
            S.op('dve', (lambda e, t=t, k=k, W=W: e.scalar_tensor_tensor(out=t[:, 0:W], in0=h[:, k, 0:W],
                                                                       scalar=self.gfin[:, 0, k:k + 1], in1=self.rstd[:, 0:W],
                                                                       op0=ALU.mult, op1=ALU.mult)),
                 reads=[('h', k), 'rstd', 'const'], writes=[tk])
            dst = self.d_ysT[:, k, :] if sample else self.d_ypT[:, k, t0:t0 + W]
            S.dma('pool', f'yout{tk[1]}', (lambda e, t=t, dst=dst, W=W: e.dma_start(out=dst, in_=t[:, 0:W])),
                  reads=[tk], writes=[('d_y', sample, ti, k)])

    def build(self):
        c = self.c
        nc = self.nc
        D, F, H, KC, QC, NS, TT, SEQ = c.D, c.F, c.H, c.KC, c.QC, c.NS, c.TT, c.SEQ
        self.d_xpT = self.din("xpT", [128, KC, SEQ])
        self.d_xsT = self.din("xsT", [128, KC, NS])
        self.d_ppT = [self.din(f"ppT{l}", [128, c.PC, SEQ]) for l in range(2)]
        self.d_psT = [self.din(f"psT{l}", [128, c.PC, NS]) for l in range(2)]
        self.d_cs_in = self.din("cs_in", [128, KC, NS, CW - 1])
        self.d_qs_in = self.din("qs_in", [128, QC, NS, SC - 1])
        self.d_gs_in = self.din("gs_in", [NS, 128, H, 128])
        d_vec = {n: self.din(n, s) for n, s in dict(
            gmix=[128, 2, KC], gffn=[128, 2, KC], gple=[128, 2, KC], gfin=[128, 1, KC],
            cwdw=[128, KC, CW], cbdw=[128, KC], clng=[128, KC], clnb=[128, KC], gwc=[128, QC, SC],
            alog=[C64, H], dtb=[C64, H], gon=[128, 1],
            c_ident=[128, 128], c_tri=[C64, C64], c_mup=[C64, C64], c_mlo=[C64, C64], c_mstrict=[C64, C64]).items()}
        self.w = {}
        for n, s in dict(conf_w_pw1=[1, D, 2 * D], conf_w_pw2=[1, D, D], gdn_w_in=[1, D, c.PROJ], gdn_w_out=[1, D, D],
                         ffn_w_gate=[2, D, F], ffn_w_up=[2, D, F], ffn_w_down=[2, F, D], ple_w_gate=[2, D, D],
                         ple_w_proj=[2, c.PLE, D]).items():
            ap = self.din(n, s)
            self.w[n] = [ap[l] for l in range(s[0])]
        self.d_ypT = self.dout("ypT", [128, KC, SEQ])
        self.d_ysT = self.dout("ysT", [128, KC, NS])
        self.d_conf_p = self.dout("conf_p", [128, KC, CW - 1])
        self.d_qkv_p = self.dout("qkv_p", [128, QC, SC - 1])
        self.d_gdn_p = self.dout("gdn_p", [128, H, 128])
        self.d_conf_s = self.dout("conf_s", [128, KC, NS, CW - 1])
        self.d_qkv_s = self.dout("qkv_s", [128, QC, NS, SC - 1])
        self.d_gdn_s = self.dout("gdn_s", [NS, 128, H, 128])

        NCH = c.NCH
        with contextlib.ExitStack() as st:
            self.st = st
            S = self.S = Sched(nc, st)
            self.h = self.sb("h", [128, KC, TT])
            self.u = self.sb("u", [128, KC, TT], BF16)
            self.wb = [self.sb(f"wb{i}", [128, 32, 256], BF16) for i in range(c.NWB)]
            self.psb = [st.enter_context(nc.psum_tensor(f"psb{i}", [128, 512], F32)) for i in range(8)]
            self.tmps = [self.sb(f"tmp{i}", [128, 256]) for i in range(5)]
            self.rstd = self.sb("rstd", [128, TT])
            self.mean = self.sb("mean", [128, TT])
            self.pT = self.sb("pT", [128, c.PC, TT], BF16)
            self.scr = self.sb("scr", [128, max(c.FGMAX, KC, H), TT], BF16)
            self.wba = self.sb("wba", [128, KC, 2 * H], BF16)
            self.g_beta = self.sb("g_beta", [C64, NCH, H])
            self.g_g = self.sb("g_g", [C64, NCH, H])
            self.hq = [self.sb(f"hq{i}", [128, TT]) for i in range(2)]
            self.hk = [self.sb(f"hk{i}", [128, TT]) for i in range(2)]
            self.hv = [self.sb(f"hv{i}", [128, TT]) for i in range(2)]
            self.hz = [self.sb(f"hz{i}", [128, TT]) for i in range(2)]
            self.ones_f = self.sb("ones_f", [128, 128])
            self.ident_f = self.sb("ident_f", [128, 128])
            self.tri_f = self.sb("tri_f", [C64, C64])
            self.m_up_t = self.sb("m_up", [C64, C64])
            self.m_lo_t = self.sb("m_lo", [C64, C64])
            self.m_strict_t = self.sb("m_strict", [C64, C64])
            self.m_up, self.m_lo, self.m_strict = self.m_up_t[:, :], self.m_lo_t[:, :], self.m_strict_t[:, :]
            self.gmix = self.sb("gmix", [128, 2, KC])
            self.gffn = self.sb("gffn", [128, 2, KC])
            self.gple = self.sb("gple", [128, 2, KC])
            self.gfin = self.sb("gfin", [128, 1, KC])
            self.cwdw = self.sb("cwdw", [128, KC, CW])
            self.cbdw = self.sb("cbdw", [128, KC])
            self.clng = self.sb("clng", [128, KC])
            self.clnb = self.sb("clnb", [128, KC])
            self.gwc = self.sb("gwc", [128, QC, SC])
            self.nA = self.sb("nA", [C64, H])
            self.dtb = self.sb("dtb", [C64, H])
            self.gon = self.sb("gon", [128, 1])
            self.ps_lin_i = self.ps_aux_i = self.tmp_i = 0

            with contextlib.ExitStack() as pst:
                self.st = pst
                self.scr_glu = self.sb("glub", [128, 4, CW - 1 + TT])
                self.chalo = self.sb("chalo", [128, KC, CW - 1])
                self.qhalo = self.sb("qhalo", [128, QC, SC - 1])
                self.Sst = self.sb("Sst", [128, H, 128])
                self.cvb = [[self.sb(f"cvb{a}{b}", [128, SC - 1 + TT]) for b in range(2)] for a in range(2)]
                class _NS:
                    pass
                self.Tb = []
                for p_ in range(2):
                    T = _NS()
                    sfx = f"_{p_}"
                    T.knb = self.sb("knb" + sfx, [128, TT], BF16)
                    T.qnb = self.sb("qnb" + sfx, [128, TT], BF16)
                    T.qrb = self.sb("qrb" + sfx, [128, TT], BF16)
                    T.t_G = self.sb("t_G" + sfx, [C64, NCH])
                    T.t_alast = self.sb("t_alast" + sfx, [128, NCH])
                    T.t_kdsc = self.sb("t_kdsc" + sfx, [C64, NCH])
                    T.t_eG = self.sb("t_eG" + sfx, [C64, NCH])
                    T.t_nbe = self.sb("t_nbe" + sfx, [C64, NCH])
                    T.t_eGrow = self.sb("t_eGrow" + sfx, [128, TT])
                    T.t_X = self.sb("t_X" + sfx, [C64, NCH, C64])
                    T.t_D = self.sb("t_D" + sfx, [C64, NCH, C64])
                    T.t_DT = self.sb("t_DT" + sfx, [C64, NCH, C64])
                    T.t_attT = self.sb("t_attT" + sfx, [C64, NCH, C64], BF16)
                    T.t_N = [self.sb(f"t_N{i}" + sfx, [C64, NCH, C64]) for i in range(2)]
                    T.t_NT = [self.sb(f"t_NT{i}" + sfx, [C64, NCH, C64]) for i in range(2)]
                    T.t_PT = [self.sb(f"t_PT{i}" + sfx, [C64, NCH, C64]) for i in range(1)]
                    T.t_kd = self.sb("t_kd" + sfx, [C64, NCH, 128], BF16)
                    T.t_bv = self.sb("t_bv" + sfx, [C64, NCH, 128])
                    T.t_R = self.sb("t_R" + sfx, [C64, 128])
                    T.t_ub = self.sb("t_ub" + sfx, [C64, 128], BF16)
                    T.t_o = self.sb("t_o" + sfx, [C64, NCH, 128])
                    T.t_oss = self.sb("t_oss" + sfx, [C64, NCH])
                    T.Sbf = self.sb("Sbf" + sfx, [128, 128], BF16)
                    self.Tb.append(T)

                self.plan = []
                S.dry = True
                self.tile(0, False)
                S.dry = False
                NP = len(self.plan)
                ntiles = c.NT + (1 if NS > 0 else 0)
                self.wtotal = NP * ntiles
                self.wpos = 0
                self.wissued = 0
                self.wcache = []
                for i0 in range(0, NP, 64):
                    n = min(64, NP - i0)
                    wc = nc.dram_tensor(f"wcache{i0 // 64}", [n, 128, 32 * 256], BF16, kind="Internal").ap()
                    self.wcache.extend(wc[i] for i in range(n))

                S.op('dve', lambda e: e.memset(self.ones_f[:, :], 1.0), writes=['const'])
                lst = [(self.ident_f, 'c_ident'), (self.tri_f, 'c_tri'), (self.m_up_t, 'c_mup'),
                       (self.m_lo_t, 'c_mlo'), (self.m_strict_t, 'c_mstrict'), (self.gmix, 'gmix'),
                       (self.gffn, 'gffn'), (self.gple, 'gple'), (self.gfin, 'gfin'), (self.cwdw, 'cwdw'),
                       (self.cbdw, 'cbdw'), (self.clng, 'clng'), (self.clnb, 'clnb'), (self.gwc, 'gwc'),
                       (self.nA, 'alog'), (self.dtb, 'dtb'), (self.gon, 'gon')]
                for i, (t, n) in enumerate(lst):
                    S.dma('sp', f'cst{i % 4}', (lambda e, t=t, n=n: e.dma_start(out=t[:], in_=d_vec[n])), writes=[('cst', i)])
                S.op('act', lambda e: e.activation(out=self.nA[:, :], in_=self.nA[:, :], func=AF.Exp),
                     reads=[('cst', i) for i in range(len(lst))], writes=['const'])
                S.op('dve', lambda e: e.tensor_scalar(out=self.nA[:, :], in0=self.nA[:, :], scalar1=-1.0, scalar2=None,
                                                      op0=ALU.mult), reads=['const'], writes=['const'])
                S.op('dve', lambda e: e.memset(self.chalo[:, :, :], 0.0), writes=[('chalo', k) for k in range(KC)])
                S.op('dve', lambda e: e.memset(self.qhalo[:, :, :], 0.0), writes=[('qhalo', k) for k in range(QC)])
                S.op('dve', lambda e: e.memset(self.Sst[:, :, :], 0.0), writes=[('S', k) for k in range(H)])

                for ti in range(c.NT):
                    self.tile(ti, False)
                S.dma('pool', 'po0', (lambda e: e.dma_start(out=self.d_conf_p, in_=self.chalo[:, :, :])),
                      reads=[('chalo', k) for k in range(KC)], writes=['d_conf_p'])
                S.dma('pool', 'po1', (lambda e: e.dma_start(out=self.d_qkv_p, in_=self.qhalo[:, :, :])),
                      reads=[('qhalo', k) for k in range(QC)], writes=['d_qkv_p'])
                S.dma('pool', 'po2', (lambda e: e.dma_start(out=self.d_gdn_p, in_=self.Sst[:, :, :])),
                      reads=[('S', k) for k in range(H)], writes=['d_gdn_p'])
                for e_ in Sched.ENG:
                    S.wait_all(e_)
                S.emit()

            if NS > 0:
                with contextlib.ExitStack() as sst:
                    self.st = sst
                    self.cs_st = [self.sb(f"cs_st{i}", [128, NS, CW - 1]) for i in range(2)]
                    self.cs_new = [self.sb(f"cs_new{i}", [128, NS, CW - 1]) for i in range(2)]
                    self.cs_prod = self.sb("cs_prod", [128, NS, CW - 1])
                    self.qs_st = [self.sb(f"qs_st{i}", [128, 3, 2, NS, SC - 1]) for i in range(2)]
                    self.qs_new = [self.sb(f"qs_new{i}", [128, 3, 2, NS, SC - 1]) for i in range(2)]
                    self.qs_prod = self.sb("qs_prod", [128, NS, SC - 1])
                    self.s_a = self.sb("s_a", [NS, H])
                    self.s_bd = self.sb("s_bd", [NS, NS, H])
                    self.s_abc = self.sb("s_abc", [128, NS, H])
                    self.s_bbc = self.sb("s_bbc", [128, NS, H])
                    self.s_S = self.sb("s_S", [128, H, 128])
                    self.s_q = self.sb("s_q", [128, H, NS])
                    self.s_k = self.sb("s_k", [128, H, NS])
                    self.s_v = self.sb("s_v", [128, H, NS])
                    self.s_z = self.sb("s_z", [128, H, NS])
                    self.s_o = self.sb("s_o", [128, H, NS])
                    self.s_r = self.sb("s_r", [128, H])
                    self.s_rows = self.sb("s_rows", [H, 256])
                    self.s_rbd = [self.sb(f"s_rbd{i}", [H, 2, 128]) for i in range(2)]
                    self.tile(0, True)
                    assert self.wpos == self.wtotal, (self.wpos, self.wtotal)
                    S.wait_all('sp')
                    S.emit()
        return nc


def _fm(x):
    T, C = x.shape
    return np.ascontiguousarray(x.reshape(T, C // 128, 128).transpose(2, 1, 0))


def _fm_inv(y):
    p, kc, T = y.shape
    return np.ascontiguousarray(y.transpose(2, 1, 0).reshape(T, kc * 128))


def _vec(v):
    lead = v.shape[:-1]
    C = v.shape[-1]
    r = v.reshape(lead + (C // 128, 128))
    return np.ascontiguousarray(np.moveaxis(r, -1, 0))


def make_consts():
    p = np.arange(C64)[:, None]
    x = np.arange(C64)[None, :]
    return dict(
        c_ident=np.eye(128, dtype=np.float32),
        c_tri=(p <= x).astype(np.float32),
        c_mup=np.where(x > p, NEG, 0.0).astype(np.float32),
        c_mlo=np.where(x < p, NEG, 0.0).astype(np.float32),
        c_mstrict=(x < p).astype(np.float32),
    )


def run(cfg, inp, nseq_cores=None):
    c = cfg
    NC = c.NCORES
    B = inp['x_prompt'].shape[0]
    NS = c.NS
    prog = Prog(c)
    nc = prog.build()
    f = lambda a: np.ascontiguousarray(np.asarray(a, dtype=np.float32))
    shared = dict(
        gmix=_vec(f(inp['g_mix'])), gffn=_vec(f(inp['g_ffn'])), gple=_vec(f(inp['g_ple'])),
        gfin=_vec(f(inp['g_final'])[None]),
        cwdw=np.ascontiguousarray(_vec(f(inp['conf_w_dw'][0])).transpose(0, 2, 1)),
        cbdw=_vec(f(inp['conf_b_dw'][0])), clng=_vec(f(inp['conf_ln_g'][0])), clnb=_vec(f(inp['conf_ln_b'][0])),
        gwc=np.ascontiguousarray(_vec(f(inp['gdn_w_conv'][0])).transpose(0, 2, 1)),
        alog=np.ascontiguousarray(np.broadcast_to(f(inp['gdn_a_log'][0])[None, :], (C64, c.H))),
        dtb=np.ascontiguousarray(np.broadcast_to(f(inp['gdn_dt_bias'][0])[None, :], (C64, c.H))),
        gon=np.ascontiguousarray(f(inp['gdn_g_onorm'][0])[:, None]),
    )
    shared.update(make_consts())
    for n in ('conf_w_pw1', 'conf_w_pw2', 'gdn_w_in', 'gdn_w_out', 'ffn_w_gate', 'ffn_w_up', 'ffn_w_down', 'ple_w_gate',
              'ple_w_proj'):
        shared[n] = f(inp[n])
    xp, xs = f(inp['x_prompt']), f(inp['x_sample'])
    pp, psm = f(inp['p_prompt']), f(inp['p_sample'])
    scc, scq, sg = f(inp['state_conv_conformer']), f(inp['state_conv_qkv']), f(inp['state_gdn'])
    in_maps = []
    for ci in range(NC):
        sq = ci % B
        sl = slice(ci * NS, (ci + 1) * NS)
        m = dict(shared)
        m['xpT'] = _fm(xp[sq])
        m['xsT'] = _fm(xs[sl, 0])
        for l in range(2):
            m[f'ppT{l}'] = _fm(pp[l, sq])
            m[f'psT{l}'] = _fm(psm[l, sl, 0])
        m['cs_in'] = np.ascontiguousarray(scc[0, sl].reshape(NS, CW - 1, c.KC, 128).transpose(3, 2, 0, 1))
        m['qs_in'] = np.ascontiguousarray(scq[0, sl].reshape(NS, SC - 1, c.QC, 128).transpose(3, 2, 0, 1))
        m['gs_in'] = np.ascontiguousarray(sg[0, sl].transpose(0, 2, 1, 3))
        in_maps.append(m)
    res = run_bass_kernel_spmd(nc, in_maps, core_ids=list(range(NC)))
    R = res.results
    D = c.D
    y_p = np.stack([_fm_inv(R[b]['ypT']) for b in range(B)])
    y_s = np.concatenate([_fm_inv(R[ci]['ysT']) for ci in range(NC)])[:, None, :]
    conf_p = np.stack([_fm_inv(R[b]['conf_p']) for b in range(B)])[None]
    qkv_p = np.stack([_fm_inv(R[b]['qkv_p']) for b in range(B)])[None]
    gdn_p = np.stack([np.ascontiguousarray(R[b]['gdn_p'].transpose(1, 0, 2)) for b in range(B)])[None]
    conf_s = np.concatenate([np.ascontiguousarray(R[ci]['conf_s'].transpose(2, 3, 1, 0)).reshape(NS, CW - 1, D)
                             for ci in range(NC)])[None]
    qkv_s = np.concatenate([np.ascontiguousarray(R[ci]['qkv_s'].transpose(2, 3, 1, 0)).reshape(NS, SC - 1, c.QKV)
                            for ci in range(NC)])[None]
    gdn_s = np.concatenate([np.ascontiguousarray(R[ci]['gdn_s'].transpose(0, 2, 1, 3)) for ci in range(NC)])[None]
    return (y_p, y_s, conf_p, qkv_p, gdn_p, conf_s, qkv_s, gdn_s)


def kernel(**inputs):
    cfg = Cfg()
    return run(cfg, inputs)
```

```python
import contextlib
import numpy as np
import concourse.bass as bass
import concourse.mybir as mybir
from concourse.bass_utils import run_bass_kernel_spmd

F32 = mybir.dt.float32
BF16 = mybir.dt.bfloat16
AF = mybir.ActivationFunctionType
ALU = mybir.AluOpType
AX = mybir.AxisListType

RMS_EPS = 1e-6
LN_EPS = 1e-5
L2_EPS = 1e-6
CW = 31
SC = 4
C64 = 64
NEG = 30000.0


class Cfg:
    def __init__(self, D=4096, F=11008, H=32, PLE=256, SEQ=2048, NS=16, TT=256, NCORES=8, NWB=3, ACTIVE=None):
        self.D, self.F, self.H, self.PLE, self.SEQ, self.NS, self.TT = D, F, H, PLE, SEQ, NS, TT
        self.NCORES, self.NWB = NCORES, NWB
        self.ACTIVE = list(ACTIVE) if ACTIVE is not None else list(range(NCORES))
        self.KC = D // 128
        self.FC = F // 128
        self.QC = 3 * H
        self.QKV = H * 384
        self.PROJ = self.QKV + H * 128 + 2 * H
        self.PC = PLE // 128
        self.NT = SEQ // TT
        self.NCH = TT // C64
        ng = -(-self.FC // 32)
        base = self.FC // ng
        self.FG = []
        s = 0
        for i in range(ng):
            n = base + (1 if i < self.FC - base * ng else 0)
            self.FG.append((s, n))
            s += n
        self.FGMAX = max(n for _, n in self.FG)


class Sched:
    ENG = ('pe', 'act', 'dve', 'pool', 'sp')

    def __init__(self, nc, stack):
        self.nc = nc
        self.stack = stack
        self.eng = dict(pe=nc.tensor, act=nc.scalar, dve=nc.vector, pool=nc.gpsimd, sp=nc.sync)
        self.prog = {e: [] for e in self.ENG}
        self.psem = {e: stack.enter_context(nc.semaphore(f"prog_{e}")) for e in self.ENG if e != 'sp'}
        self.pcnt = {e: 0 for e in self.ENG}
        self.seen = {e: {} for e in self.ENG}
        self.last_w = {}
        self.readers = {}
        self.dsem = {}
        self.dcnt = {}
        self.dry = False
        self.ns = None
        self.cap = None

    def _deps(self, e, reads, writes):
        best = {}
        for k in reads:
            t = self.last_w.get(k)
            if t is not None and best.get(t[0], 0) < t[1]:
                best[t[0]] = t[1]
        for k in writes:
            t = self.last_w.get(k)
            if t is not None and best.get(t[0], 0) < t[1]:
                best[t[0]] = t[1]
            for t in self.readers.get(k, ()):
                if best.get(t[0], 0) < t[1]:
                    best[t[0]] = t[1]
        seen = self.seen[e]
        for s, v in best.items():
            if seen.get(s, 0) < v:
                seen[s] = v
                self.prog[e].append(('w', s, v))

    def _commit(self, tok, reads, writes):
        for k in writes:
            self.last_w[k] = tok
            self.readers[k] = []
        for k in reads:
            if k in writes:
                continue
            self.readers.setdefault(k, []).append(tok)

    PRIV = ('knb', 'qnb', 'qrb', 'Sbf')

    def _nsk(self, k):
        if isinstance(k, str):
            return (k, 'ns', self.ns) if (k.startswith('t_') or k in self.PRIV) else k
        if isinstance(k, tuple) and isinstance(k[0], str) and k[0].startswith('t_'):
            return k + ('ns', self.ns)
        return k

    def op(self, e, fn, reads=(), writes=()):
        if self.dry:
            return
        if self.ns is not None:
            reads = [self._nsk(k) for k in reads]
            writes = [self._nsk(k) for k in writes]
        if self.cap is not None:
            self.cap.append(('op', e, fn, reads, writes))
            return
        px = [k for k in reads if isinstance(k, tuple) and k[0] == 'ps']
        if px:
            reads = [k for k in reads if not (isinstance(k, tuple) and k[0] == 'ps')]
            writes = list(writes) + [k for k in px if k not in writes]
        self._deps(e, reads, writes)
        self.pcnt[e] += 1
        tok = (('p', e), self.pcnt[e])
        self.prog[e].append(('o', fn, ('p', e)))
        self._commit(tok, reads, writes)

    def dma(self, e, sem_name, fn, reads=(), writes=()):
        if self.dry:
            return
        if self.cap is not None:
            self.cap.append(('dma', e, sem_name, fn, reads, writes))
            return
        if sem_name not in self.dsem:
            self.dsem[sem_name] = self.stack.enter_context(self.nc.semaphore(f"dma_{sem_name}"))
            self.dcnt[sem_name] = 0
        self._deps(e, reads, writes)
        self.dcnt[sem_name] += 16
        tok = (('d', sem_name), self.dcnt[sem_name])
        self.prog[e].append(('d', fn, ('d', sem_name)))
        self._commit(tok, reads, writes)

    def replay_zipped(self, caps):
        for i in range(max(len(x) for x in caps)):
            for x in caps:
                if i < len(x):
                    it = x[i]
                    if it[0] == 'op':
                        self.op(*it[1:])
                    else:
                        self.dma(*it[1:])

    def wait_all(self, e):
        allt = [(('p', x), self.pcnt[x]) for x in self.psem] + [(('d', n), c) for n, c in self.dcnt.items()]
        for s, c in allt:
            if c > 0 and self.seen[e].get(s, 0) < c:
                self.seen[e][s] = c
                self.prog[e].append(('w', s, c))

    def _sem(self, s):
        return self.psem[s[1]] if s[0] == 'p' else self.dsem[s[1]]

    def emit(self):
        nc = self.nc

        def run(e):
            eng = self.eng[e]
            for it in self.prog[e]:
                if it[0] == 'w':
                    eng.wait_ge(self._sem(it[1]), it[2])
                elif it[0] == 'o':
                    it[1](eng).then_inc(self._sem(it[2]), 1)
                else:
                    it[1](eng).then_inc(self._sem(it[2]), 16)

        with nc.Block() as block:
            @block.tensor
            def _(eng):
                run('pe')

            @block.scalar
            def _(eng):
                run('act')

            @block.vector
            def _(eng):
                run('dve')

            @block.gpsimd
            def _(eng):
                run('pool')

            @block.sync
            def _(eng):
                run('sp')
        self.prog = {e: [] for e in self.ENG}


class Prog:
    def __init__(self, cfg):
        self.c = cfg
        self.nc = bass.Bass("TRN2", target_bir_lowering=False)
        self.uid = 0
        self.aux_set = None

    def sb(self, name, shape, dt=F32):
        return self.st.enter_context(self.nc.sbuf_tensor("sb_" + name, list(shape), dt))

    def din(self, name, shape, dt=F32):
        return self.nc.dram_tensor(name, list(shape), dt, kind="ExternalInput").ap()

    def dout(self, name, shape, dt=F32):
        return self.nc.dram_tensor(name, list(shape), dt, kind="ExternalOutput").ap()

    def ps(self, kind):
        if kind == 'lin':
            r = self.ps_lin_i % 4
            self.ps_lin_i += 1
            return self.psb[r][:, 0:256], ('ps', r)
        if self.aux_set is not None:
            st_ = self.aux_set
            r = st_[self.ps_aux_i % len(st_)]
        else:
            r = self.ps_aux_i % 4
        self.ps_aux_i += 1
        return self.psb[4 + r][:, 0:256], ('ps', 4 + r)

    def tmp(self, kind):
        n = len(self.tmps)
        i = self.tmp_i % n
        self.tmp_i += 1
        return self.tmps[i], ('tmp', i)

    def wget(self, src, nk, ncols, tag, hold=0):
        c = self.c
        if self.S.dry:
            self.plan.append((src, nk, ncols, tag))
            return None, None
        i = self.wpos
        self.wpos += 1
        assert self.plan[i % len(self.plan)][3] == tag, (self.plan[i % len(self.plan)][3], tag)
        self._wissue(min(i - hold + c.NWB - 1, self.wtotal - 1))
        slot = i % c.NWB
        return self.wb[slot], ('wb', slot)

    def _wissue(self, upto):
        c = self.c
        S = self.S
        NP = len(self.plan)
        while self.wissued <= upto:
            g = self.wissued
            self.wissued += 1
            src, nk, ncols, tag = self.plan[g % NP]
            slot = g % c.NWB
            pid = g % NP
            dst = self.wb[slot][:, 0:nk, 0:ncols]
            cache = self.wcache[pid]
            cview = cache[:, 0:nk * ncols].rearrange("p (k n) -> p k n", k=nk)
            if g < NP:
                srcv = src.rearrange("(kc p) n -> p kc n", p=128)
                S.dma('pool', f'wl{slot}', (lambda e, dst=dst, srcv=srcv: e.dma_start(out=dst, in_=srcv)),
                      writes=[('wb', slot)])
                if self.wtotal > NP:
                    S.dma('sp', f'wc{slot}', (lambda e, dst=dst, cview=cview: e.dma_start(out=cview, in_=dst)),
                          reads=[('wb', slot)], writes=[('wcache', pid)])
            else:
                S.dma('sp', f'wh{slot}', (lambda e, dst=dst, cview=cview: e.dma_start(out=dst, in_=cview)),
                      reads=[('wcache', pid)], writes=[('wb', slot)])

    def mm_acc(self, psap, pskey, parts, W, extra_reads=()):
        S = self.S
        if S.dry:
            return
        seq = []
        reads = list(extra_reads)
        for (wt, wk, kcs, moff, msz, it, ikcs, ikeys) in parts:
            for a, b in zip(kcs, ikcs):
                seq.append((wt[:, a, moff:moff + msz], it[:, b, 0:W]))
            reads.append(wk)
            reads.extend(ikeys)
        n = len(seq)

        def fn(e, seq=seq, psap=psap, W=W, n=n):
            ins = None
            for i, (l, r) in enumerate(seq):
                ins = e.matmul(psap[0:l.shape[-1], 0:W], lhsT=l, rhs=r, start=(i == 0), stop=(i == n - 1))
            return ins
        S.op('pe', fn, reads=reads, writes=[pskey])

    def linear(self, Wsrc, K_chunks, col0, ncols_total, in_tile, in_key_fn, W, cb, tag, k0=0):
        npan = -(-ncols_total // 256)
        for pi in range(npan):
            c0 = col0 + pi * 256
            ncol = min(256, col0 + ncols_total - c0)
            subs = []
            kk = 0
            while kk < K_chunks:
                nk = min(32, K_chunks - kk)
                src = Wsrc[(k0 + kk) * 128:(k0 + kk + nk) * 128, c0:c0 + ncol]
                wt, wk = self.wget(src, nk, ncol, (tag, pi, kk), hold=len(subs))
                subs.append((wt, wk, kk, nk))
                kk += nk
            for mi in range(ncol // 128):
                if self.S.dry:
                    continue
                psap, pskey = self.ps('lin')
                parts = []
                for (wt, wk, kk, nk) in subs:
                    parts.append((wt, wk, list(range(nk)), mi * 128, 128, in_tile,
                                  list(range(kk, kk + nk)), [in_key_fn(k) for k in range(kk, kk + nk)]))
                self.mm_acc(psap, pskey, parts, W)
                cb(pi * 2 + mi, psap, pskey)

    def colsum_sq(self, src_fn, nchunks, W, scale_inv, eps, out_rstd, out_key):
        S = self.S
        if S.dry:
            return
        psap, pskey = self.ps('aux')
        for k in range(nchunks):
            sap, skey = src_fn(k)
            t, tk = self.tmp('sq')
            S.op('act', (lambda e, t=t, sap=sap, W=W: e.activation(out=t[:, 0:W], in_=sap, func=AF.Square)),
                 reads=[skey], writes=[tk])
            S.op('pe', (lambda e, t=t, psap=psap, W=W, k=k, n=nchunks: e.matmul(
                psap[:, 0:W], lhsT=self.ones_f[:, :], rhs=t[:, 0:W], start=(k == 0), stop=(k == n - 1))),
                 reads=[tk, 'const'], writes=[pskey])
        S.op('dve', (lambda e, psap=psap, W=W: e.tensor_scalar(out=out_rstd[:, 0:W], in0=psap[:, 0:W], scalar1=scale_inv,
                                                           scalar2=eps, op0=ALU.mult, op1=ALU.add)),
             reads=[pskey], writes=[out_key])
        self.rsqrt_(out_rstd[:, 0:W], out_key)

    def rsqrt_(self, ap, key):
        S = self.S
        S.op('act', (lambda e, ap=ap: e.activation(out=ap, in_=ap, func=AF.Sqrt)), reads=[key], writes=[key])
        S.op('dve', (lambda e, ap=ap: e.reciprocal(out=ap, in_=ap)), reads=[key], writes=[key])

    def rmsnorm(self, gtile, gl, W):
        c = self.c
        S = self.S
        if S.dry:
            return
        h, u = self.h, self.u
        self.colsum_sq(lambda k: (h[:, k, 0:W], ('h', k)), c.KC, W, 1.0 / c.D, RMS_EPS, self.rstd, 'rstd')
        for k in range(c.KC):
            S.op('dve', (lambda e, k=k, W=W: e.scalar_tensor_tensor(
                out=u[:, k, 0:W], in0=h[:, k, 0:W], scalar=gtile[:, gl, k:k + 1], in1=self.rstd[:, 0:W],
                op0=ALU.mult, op1=ALU.mult)), reads=[('h', k), 'rstd', 'const'], writes=[('u', k)])

    def conformer(self, W, sample, t0):
        c = self.c
        S = self.S
        KC = c.KC
        h, u = self.h, self.u
        Wp1 = self.w['conf_w_pw1'][0]
        Wp2 = self.w['conf_w_pw2'][0]
        cbuf = self.scr
        glub = self.scr_glu
        ps_mean = ps_var = None
        if not S.dry:
            ps_mean, km = self.ps('aux')
            ps_var, kv = self.ps('aux')
        if sample and not S.dry:
            pass
        for j in range(KC // 2):
            a_src = Wp1[:, j * 256:(j + 1) * 256]
            g_src = Wp1[:, c.D + j * 256:c.D + (j + 1) * 256]
            wa, wak = self.wget(a_src, KC, 256, ('pw1a', j))
            wg, wgk = self.wget(g_src, KC, 256, ('pw1g', j), hold=1)
            if S.dry:
                continue
            caps = []
            for mi in range(2):
                if not sample:
                    S.cap = []
                    caps.append(S.cap)
                bi = 0 if sample else mi
                ch = 2 * j + mi
                gi = ch % 4
                pa, pak = self.ps('lin')
                pg, pgk = self.ps('lin')
                ukeys = [('u', k) for k in range(KC)]
                self.mm_acc(pa, pak, [(wa, wak, list(range(KC)), mi * 128, 128, u, list(range(KC)), ukeys)], W)
                self.mm_acc(pg, pgk, [(wg, wgk, list(range(KC)), mi * 128, 128, u, list(range(KC)), ukeys)], W)
                sg, sgk = self.c_a[bi], ('c_a', bi)
                S.op('act', (lambda e, sg=sg, pg=pg, W=W: e.activation(out=sg[:, 0:W], in_=pg[:, 0:W], func=AF.Sigmoid)),
                     reads=[pgk], writes=[sgk])
                acc, acck = self.c_b[bi], ('c_b', bi)
                if not sample:
                    gk = ('glu', gi)
                    S.op('act', (lambda e, gi=gi, ch=ch: e.activation(out=glub[:, gi, 0:30], in_=self.chalo[:, ch, :],
                                                                     func=AF.Copy)),
                         reads=[('chalo', ch)], writes=[gk])
                    S.op('dve', (lambda e, gi=gi, pa=pa, sg=sg, W=W: e.tensor_tensor(
                        out=glub[:, gi, 30:30 + W], in0=pa[:, 0:W], in1=sg[:, 0:W], op=ALU.mult)),
                         reads=[pak, sgk, gk], writes=[gk])
                    S.op('act', (lambda e, gi=gi, ch=ch, W=W: e.activation(out=self.chalo[:, ch, :],
                                                                          in_=glub[:, gi, W:W + 30], func=AF.Copy)),
                         reads=[gk], writes=[('chalo', ch)])
                    for w in range(CW):
                        if w == 0:
                            S.op('dve', (lambda e, acc=acc, gi=gi, ch=ch, W=W: e.tensor_scalar(
                                out=acc[:, 0:W], in0=glub[:, gi, 0:W], scalar1=self.cwdw[:, ch, 0:1],
                                scalar2=self.cbdw[:, ch:ch + 1], op0=ALU.mult, op1=ALU.add)),
                                 reads=[gk, 'const'], writes=[acck])
                        else:
                            S.op('dve', (lambda e, acc=acc, gi=gi, ch=ch, W=W, w=w: e.scalar_tensor_tensor(
                                out=acc[:, 0:W], in0=glub[:, gi, w:w + W], scalar=self.cwdw[:, ch, w:w + 1],
                                in1=acc[:, 0:W], op0=ALU.mult, op1=ALU.add)),
                                 reads=[gk, acck, 'const'], writes=[acck])
                else:
                    NS = c.NS
                    st, stk = self.cs_st[0], ('csst', 0)
                    nst, nstk = self.cs_new[0], ('csnew', 0)
                    S.dma('pool', 'csin', (lambda e, st=st, ch=ch: e.dma_start(out=st[:, :, :], in_=self.d_cs_in[:, ch, :, :])),
                          writes=[stk])
                    gl, glk = self.c_c[bi], ('c_c', bi)
                    S.op('dve', (lambda e, gl=gl, pa=pa, sg=sg, W=W: e.tensor_tensor(
                        out=gl[:, 0:W], in0=pa[:, 0:W], in1=sg[:, 0:W], op=ALU.mult)),
                         reads=[pak, sgk], writes=[glk])
                    S.op('act', (lambda e, st=st, nst=nst: e.activation(out=nst[:, :, 0:29], in_=st[:, :, 1:30], func=AF.Copy)),
                         reads=[stk], writes=[nstk])
                    S.op('act', (lambda e, gl=gl, nst=nst, W=W: e.activation(out=nst[:, :, 29], in_=gl[:, 0:W], func=AF.Copy)),
                         reads=[glk, nstk], writes=[nstk])
                    S.dma('pool', 'csout', (lambda e, nst=nst, ch=ch: e.dma_start(out=self.d_conf_s[:, ch, :, :], in_=nst[:, :, :])),
                          reads=[nstk], writes=[('d_conf_s', ch)])
                    pr, prk = self.cs_prod[0], ('csprod', 0)
                    S.op('dve', (lambda e, st=st, ch=ch, pr=pr, NS=NS: e.tensor_tensor(
                        out=pr[:, :, :], in0=st[:, :, :], in1=self.cwdw[:, ch, 0:30].unsqueeze(1).to_broadcast([128, NS, 30]),
                        op=ALU.mult)), reads=[stk, 'const'], writes=[prk])
                    S.op('dve', (lambda e, acc=acc, pr=pr, W=W: e.tensor_reduce(out=acc[:, 0:W], in_=pr[:, :, :], axis=AX.X, op=ALU.add)),
                         reads=[prk], writes=[acck])
                    S.op('dve', (lambda e, acc=acc, gl=gl, ch=ch, W=W: e.scalar_tensor_tensor(
                        out=acc[:, 0:W], in0=gl[:, 0:W], scalar=self.cwdw[:, ch, 30:31], in1=acc[:, 0:W],
                        op0=ALU.mult, op1=ALU.add)), reads=[glk, acck, 'const'], writes=[acck])
                    S.op('dve', (lambda e, acc=acc, ch=ch, W=W: e.tensor_scalar(
                        out=acc[:, 0:W], in0=acc[:, 0:W], scalar1=self.cbdw[:, ch:ch + 1], scalar2=None, op0=ALU.add)),
                         reads=[acck, 'const'], writes=[acck])
                S.op('pe', (lambda e, acc=acc, W=W, ch=ch: e.matmul(ps_mean[:, 0:W], lhsT=self.ones_f[:, :], rhs=acc[:, 0:W],
                                                               start=(ch == 0), stop=(ch == KC - 1))),
                     reads=[acck, 'const'], writes=[km])
                sq, sqk = self.c_a[bi], ('c_a', bi)
                S.op('act', (lambda e, sq=sq, acc=acc, W=W: e.activation(out=sq[:, 0:W], in_=acc[:, 0:W], func=AF.Square)),
                     reads=[acck], writes=[sqk])
                S.op('pe', (lambda e, sq=sq, W=W, ch=ch: e.matmul(ps_var[:, 0:W], lhsT=self.ones_f[:, :], rhs=sq[:, 0:W],
                                                             start=(ch == 0), stop=(ch == KC - 1))),
                     reads=[sqk, 'const'], writes=[kv])
                S.op('act', (lambda e, acc=acc, W=W, ch=ch: e.activation(out=cbuf[:, ch, 0:W], in_=acc[:, 0:W], func=AF.Copy)),
                     reads=[acck], writes=[('scr', ch)])
                S.cap = None
            if caps:
                S.replay_zipped(caps)
        if not S.dry:
            mean, var = self.mean, self.rstd
            invD = 1.0 / c.D
            S.op('dve', (lambda e, W=W: e.tensor_scalar(out=mean[:, 0:W], in0=ps_mean[:, 0:W], scalar1=invD, scalar2=None,
                                                      op0=ALU.mult)), reads=[km], writes=['mean'])
            msq, msqk = self.tmp('msq')
            S.op('dve', (lambda e, W=W, msq=msq: e.tensor_tensor(out=msq[:, 0:W], in0=mean[:, 0:W], in1=mean[:, 0:W], op=ALU.mult)),
                 reads=['mean'], writes=[msqk])
            S.op('dve', (lambda e, W=W, msq=msq: e.scalar_tensor_tensor(out=var[:, 0:W], in0=ps_var[:, 0:W], scalar=invD,
                                                                      in1=msq[:, 0:W], op0=ALU.mult, op1=ALU.subtract)),
                 reads=[kv, msqk], writes=['rstd'])
            S.op('dve', (lambda e, W=W: e.tensor_scalar(out=var[:, 0:W], in0=var[:, 0:W], scalar1=LN_EPS, scalar2=None,
                                                      op0=ALU.add)), reads=['rstd'], writes=['rstd'])
            self.rsqrt_(var[:, 0:W], 'rstd')
            for ch in range(KC):
                t1, t1k = self.tmp('ln1')
                S.op('dve', (lambda e, t1=t1, ch=ch, W=W: e.tensor_tensor(out=t1[:, 0:W], in0=cbuf[:, ch, 0:W], in1=mean[:, 0:W],
                                                                        op=ALU.subtract)),
                     reads=[('scr', ch), 'mean'], writes=[t1k])
                S.op('dve', (lambda e, t1=t1, W=W: e.tensor_tensor(out=t1[:, 0:W], in0=t1[:, 0:W], in1=var[:, 0:W], op=ALU.mult)),
                     reads=[t1k, 'rstd'], writes=[t1k])
                S.op('act', (lambda e, t1=t1, ch=ch, W=W: e.activation(out=u[:, ch, 0:W], in_=t1[:, 0:W], func=AF.Silu,
                                                                     bias=self.clnb[:, ch:ch + 1], scale=self.clng[:, ch:ch + 1])),
                     reads=[t1k, 'const'], writes=[('u', ch)])

        def cb(mi, psap, pskey):
            S.op('dve', (lambda e, mi=mi, psap=psap, W=W: e.tensor_tensor(out=h[:, mi, 0:W], in0=h[:, mi, 0:W], in1=psap[:, 0:W],
                                                                        op=ALU.add)),
                 reads=[pskey, ('h', mi)], writes=[('h', mi)])
        self.linear(Wp2, KC, 0, c.D, u, lambda k: ('u', k), W, cb, 'pw2')

    def ffn(self, layer, W):
        c = self.c
        S = self.S
        KC = c.KC
        h, u = self.h, self.u
        Wg = self.w['ffn_w_gate'][layer]
        Wu = self.w['ffn_w_up'][layer]
        Wd = self.w['ffn_w_down'][layer]
        hid = self.scr
        ukeys = [('u', k) for k in range(KC)]
        for (f0, fn_) in c.FG:
            j = 0
            while j < fn_:
                ncol = min(2, fn_ - j) * 128
                c0 = (f0 + j) * 128
                wg, wgk = self.wget(Wg[:, c0:c0 + ncol], KC, ncol, ('ffg', layer, f0 + j))
                wu, wuk = self.wget(Wu[:, c0:c0 + ncol], KC, ncol, ('ffu', layer, f0 + j), hold=1)
                if not S.dry:
                    for mi in range(ncol // 128):
                        jj = j + mi
                        pg, pgk = self.ps('lin')
                        pu, puk = self.ps('lin')
                        self.mm_acc(pg, pgk, [(wg, wgk, list(range(KC)), mi * 128, 128, u, list(range(KC)), ukeys)], W)
                        self.mm_acc(pu, puk, [(wu, wuk, list(range(KC)), mi * 128, 128, u, list(range(KC)), ukeys)], W)
                        sg, sgk = self.tmp('silu')
                        S.op('act', (lambda e, sg=sg, pg=pg, W=W: e.activation(out=sg[:, 0:W], in_=pg[:, 0:W], func=AF.Silu)),
                             reads=[pgk], writes=[sgk])
                        S.op('dve', (lambda e, sg=sg, pu=pu, jj=jj, W=W: e.tensor_tensor(out=hid[:, jj, 0:W], in0=pu[:, 0:W],
                                                                                     in1=sg[:, 0:W], op=ALU.mult)),
                             reads=[puk, sgk], writes=[('scr', jj)])
                j += 2

            def cb(mi, psap, pskey):
                S.op('dve', (lambda e, mi=mi, psap=psap, W=W: e.tensor_tensor(out=h[:, mi, 0:W], in0=h[:, mi, 0:W],
                                                                            in1=psap[:, 0:W], op=ALU.add)),
                     reads=[pskey, ('h', mi)], writes=[('h', mi)])
            self.linear(Wd, fn_, 0, c.D, hid, lambda k: ('scr', k), W, cb, ('ffd', layer, f0), k0=f0)

    def ple(self, layer, W, sample, t0):
        c = self.c
        S = self.S
        h, u = self.h, self.u
        Wpg = self.w['ple_w_gate'][layer]
        Wpp = self.w['ple_w_proj'][layer]
        pT = self.pT
        if not S.dry:
            src = (self.d_psT[layer] if sample else self.d_ppT[layer][:, :, t0:t0 + W])
            S.dma('pool', 'pin', (lambda e, src=src, W=W: e.dma_start(out=pT[:, :, 0:W], in_=src)), writes=['pT'])
        pkeys = ['pT'] * c.PC
        for j in range(c.KC // 2):
            wg, wgk = self.wget(Wpg[:, j * 256:(j + 1) * 256], c.KC, 256, ('pleg', layer, j))
            wp, wpk = self.wget(Wpp[:, j * 256:(j + 1) * 256], c.PC, 256, ('plep', layer, j), hold=1)
            if S.dry:
                continue
            for mi in range(2):
                ch = 2 * j + mi
                pg, pgk = self.ps('lin')
                pp, ppk = self.ps('lin')
                self.mm_acc(pg, pgk, [(wg, wgk, list(range(c.KC)), mi * 128, 128, u, list(range(c.KC)),
                                       [('u', k) for k in range(c.KC)])], W)
                self.mm_acc(pp, ppk, [(wp, wpk, list(range(c.PC)), mi * 128, 128, pT, list(range(c.PC)), pkeys)], W)
                sg, sgk = self.tmp('sig')
                S.op('act', (lambda e, sg=sg, pg=pg, W=W: e.activation(out=sg[:, 0:W], in_=pg[:, 0:W], func=AF.Sigmoid)),
                     reads=[pgk], writes=[sgk])
                S.op('dve', (lambda e, sg=sg, pp=pp, W=W: e.tensor_tensor(out=sg[:, 0:W], in0=pp[:, 0:W], in1=sg[:, 0:W],
                                                                        op=ALU.mult)), reads=[ppk, sgk], writes=[sgk])
                S.op('dve', (lambda e, sg=sg, ch=ch, W=W: e.tensor_tensor(out=h[:, ch, 0:W], in0=h[:, ch, 0:W], in1=sg[:, 0:W],
                                                                        op=ALU.add)),
                     reads=[sgk, ('h', ch)], writes=[('h', ch)])

    def gdn_gates(self, W, ntok_blocks):
        c = self.c
        S = self.S
        H = c.H
        Win = self.w['gdn_w_in'][0]
        cb0 = c.QKV + H * 128
        if not S.dry:
            S.dma('pool', 'wba', (lambda e: e.dma_start(out=self.wba[:, :, :],
                                                       in_=Win[:, cb0:cb0 + 2 * H].rearrange("(kc p) n -> p kc n", p=128))),
                  writes=['wba'])
        if S.dry:
            return
        tb = min(W, C64)
        for blk in range(ntok_blocks):
            psap, pskey = self.ps('aux')
            u = self.u

            def fn(e, blk=blk, psap=psap, tb=tb):
                ins = None
                for k in range(c.KC):
                    ins = e.matmul(psap[0:tb, 0:2 * H], lhsT=u[:, k, blk * tb:(blk + 1) * tb], rhs=self.wba[:, k, :],
                                   start=(k == 0), stop=(k == c.KC - 1))
                return ins
            S.op('pe', fn, reads=['wba'] + [('u', k) for k in range(c.KC)], writes=[pskey])
            S.op('act', (lambda e, blk=blk, psap=psap, tb=tb: e.activation(out=self.g_beta[0:tb, blk, :], in_=psap[0:tb, 0:H],
                                                                         func=AF.Sigmoid)),
                 reads=[pskey], writes=['g_beta'])
            S.op('dve', (lambda e, blk=blk, psap=psap, tb=tb: e.tensor_tensor(out=self.g_g[0:tb, blk, :], in0=psap[0:tb, H:2 * H],
                                                                            in1=self.dtb[0:tb, :], op=ALU.add)),
                 reads=[pskey, 'const'], writes=['g_g'])
            S.op('act', (lambda e, blk=blk, tb=tb: e.activation(out=self.g_g[0:tb, blk, :], in_=self.g_g[0:tb, blk, :], func=AF.Exp)),
                 reads=['g_g'], writes=['g_g'])
            S.op('dve', (lambda e, blk=blk, tb=tb: e.tensor_scalar(out=self.g_g[0:tb, blk, :], in0=self.g_g[0:tb, blk, :],
                                                                 scalar1=1.0, scalar2=None, op0=ALU.add)),
                 reads=['g_g'], writes=['g_g'])
            S.op('act', (lambda e, blk=blk, tb=tb: e.activation(out=self.g_g[0:tb, blk, :], in_=self.g_g[0:tb, blk, :], func=AF.Ln)),
                 reads=['g_g'], writes=['g_g'])
            S.op('dve', (lambda e, blk=blk, tb=tb: e.tensor_tensor(out=self.g_g[0:tb, blk, :], in0=self.g_g[0:tb, blk, :],
                                                                 in1=self.nA[0:tb, :], op=ALU.mult)),
                 reads=['g_g', 'const'], writes=['g_g'])

    def gdn_proj_head(self, hd, W, which, sample):
        c = self.c
        S = self.S
        H = c.H
        KC = c.KC
        j = hd // 2
        Win = self.w['gdn_w_in'][0]
        nh = min(2, H - 2 * j)
        dsts = {'q': self.hq, 'k': self.hk, 'v': self.hv}
        par = 0
        qst = qnew = None
        if sample and not S.dry:
            qst, qnew = self.qs_st[par], self.qs_new[par]
            for idx in range(3):
                ch0 = idx * H + 2 * j
                S.dma('pool', f'qsin{par}', (lambda e, qst=qst, idx=idx, ch0=ch0: e.dma_start(
                    out=qst[:, idx, 0:nh, :, :], in_=self.d_qs_in[:, ch0:ch0 + nh, :, :])), writes=[('qs_st', par)])
            S.op('act', (lambda e, qst=qst, qnew=qnew: e.activation(out=qnew[:, :, :, :, 0:2], in_=qst[:, :, :, :, 1:3],
                                                                    func=AF.Copy)),
                 reads=[('qs_st', par)], writes=[('qs_new', par)])
        for idx, nm in enumerate(('q', 'k', 'v', 'z')):
            colbase = (idx * H * 128 if nm != 'z' else c.QKV) + j * 256
            wt, wk = self.wget(Win[:, colbase:colbase + nh * 128], KC, nh * 128, ('gin', nm, j))
            if S.dry:
                continue
            for hh in range(nh):
                head = 2 * j + hh
                psap, pskey = self.ps('lin')
                self.mm_acc(psap, pskey, [(wt, wk, list(range(KC)), hh * 128, 128, self.u, list(range(KC)),
                                           [('u', k) for k in range(KC)])], W)
                if nm == 'z':
                    S.op('act', (lambda e, hh=hh, psap=psap, W=W: e.activation(out=self.hz[hh][:, 0:W], in_=psap[:, 0:W],
                                                                             func=AF.Silu)),
                         reads=[pskey], writes=[('hz', hh)])
                    continue
                qch = idx * H + head
                dst = dsts[nm][hh]
                dkey = ('h' + nm, hh)
                if not sample:
                    cv, cvk = self.cvb[idx % 2][hh], ('cvb', idx % 2, hh)
                    S.op('act', (lambda e, cv=cv, qch=qch: e.activation(out=cv[:, 0:3], in_=self.qhalo[:, qch, :], func=AF.Copy)),
                         reads=[('qhalo', qch)], writes=[cvk])
                    S.op('act', (lambda e, cv=cv, psap=psap, W=W: e.activation(out=cv[:, 3:3 + W], in_=psap[:, 0:W], func=AF.Copy)),
                         reads=[pskey, cvk], writes=[cvk])
                    S.op('act', (lambda e, cv=cv, qch=qch, W=W: e.activation(out=self.qhalo[:, qch, :], in_=cv[:, W:W + 3],
                                                                           func=AF.Copy)),
                         reads=[cvk], writes=[('qhalo', qch)])
                    for w in range(SC):
                        if w == 0:
                            S.op('dve', (lambda e, cv=cv, dst=dst, qch=qch, W=W: e.tensor_scalar(
                                out=dst[:, 0:W], in0=cv[:, 0:W], scalar1=self.gwc[:, qch, 0:1], scalar2=None, op0=ALU.mult)),
                                 reads=[cvk, 'const'], writes=[dkey])
                        else:
                            S.op('dve', (lambda e, cv=cv, dst=dst, qch=qch, W=W, w=w: e.scalar_tensor_tensor(
                                out=dst[:, 0:W], in0=cv[:, w:w + W], scalar=self.gwc[:, qch, w:w + 1], in1=dst[:, 0:W],
                                op0=ALU.mult, op1=ALU.add)), reads=[cvk, dkey, 'const'], writes=[dkey])
                else:
                    NS = c.NS
                    S.op('act', (lambda e, qnew=qnew, idx=idx, hh=hh, psap=psap, W=W: e.activation(
                        out=qnew[:, idx, hh, :, 2], in_=psap[:, 0:W], func=AF.Copy)),
                         reads=[pskey, ('qs_new', par)], writes=[('qs_new', par)])
                    pr, prk = self.qs_prod, 'qsprod'
                    S.op('dve', (lambda e, qst=qst, idx=idx, hh=hh, qch=qch, pr=pr, NS=NS: e.tensor_tensor(
                        out=pr[:, :, :], in0=qst[:, idx, hh, :, :],
                        in1=self.gwc[:, qch, 0:3].unsqueeze(1).to_broadcast([128, NS, 3]), op=ALU.mult)),
                         reads=[('qs_st', par), 'const'], writes=[prk])
                    S.op('dve', (lambda e, dst=dst, pr=pr, W=W: e.tensor_reduce(out=dst[:, 0:W], in_=pr[:, :, :], axis=AX.X,
                                                                               op=ALU.add)),
                         reads=[prk], writes=[dkey])
                    S.op('dve', (lambda e, dst=dst, psap=psap, qch=qch, W=W: e.scalar_tensor_tensor(
                        out=dst[:, 0:W], in0=psap[:, 0:W], scalar=self.gwc[:, qch, 3:4], in1=dst[:, 0:W],
                        op0=ALU.mult, op1=ALU.add)), reads=[pskey, dkey, 'const'], writes=[dkey])
                S.op('act', (lambda e, dst=dst, W=W: e.activation(out=dst[:, 0:W], in_=dst[:, 0:W], func=AF.Silu)),
                     reads=[dkey], writes=[dkey])
                if nm in ('q', 'k'):
                    rn, rnk = self.tmp('rn')
                    self.colsum_sq(lambda k, dst=dst, dkey=dkey, W=W: (dst[:, 0:W], dkey), 1, W, 1.0, L2_EPS, rn, rnk)
                    sc = (128.0 ** -0.5) if nm == 'q' else 1.0
                    S.op('dve', (lambda e, dst=dst, rn=rn, W=W, sc=sc: e.scalar_tensor_tensor(
                        out=dst[:, 0:W], in0=dst[:, 0:W], scalar=sc, in1=rn[:, 0:W], op0=ALU.mult, op1=ALU.mult)),
                         reads=[dkey, rnk], writes=[dkey])

        if sample and not S.dry:
            for idx in range(3):
                ch0 = idx * H + 2 * j
                S.dma('pool', f'qsout{par}', (lambda e, qnew=qnew, idx=idx, ch0=ch0: e.dma_start(
                    out=self.d_qkv_s[:, ch0:ch0 + nh, :, :], in_=qnew[:, idx, 0:nh, :, :])),
                      reads=[('qs_new', par)], writes=[('d_qkv_s', idx, j)])

    def gdn_prompt_head(self, hd, hh, W):
        c = self.c
        S = self.S
        H = c.H
        NCH = W // C64
        NW = NCH * C64
        T = self.Tb[hh]
        qn, kn, vv = self.hq[hh], self.hk[hh], self.hv[hh]
        kq, kk, kv_ = ('hq', hh), ('hk', hh), ('hv', hh)
        ident = self.ident_f
        knb, qnb = T.knb, T.qnb
        S.op('act', (lambda e: e.activation(out=knb[:, 0:NW], in_=kn[:, 0:NW], func=AF.Copy)), reads=[kk], writes=['knb'])
        gcol = self.g_g[0:C64, 0:NCH, hd]
        bcol = self.g_beta[0:C64, 0:NCH, hd]
        pG, pGk = self.ps('aux')
        S.op('pe', (lambda e: e.matmul(pG[0:C64, 0:NCH], lhsT=self.tri_f[0:C64, 0:C64], rhs=gcol, start=True, stop=True)),
             reads=['g_g', 'const'], writes=[pGk])
        S.op('pe', (lambda e: e.matmul(pG[:, 64:64 + NCH], lhsT=self.ones_f[0:C64, :], rhs=gcol, start=True, stop=True)),
             reads=['g_g', 'const', pGk], writes=[pGk])
        Gc = T.t_G
        S.op('dve', (lambda e: e.tensor_copy(out=Gc[:, 0:NCH], in_=pG[0:C64, 0:NCH])), reads=[pGk], writes=['t_G'])
        S.op('act', (lambda e: e.activation(out=T.t_alast[:, 0:NCH], in_=pG[:, 64:64 + NCH], func=AF.Exp)),
             reads=[pGk], writes=['t_alast'])
        S.op('dve', (lambda e: e.tensor_tensor(out=T.t_kdsc[:, 0:NCH], in0=pG[0:C64, 64:64 + NCH], in1=Gc[:, 0:NCH],
                                               op=ALU.subtract)), reads=[pGk, 't_G'], writes=['t_kdsc'])
        S.op('act', (lambda e: e.activation(out=T.t_kdsc[:, 0:NCH], in_=T.t_kdsc[:, 0:NCH], func=AF.Exp)),
             reads=['t_kdsc'], writes=['t_kdsc'])
        S.op('act', (lambda e: e.activation(out=T.t_eG[:, 0:NCH], in_=Gc[:, 0:NCH], func=AF.Exp)), reads=['t_G'],
             writes=['t_eG'])
        S.op('dve', (lambda e: e.scalar_tensor_tensor(out=T.t_nbe[:, 0:NCH], in0=T.t_eG[:, 0:NCH], scalar=-1.0, in1=bcol,
                                                      op0=ALU.mult, op1=ALU.mult)), reads=['t_eG', 'g_beta'], writes=['t_nbe'])
        gt, gtk = T.t_X, 't_X'
        S.op('dve', (lambda e: e.tensor_tensor(out=gt[:, 0:NCH, :], in0=gcol.unsqueeze(2).to_broadcast([C64, NCH, C64]),
                                               in1=self.tri_f[0:C64, 0:C64].unsqueeze(1).to_broadcast([C64, NCH, C64]),
                                               op=ALU.mult)), reads=['g_g', 'const'], writes=[gtk])
        pR, pRk = self.ps('aux')
        S.op('pe', (lambda e: e.matmul(pR[:, 0:NW], lhsT=self.ones_f[0:C64, :], rhs=gt[:, 0:NCH, :].rearrange("p c x -> p (c x)"),
                                       start=True, stop=True)), reads=[gtk, 'const'], writes=[pRk])
        S.op('act', (lambda e: e.activation(out=T.t_eGrow[:, 0:NW], in_=pR[:, 0:NW], func=AF.Exp)), reads=[pRk],
             writes=['t_eGrow'])
        S.op('dve', (lambda e: e.tensor_tensor(out=qnb[:, 0:NW], in0=qn[:, 0:NW], in1=T.t_eGrow[:, 0:NW], op=ALU.mult)),
             reads=[kq, 't_eGrow'], writes=['qnb'])
        X, Xk = T.t_X, 't_X'
        S.op('dve', (lambda e: e.tensor_tensor(out=X[:, 0:NCH, :], in0=pR[0:C64, 0:NW].rearrange("p (c x) -> p c x", c=NCH),
                                               in1=Gc[:, 0:NCH].unsqueeze(2).to_broadcast([C64, NCH, C64]), op=ALU.subtract)),
             reads=[pRk, 't_G'], writes=[Xk])
        D, DT = T.t_D, T.t_DT
        S.op('dve', (lambda e: e.scalar_tensor_tensor(out=D[:, 0:NCH, :], in0=X[:, 0:NCH, :], scalar=-1.0,
                                                      in1=self.m_up.unsqueeze(1).to_broadcast([C64, NCH, C64]),
                                                      op0=ALU.mult, op1=ALU.subtract)), reads=[Xk, 'const'], writes=['t_D'])
        S.op('act', (lambda e: e.activation(out=D[:, 0:NCH, :], in_=D[:, 0:NCH, :], func=AF.Exp)), reads=['t_D'], writes=['t_D'])
        S.op('dve', (lambda e: e.tensor_tensor(out=DT[:, 0:NCH, :], in0=X[:, 0:NCH, :],
                                               in1=self.m_lo.unsqueeze(1).to_broadcast([C64, NCH, C64]), op=ALU.subtract)),
             reads=[Xk, 'const'], writes=['t_DT'])
        S.op('act', (lambda e: e.activation(out=DT[:, 0:NCH, :], in_=DT[:, 0:NCH, :], func=AF.Exp)), reads=['t_DT'],
             writes=['t_DT'])
        S.op('dve', (lambda e: e.tensor_tensor(out=D[:, 0:NCH, :], in0=D[:, 0:NCH, :],
                                               in1=self.m_strict.unsqueeze(1).to_broadcast([C64, NCH, C64]), op=ALU.mult)),
             reads=['t_D', 'const'], writes=['t_D'])
        S.op('dve', (lambda e: e.scalar_tensor_tensor(out=D[:, 0:NCH, :], in0=D[:, 0:NCH, :], scalar=-1.0,
                                                      in1=bcol.unsqueeze(2).to_broadcast([C64, NCH, C64]),
                                                      op0=ALU.mult, op1=ALU.mult)), reads=['t_D', 'g_beta'], writes=['t_D'])
        pA, pAk = self.ps('aux')
        pQ, pQk = self.ps('aux')

        def fa(e):
            ins = None
            for ci in range(NCH):
                sl = slice(ci * C64, (ci + 1) * C64)
                ins = e.matmul(pA[0:C64, sl], lhsT=knb[:, sl], rhs=knb[:, sl], start=True, stop=True)
            return ins
        S.op('pe', fa, reads=['knb'], writes=[pAk])
        S.op('act', (lambda e: e.activation(out=T.qrb[:, 0:NW], in_=qn[:, 0:NW], func=AF.Copy)), reads=[kq], writes=['qrb'])

        def fq(e):
            ins = None
            for ci in range(NCH):
                sl = slice(ci * C64, (ci + 1) * C64)
                ins = e.matmul(pQ[0:C64, sl], lhsT=knb[:, sl], rhs=T.qrb[:, sl], start=True, stop=True)
            return ins
        S.op('pe', fq, reads=['knb', 'qrb'], writes=[pQk])
        Nm, NmT = T.t_N, T.t_NT
        S.op('dve', (lambda e: e.tensor_tensor(out=Nm[0][:, 0:NCH, :], in0=pA[0:C64, 0:NW].rearrange("p (c x) -> p c x", c=NCH),
                                               in1=D[:, 0:NCH, :], op=ALU.mult)), reads=[pAk, 't_D'], writes=[('t_N', 0)])
        S.op('dve', (lambda e: e.tensor_tensor(out=T.t_attT[:, 0:NCH, :],
                                               in0=pQ[0:C64, 0:NW].rearrange("p (c x) -> p c x", c=NCH),
                                               in1=DT[:, 0:NCH, :], op=ALU.mult)), reads=[pQk, 't_DT'], writes=['t_attT'])
        pT_, pTk = self.ps('aux')

        def ft(e):
            ins = None
            for ci in range(NCH):
                sl = slice(ci * C64, (ci + 1) * C64)
                ins = e.transpose(pT_[0:C64, sl], Nm[0][:, ci, :], ident[0:C64, 0:C64])
            return ins
        S.op('pe', ft, reads=[('t_N', 0), 'const'], writes=[pTk])
        S.op('act', (lambda e: e.activation(out=NmT[0][:, 0:NCH, :], in_=pT_[0:C64, 0:NW].rearrange("p (c x) -> p c x", c=NCH),
                                            func=AF.Copy)), reads=[pTk], writes=[('t_NT', 0)])
        PT = T.t_PT
        S.op('dve', (lambda e: e.tensor_tensor(out=PT[0][:, 0:NCH, :], in0=NmT[0][:, 0:NCH, :],
                                               in1=ident[0:C64, 0:C64].unsqueeze(1).to_broadcast([C64, NCH, C64]), op=ALU.add)),
             reads=[('t_NT', 0), 'const'], writes=[('t_PT', 0)])
        cur = 0
        for lvl in range(1, 6):
            nxt = 1 - cur
            pM, pMk = self.ps('aux')

            def fm(e, cur=cur, pM=pM):
                ins = None
                for ci in range(NCH):
                    sl = slice(ci * C64, (ci + 1) * C64)
                    ins = e.matmul(pM[0:C64, sl], lhsT=NmT[cur][:, ci, :], rhs=Nm[cur][:, ci, :], start=True, stop=True)
                return ins
            S.op('pe', fm, reads=[('t_N', cur), ('t_NT', cur)], writes=[pMk])
            S.op('act', (lambda e, nxt=nxt, pM=pM: e.activation(out=Nm[nxt][:, 0:NCH, :],
                                                               in_=pM[0:C64, 0:NW].rearrange("p (c x) -> p c x", c=NCH),
                                                               func=AF.Copy)), reads=[pMk], writes=[('t_N', nxt)])
            if lvl < 5:
                pMT, pMTk = self.ps('aux')

                def fmt(e, cur=cur, pMT=pMT):
                    ins = None
                    for ci in range(NCH):
                        sl = slice(ci * C64, (ci + 1) * C64)
                        ins = e.matmul(pMT[0:C64, sl], lhsT=Nm[cur][:, ci, :], rhs=NmT[cur][:, ci, :], start=True, stop=True)
                    return ins
                S.op('pe', fmt, reads=[('t_N', cur), ('t_NT', cur)], writes=[pMTk])
                S.op('dve', (lambda e, nxt=nxt, pMT=pMT: e.tensor_copy(out=NmT[nxt][:, 0:NCH, :],
                                                                      in_=pMT[0:C64, 0:NW].rearrange("p (c x) -> p c x", c=NCH))),
                     reads=[pMTk], writes=[('t_NT', nxt)])
            pP, pPk = self.ps('aux')

            def fp(e, nxt=nxt, cur=cur, pP=pP):
                ins = None
                for ci in range(NCH):
                    sl = slice(ci * C64, (ci + 1) * C64)
                    ins = e.matmul(pP[0:C64, sl], lhsT=Nm[nxt][:, ci, :], rhs=PT[0][:, ci, :], start=True, stop=True)
                return ins
            S.op('pe', fp, reads=[('t_N', nxt), ('t_PT', 0)], writes=[pPk])
            S.op('dve', (lambda e, nxt=nxt, cur=cur, pP=pP: e.tensor_tensor(
                out=PT[0][:, 0:NCH, :], in0=pP[0:C64, 0:NW].rearrange("p (c x) -> p c x", c=NCH), in1=PT[0][:, 0:NCH, :],
                op=ALU.add)), reads=[pPk, ('t_PT', 0)], writes=[('t_PT', 0)])
            cur = nxt
        PTf, PTk = PT[0], ('t_PT', 0)
        for ci in range(NCH):
            sl = slice(ci * C64, (ci + 1) * C64)
            pk, pkk = self.ps('aux')
            S.op('pe', (lambda e, pk=pk, sl=sl: e.transpose(pk[0:C64, 0:128], kn[:, sl], ident[:, :])), reads=[kk, 'const'],
                 writes=[pkk])
            S.op('pe', (lambda e, pk=pk, sl=sl: e.transpose(pk[0:C64, 128:256], vv[:, sl], ident[:, :])),
                 reads=[kv_, 'const', pkk], writes=[pkk])
            S.op('dve', (lambda e, pk=pk, ci=ci: e.tensor_scalar(out=T.t_kd[:, ci, :], in0=pk[0:C64, 0:128],
                                                               scalar1=T.t_kdsc[:, ci:ci + 1], scalar2=None, op0=ALU.mult)),
                 reads=[pkk, 't_kdsc'], writes=[('t_kd', ci)])
            S.op('dve', (lambda e, pk=pk, ci=ci: e.tensor_scalar(out=T.t_bv[:, ci, :], in0=pk[0:C64, 128:256],
                                                               scalar1=self.g_beta[0:C64, ci, hd:hd + 1], scalar2=None,
                                                               op0=ALU.mult)),
                 reads=[pkk, 'g_beta'], writes=[('t_bv', ci)])
        Sf = self.Sst[:, hd, :]
        Sk = ('S', hd)
        Sb = T.Sbf
        S.op('act', (lambda e: e.activation(out=Sb[:, :], in_=Sf, func=AF.Copy)), reads=[Sk], writes=['Sbf'])
        for ci in range(NCH):
            sl = slice(ci * C64, (ci + 1) * C64)
            p1, p1k = self.ps('aux')
            S.op('pe', (lambda e, p1=p1, sl=sl: e.matmul(p1[0:C64, 0:128], lhsT=knb[:, sl], rhs=Sb[:, :], start=True, stop=True)),
                 reads=['knb', 'Sbf'], writes=[p1k])
            S.op('dve', (lambda e, p1=p1, ci=ci: e.scalar_tensor_tensor(out=T.t_R[:, :], in0=p1[0:C64, 0:128],
                                                                      scalar=T.t_nbe[:, ci:ci + 1], in1=T.t_bv[:, ci, :],
                                                                      op0=ALU.mult, op1=ALU.add)),
                 reads=[p1k, 't_nbe', ('t_bv', ci)], writes=['t_R'])
            S.op('pe', (lambda e, p1=p1, ci=ci: e.matmul(p1[0:C64, 128:256], lhsT=PTf[:, ci, :], rhs=T.t_R[:, :],
                                                       start=True, stop=True)), reads=[PTk, 't_R', p1k], writes=[p1k])
            S.op('act', (lambda e, p1=p1: e.activation(out=T.t_ub[:, :], in_=p1[0:C64, 128:256], func=AF.Copy)),
                 reads=[p1k], writes=['t_ub'])
            p2, p2k = self.ps('aux')

            def fo(e, p2=p2, sl=sl, ci=ci):
                e.matmul(p2[0:C64, 0:128], lhsT=qnb[:, sl], rhs=Sb[:, :], start=True, stop=False)
                return e.matmul(p2[0:C64, 0:128], lhsT=T.t_attT[:, ci, :], rhs=T.t_ub[:, :], start=False, stop=True)
            S.op('pe', fo, reads=['qnb', 'Sbf', 't_attT', 't_ub'], writes=[p2k])
            S.op('act', (lambda e, p2=p2, ci=ci: e.activation(out=T.t_o[:, ci, :], in_=p2[0:C64, 0:128], func=AF.Copy)),
                 reads=[p2k], writes=[('t_o', ci)])
            p3, p3k = self.ps('aux')
            S.op('pe', (lambda e, p3=p3, ci=ci: e.matmul(p3[:, 0:128], lhsT=T.t_kd[:, ci, :], rhs=T.t_ub[:, :],
                                                       start=True, stop=True)), reads=[('t_kd', ci), 't_ub'], writes=[p3k])
            S.op('dve', (lambda e, p3=p3, ci=ci: e.scalar_tensor_tensor(out=Sf, in0=Sf, scalar=T.t_alast[:, ci:ci + 1],
                                                                      in1=p3[:, 0:128], op0=ALU.mult, op1=ALU.add)),
                 reads=[p3k, 't_alast', Sk], writes=[Sk])
            if ci < NCH - 1:
                S.op('act', (lambda e: e.activation(out=Sb[:, :], in_=Sf, func=AF.Copy)), reads=[Sk], writes=['Sbf'])
        sq = T.t_bv
        sqkeys = [('t_bv', ci) for ci in range(NCH)]
        S.op('dve', (lambda e: e.tensor_tensor(out=sq[:, 0:NCH, :], in0=T.t_o[:, 0:NCH, :], in1=T.t_o[:, 0:NCH, :],
                                               op=ALU.mult)), reads=[('t_o', ci) for ci in range(NCH)], writes=sqkeys)
        S.op('dve', (lambda e: e.tensor_reduce(out=T.t_oss[:, 0:NCH], in_=sq[:, 0:NCH, :], axis=AX.X, op=ALU.add)),
             reads=sqkeys, writes=['t_oss'])
        S.op('dve', (lambda e: e.tensor_scalar(out=T.t_oss[:, 0:NCH], in0=T.t_oss[:, 0:NCH], scalar1=1.0 / 128,
                                               scalar2=RMS_EPS, op0=ALU.mult, op1=ALU.add)), reads=['t_oss'], writes=['t_oss'])
        self.rsqrt_(T.t_oss[:, 0:NCH], 't_oss')
        S.op('dve', (lambda e: e.tensor_tensor(out=sq[:, 0:NCH, :], in0=T.t_o[:, 0:NCH, :],
                                               in1=T.t_oss[:, 0:NCH].unsqueeze(2).to_broadcast([C64, NCH, 128]), op=ALU.mult)),
             reads=[('t_o', ci) for ci in range(NCH)] + ['t_oss'] + sqkeys, writes=sqkeys)
        pO, pOk = self.ps('aux')

        def fto(e):
            ins = None
            for ci in range(NCH):
                ins = e.transpose(pO[:, ci * C64:(ci + 1) * C64], sq[:, ci, :], ident[0:C64, 0:C64])
            return ins
        S.op('pe', fto, reads=sqkeys + ['const'], writes=[pOk])
        S.op('dve', (lambda e: e.scalar_tensor_tensor(out=self.scr[:, hd, 0:NW], in0=pO[:, 0:NW], scalar=self.gon[:, 0:1],
                                                      in1=self.hz[hh][:, 0:NW], op0=ALU.mult, op1=ALU.mult)),
             reads=[pOk, ('hz', hh), 'const'], writes=[('scr', hd)])

    def gdn_sample_states(self, W):
        c = self.c
        S = self.S
        H, NS = c.H, c.NS
        ident = self.ident_f
        S.op('act', (lambda e: e.activation(out=self.s_a[0:NS, :], in_=self.g_g[0:NS, 0, :], func=AF.Exp)),
             reads=['g_g'], writes=['s_a'])
        for b in range(NS):
            Sb_, Sbk = self.s_S, 's_S'
            S.dma('pool', 'sin', (lambda e, Sb_=Sb_, b=b: e.dma_start(out=Sb_[:, :, :], in_=self.d_gs_in[b])), writes=[Sbk])
            S.op('dve', (lambda e, b=b: e.tensor_scalar(out=self.s_bd[0:NS, 0, :], in0=self.s_a[0:NS, :],
                                                       scalar1=ident[0:NS, b:b + 1], scalar2=None, op0=ALU.mult)),
                 reads=['s_a', 'const'], writes=['s_bd'])
            S.op('dve', (lambda e, b=b: e.tensor_scalar(out=self.s_bd[0:NS, 1, :], in0=self.g_beta[0:NS, 0, :],
                                                       scalar1=ident[0:NS, b:b + 1], scalar2=None, op0=ALU.mult)),
                 reads=['g_beta', 'const', 's_bd'], writes=['s_bd'])
            pb, pbk = self.ps('aux')
            S.op('pe', (lambda e, pb=pb: e.matmul(pb[:, 0:2 * H], lhsT=self.ones_f[0:NS, :],
                                                  rhs=self.s_bd[0:NS, :, :].rearrange("p a h -> p (a h)"),
                                                  start=True, stop=True)), reads=['s_bd', 'const'], writes=[pbk])
            S.op('act', (lambda e, pb=pb: e.activation(out=self.s_ab[:, :, :],
                                                       in_=pb[:, 0:2 * H].rearrange("p (a h) -> p a h", a=2), func=AF.Copy)),
                 reads=[pbk], writes=['s_ab'])
            pk, pkk = self.ps('aux')

            def fks(e, Sb_=Sb_, pk=pk, b=b):
                ins = None
                for hd in range(H):
                    ins = e.matmul(pk[:, hd:hd + 1], lhsT=Sb_[:, hd, :], rhs=self.s_k[:, hd, b:b + 1], start=True, stop=True)
                return ins
            S.op('pe', fks, reads=[Sbk, 's_k'], writes=[pkk])
            r, rk = self.s_r, 's_r'
            S.op('dve', (lambda e, pk=pk, b=b: e.tensor_tensor(out=r[:, :], in0=pk[:, 0:H], in1=self.s_ab[:, 0, :], op=ALU.mult)),
                 reads=[pkk, 's_ab'], writes=[rk])
            S.op('dve', (lambda e, b=b: e.tensor_tensor(out=r[:, :], in0=self.s_v[:, :, b], in1=r[:, :], op=ALU.subtract)),
                 reads=[rk, 's_v'], writes=[rk])
            S.op('dve', (lambda e, b=b: e.tensor_tensor(out=r[:, :], in0=r[:, :], in1=self.s_ab[:, 1, :], op=ALU.mult)),
                 reads=[rk, 's_ab'], writes=[rk])
            pt, ptk = self.ps('aux')
            S.op('pe', (lambda e, pt=pt, b=b: e.transpose(pt[0:H, 0:128], self.s_k[:, :, b], ident[:, :])), reads=['s_k', 'const'],
                 writes=[ptk])
            S.op('pe', (lambda e, pt=pt: e.transpose(pt[0:H, 128:256], r[:, :], ident[:, :])), reads=[rk, 'const', ptk],
                 writes=[ptk])
            S.op('act', (lambda e, pt=pt: e.activation(out=self.s_rows[0:H, :], in_=pt[0:H, 0:256], func=AF.Copy)), reads=[ptk],
                 writes=['s_rows'])
            for g0 in range(0, H, 2):
                ng = min(2, H - g0)
                po, pok = self.ps('aux')
                rb, rbk = self.s_rbd[(g0 // 2) % 2], ('s_rbd', (g0 // 2) % 2)
                S.op('dve', (lambda e, rb=rb, g0=g0, ng=ng: e.tensor_tensor(
                    out=rb[0:H, 0:ng, :], in0=self.s_rows[0:H, 128:256].unsqueeze(1).to_broadcast([H, ng, 128]),
                    in1=ident[0:H, g0:g0 + ng].unsqueeze(2).to_broadcast([H, ng, 128]), op=ALU.mult)),
                     reads=['s_rows', 'const'], writes=[rbk])
                S.op('pe', (lambda e, po=po, rb=rb, ng=ng: e.matmul(
                    po[:, 0:ng * 128], lhsT=self.s_rows[0:H, 0:128],
                    rhs=rb[0:H, 0:ng, :].rearrange("p h d -> p (h d)"), start=True, stop=True)),
                     reads=['s_rows', rbk], writes=[pok])
                S.op('dve', (lambda e, Sb_=Sb_, b=b, g0=g0, ng=ng: e.tensor_tensor(
                    out=Sb_[:, g0:g0 + ng, :], in0=Sb_[:, g0:g0 + ng, :],
                    in1=self.s_ab[:, 0, g0:g0 + ng].unsqueeze(2).to_broadcast([128, ng, 128]), op=ALU.mult)),
                     reads=[Sbk, 's_ab'], writes=[Sbk])
                S.op('dve', (lambda e, Sb_=Sb_, po=po, g0=g0, ng=ng: e.tensor_tensor(
                    out=Sb_[:, g0:g0 + ng, :], in0=Sb_[:, g0:g0 + ng, :],
                    in1=po[:, 0:ng * 128].rearrange("p (h d) -> p h d", h=ng), op=ALU.add)),
                     reads=[Sbk, pok], writes=[Sbk])
            pq, pqk = self.ps('aux')

            def foq(e, Sb_=Sb_, pq=pq, b=b):
                ins = None
                for hd in range(H):
                    ins = e.matmul(pq[:, hd:hd + 1], lhsT=Sb_[:, hd, :], rhs=self.s_q[:, hd, b:b + 1], start=True, stop=True)
                return ins
            S.op('pe', foq, reads=[Sbk, 's_q'], writes=[pqk])
            S.op('act', (lambda e, pq=pq, b=b: e.activation(out=self.s_o[:, :, b], in_=pq[:, 0:H], func=AF.Copy)), reads=[pqk],
                 writes=['s_o'])
            S.dma('pool', 'sout', (lambda e, Sb_=Sb_, b=b: e.dma_start(out=self.d_gdn_s[b], in_=Sb_[:, :, :])),
                  reads=[Sbk], writes=[('d_gdn_s', b)])
        HB = H * NS
        of = self.s_o[:, :, :].rearrange("p h b -> p (h b)")
        hstep = max(1, 256 // NS)
        for h0 in range(0, H, hstep):
            nh = min(hstep, H - h0)
            o0, n = h0 * NS, nh * NS
            rn, rnk = self.tmp('rn')
            self.colsum_sq(lambda k, o0=o0, n=n: (of[:, o0:o0 + n], 's_o'), 1, n, 1.0 / 128, RMS_EPS, rn, rnk)
            t, tk = self.tmp('on')
            S.op('dve', (lambda e, t=t, rn=rn, o0=o0, n=n: e.scalar_tensor_tensor(out=t[:, 0:n], in0=of[:, o0:o0 + n],
                                                                               scalar=self.gon[:, 0:1], in1=rn[:, 0:n],
                                                                               op0=ALU.mult, op1=ALU.mult)),
                 reads=['s_o', rnk, 'const'], writes=[tk])
            S.op('dve', (lambda e, t=t, n=n, h0=h0, nh=nh: e.tensor_tensor(
                out=self.scr[:, h0:h0 + nh, 0:NS], in0=t[:, 0:n].rearrange("p (h b) -> p h b", h=nh),
                in1=self.s_z[:, h0:h0 + nh, :], op=ALU.mult)),
                 reads=[tk, 's_z'], writes=[('scr', hx) for hx in range(h0, h0 + nh)])

    def gdn(self, W, sample, t0):
        c = self.c
        S = self.S
        H, NS = c.H, c.NS
        h = self.h
        self.gdn_gates(W, 1 if sample else W // C64)
        for j in range((H + 1) // 2):
            self.gdn_proj_head(2 * j, W, None, sample)
            if S.dry:
                continue
            if not sample:
                caps = []
                for hh in range(min(2, H - 2 * j)):
                    S.cap = []
                    S.ns = hh
                    self.aux_set = (2 * hh, 2 * hh + 1)
                    self.gdn_prompt_head(2 * j + hh, hh, W)
                    caps.append(S.cap)
                    S.cap = None
                    S.ns = None
                    self.aux_set = None
                S.replay_zipped(caps)
                continue
            for hh in range(min(2, H - 2 * j)):
                hd = 2 * j + hh
                if True:
                    for nm, src, dst in (('q', self.hq, self.s_q), ('k', self.hk, self.s_k), ('v', self.hv, self.s_v),
                                         ('z', self.hz, self.s_z)):
                        S.op('act', (lambda e, src=src, dst=dst, hh=hh, hd=hd: e.activation(out=dst[:, hd, :], in_=src[hh][:, 0:NS],
                                                                                            func=AF.Copy)),
                             reads=[('h' + nm, hh)], writes=['s_' + nm])
        if sample and not S.dry:
            self.gdn_sample_states(W)

        def cb(mi, psap, pskey):
            S.op('dve', (lambda e, mi=mi, psap=psap, W=W: e.tensor_tensor(out=h[:, mi, 0:W], in0=h[:, mi, 0:W], in1=psap[:, 0:W],
                                                                        op=ALU.add)),
                 reads=[pskey, ('h', mi)], writes=[('h', mi)])
        self.linear(self.w['gdn_w_out'][0], c.KC, 0, c.D, self.scr, lambda k: ('scr', k), W, cb, 'gout')

    def tile(self, ti, sample):
        c = self.c
        S = self.S
        W = c.NS if sample else c.TT
        t0 = 0 if sample else ti * c.TT
        h = self.h
        if not S.dry:
            src = self.d_xsT if sample else self.d_xpT[:, :, t0:t0 + W]
            S.dma('pool', 'xin', (lambda e, src=src, W=W: e.dma_start(out=h[:, :, 0:W], in_=src)),
                  writes=[('h', k) for k in range(c.KC)])
        for layer in range(2):
            self.rmsnorm(self.gmix, layer, W)
            if layer == 0:
                self.conformer(W, sample, t0)
            else:
                self.gdn(W, sample, t0)
            self.rmsnorm(self.gffn, layer, W)
            self.ffn(layer, W)
            self.rmsnorm(self.gple, layer, W)
            self.ple(layer, W, sample, t0)
        if S.dry:
            return
        self.colsum_sq(lambda k: (h[:, k, 0:W], ('h', k)), c.KC, W, 1.0 / c.D, RMS_EPS, self.rstd, 'rstd')
        for k in range(c.KC):
            t, tk = self.tmp('y')
            S.op('dve', (lambda e, t=t, k=k, W=W: e.scalar_tensor_tensor(out=t[:, 0:W], in0=h[:, k, 0:W],
                                                                       scalar=self.gfin[:, 0, k:k + 1], in1=self.rstd[:, 0:W],
                                                                       op0=ALU.mult, op1=ALU.mult)),
                 reads=[('h', k), 'rstd', 'const'], writes=[tk])
            dst = self.d_ysT[:, k, :] if sample else self.d_ypT[:, k, t0:t0 + W]
            S.dma('pool', f'yout{tk[1]}', (lambda e, t=t, dst=dst, W=W: e.dma_start(out=dst, in_=t[:, 0:W])),
                  reads=[tk], writes=[('d_y', sample, ti, k)])

    def build(self):
        c = self.c
        nc = self.nc
        D, F, H, KC, QC, NS, TT, SEQ = c.D, c.F, c.H, c.KC, c.QC, c.NS, c.TT, c.SEQ
        self.d_xpT = self.din("xpT", [128, KC, SEQ])
        self.d_xsT = self.din("xsT", [128, KC, NS])
        self.d_ppT = [self.din(f"ppT{l}", [128, c.PC, SEQ]) for l in range(2)]
        self.d_psT = [self.din(f"psT{l}", [128, c.PC, NS]) for l in range(2)]
        self.d_cs_in = self.din("cs_in", [128, KC, NS, CW - 1])
        self.d_qs_in = self.din("qs_in", [128, QC, NS, SC - 1])
        self.d_gs_in = self.din("gs_in", [NS, 128, H, 128])
        d_vec = {n: self.din(n, s) for n, s in dict(
            gmix=[128, 2, KC], gffn=[128, 2, KC], gple=[128, 2, KC], gfin=[128, 1, KC],
            cwdw=[128, KC, CW], cbdw=[128, KC], clng=[128, KC], clnb=[128, KC], gwc=[128, QC, SC],
            alog=[C64, H], dtb=[C64, H], gon=[128, 1],
            c_ident=[128, 128], c_tri=[C64, C64], c_mup=[C64, C64], c_mlo=[C64, C64], c_mstrict=[C64, C64]).items()}
        self.w = {}
        for n, s in dict(conf_w_pw1=[1, D, 2 * D], conf_w_pw2=[1, D, D], gdn_w_in=[1, D, c.PROJ], gdn_w_out=[1, D, D],
                         ffn_w_gate=[2, D, F], ffn_w_up=[2, D, F], ffn_w_down=[2, F, D], ple_w_gate=[2, D, D],
                         ple_w_proj=[2, c.PLE, D]).items():
            ap = self.din(n, s)
            self.w[n] = [ap[l] for l in range(s[0])]
        self.d_ypT = self.dout("ypT", [128, KC, SEQ])
        self.d_ysT = self.dout("ysT", [128, KC, NS])
        self.d_conf_p = self.dout("conf_p", [128, KC, CW - 1])
        self.d_qkv_p = self.dout("qkv_p", [128, QC, SC - 1])
        self.d_gdn_p = self.dout("gdn_p", [128, H, 128])
        self.d_conf_s = self.dout("conf_s", [128, KC, NS, CW - 1])
        self.d_qkv_s = self.dout("qkv_s", [128, QC, NS, SC - 1])
        self.d_gdn_s = self.dout("gdn_s", [NS, 128, H, 128])

        NCH = c.NCH
        with contextlib.ExitStack() as st:
            self.st = st
            S = self.S = Sched(nc, st)
            self.h = self.sb("h", [128, KC, TT])
            self.u = self.sb("u", [128, KC, TT], BF16)
            self.wb = [self.sb(f"wb{i}", [128, 32, 256], BF16) for i in range(c.NWB)]
            self.psb = [st.enter_context(nc.psum_tensor(f"psb{i}", [128, 512], F32)) for i in range(8)]
            self.tmps = [self.sb(f"tmp{i}", [128, 256]) for i in range(3)]
            self.c_a = [self.sb(f"c_a{i}", [128, 256]) for i in range(2)]
            self.c_b = [self.sb(f"c_b{i}", [128, 256]) for i in range(2)]
            self.rstd = self.sb("rstd", [128, TT])
            self.mean = self.sb("mean", [128, TT])
            self.pT = self.sb("pT", [128, c.PC, TT], BF16)
            self.scr = self.sb("scr", [128, max(c.FGMAX, KC, H), TT], BF16)
            self.wba = self.sb("wba", [128, KC, 2 * H], BF16)
            self.g_beta = self.sb("g_beta", [C64, NCH, H])
            self.g_g = self.sb("g_g", [C64, NCH, H])
            self.hq = [self.sb(f"hq{i}", [128, TT]) for i in range(2)]
            self.hk = [self.sb(f"hk{i}", [128, TT]) for i in range(2)]
            self.hv = [self.sb(f"hv{i}", [128, TT]) for i in range(2)]
            self.hz = [self.sb(f"hz{i}", [128, TT]) for i in range(2)]
            self.ones_f = self.sb("ones_f", [128, 128])
            self.ident_f = self.sb("ident_f", [128, 128])
            self.tri_f = self.sb("tri_f", [C64, C64])
            self.m_up_t = self.sb("m_up", [C64, C64])
            self.m_lo_t = self.sb("m_lo", [C64, C64])
            self.m_strict_t = self.sb("m_strict", [C64, C64])
            self.m_up, self.m_lo, self.m_strict = self.m_up_t[:, :], self.m_lo_t[:, :], self.m_strict_t[:, :]
            self.gmix = self.sb("gmix", [128, 2, KC])
            self.gffn = self.sb("gffn", [128, 2, KC])
            self.gple = self.sb("gple", [128, 2, KC])
            self.gfin = self.sb("gfin", [128, 1, KC])
            self.cwdw = self.sb("cwdw", [128, KC, CW])
            self.cbdw = self.sb("cbdw", [128, KC])
            self.clng = self.sb("clng", [128, KC])
            self.clnb = self.sb("clnb", [128, KC])
            self.gwc = self.sb("gwc", [128, QC, SC])
            self.nA = self.sb("nA", [C64, H])
            self.dtb = self.sb("dtb", [C64, H])
            self.gon = self.sb("gon", [128, 1])
            self.ps_lin_i = self.ps_aux_i = self.tmp_i = 0

            with contextlib.ExitStack() as pst:
                self.st = pst
                self.scr_glu = self.sb("glub", [128, 4, CW - 1 + TT])
                self.chalo = self.sb("chalo", [128, KC, CW - 1])
                self.qhalo = self.sb("qhalo", [128, QC, SC - 1])
                self.Sst = self.sb("Sst", [128, H, 128])
                self.cvb = [[self.sb(f"cvb{a}{b}", [128, SC - 1 + TT]) for b in range(2)] for a in range(2)]
                class _NS:
                    pass
                self.Tb = []
                for p_ in range(2):
                    T = _NS()
                    sfx = f"_{p_}"
                    T.knb = self.sb("knb" + sfx, [128, TT], BF16)
                    T.qnb = self.sb("qnb" + sfx, [128, TT], BF16)
                    T.qrb = self.sb("qrb" + sfx, [128, TT], BF16)
                    T.t_G = self.sb("t_G" + sfx, [C64, NCH])
                    T.t_alast = self.sb("t_alast" + sfx, [128, NCH])
                    T.t_kdsc = self.sb("t_kdsc" + sfx, [C64, NCH])
                    T.t_eG = self.sb("t_eG" + sfx, [C64, NCH])
                    T.t_nbe = self.sb("t_nbe" + sfx, [C64, NCH])
                    T.t_eGrow = self.sb("t_eGrow" + sfx, [128, TT])
                    T.t_X = self.sb("t_X" + sfx, [C64, NCH, C64])
                    T.t_D = self.sb("t_D" + sfx, [C64, NCH, C64])
                    T.t_DT = self.sb("t_DT" + sfx, [C64, NCH, C64])
                    T.t_attT = self.sb("t_attT" + sfx, [C64, NCH, C64], BF16)
                    T.t_N = [self.sb(f"t_N{i}" + sfx, [C64, NCH, C64]) for i in range(2)]
                    T.t_NT = [self.sb(f"t_NT{i}" + sfx, [C64, NCH, C64]) for i in range(2)]
                    T.t_PT = [self.sb(f"t_PT{i}" + sfx, [C64, NCH, C64]) for i in range(1)]
                    T.t_kd = self.sb("t_kd" + sfx, [C64, NCH, 128], BF16)
                    T.t_bv = self.sb("t_bv" + sfx, [C64, NCH, 128])
                    T.t_R = self.sb("t_R" + sfx, [C64, 128])
                    T.t_ub = self.sb("t_ub" + sfx, [C64, 128], BF16)
                    T.t_o = self.sb("t_o" + sfx, [C64, NCH, 128])
                    T.t_oss = self.sb("t_oss" + sfx, [C64, NCH])
                    T.Sbf = self.sb("Sbf" + sfx, [128, 128], BF16)
                    self.Tb.append(T)

                self.plan = []
                S.dry = True
                self.tile(0, False)
                S.dry = False
                NP = len(self.plan)
                ntiles = c.NT + (1 if NS > 0 else 0)
                self.wtotal = NP * ntiles
                self.wpos = 0
                self.wissued = 0
                self.wcache = []
                for i0 in range(0, NP, 64):
                    n = min(64, NP - i0)
                    wc = nc.dram_tensor(f"wcache{i0 // 64}", [n, 128, 32 * 256], BF16, kind="Internal").ap()
                    self.wcache.extend(wc[i] for i in range(n))

                S.op('dve', lambda e: e.memset(self.ones_f[:, :], 1.0), writes=['const'])
                lst = [(self.ident_f, 'c_ident'), (self.tri_f, 'c_tri'), (self.m_up_t, 'c_mup'),
                       (self.m_lo_t, 'c_mlo'), (self.m_strict_t, 'c_mstrict'), (self.gmix, 'gmix'),
                       (self.gffn, 'gffn'), (self.gple, 'gple'), (self.gfin, 'gfin'), (self.cwdw, 'cwdw'),
                       (self.cbdw, 'cbdw'), (self.clng, 'clng'), (self.clnb, 'clnb'), (self.gwc, 'gwc'),
                       (self.nA, 'alog'), (self.dtb, 'dtb'), (self.gon, 'gon')]
                for i, (t, n) in enumerate(lst):
                    S.dma('sp', f'cst{i % 4}', (lambda e, t=t, n=n: e.dma_start(out=t[:], in_=d_vec[n])), writes=[('cst', i)])
                S.op('act', lambda e: e.activation(out=self.nA[:, :], in_=self.nA[:, :], func=AF.Exp),
                     reads=[('cst', i) for i in range(len(lst))], writes=['const'])
                S.op('dve', lambda e: e.tensor_scalar(out=self.nA[:, :], in0=self.nA[:, :], scalar1=-1.0, scalar2=None,
                                                      op0=ALU.mult), reads=['const'], writes=['const'])
                S.op('dve', lambda e: e.memset(self.chalo[:, :, :], 0.0), writes=[('chalo', k) for k in range(KC)])
                S.op('dve', lambda e: e.memset(self.qhalo[:, :, :], 0.0), writes=[('qhalo', k) for k in range(QC)])
                S.op('dve', lambda e: e.memset(self.Sst[:, :, :], 0.0), writes=[('S', k) for k in range(H)])

                for ti in range(c.NT):
                    self.tile(ti, False)
                S.dma('pool', 'po0', (lambda e: e.dma_start(out=self.d_conf_p, in_=self.chalo[:, :, :])),
                      reads=[('chalo', k) for k in range(KC)], writes=['d_conf_p'])
                S.dma('pool', 'po1', (lambda e: e.dma_start(out=self.d_qkv_p, in_=self.qhalo[:, :, :])),
                      reads=[('qhalo', k) for k in range(QC)], writes=['d_qkv_p'])
                S.dma('pool', 'po2', (lambda e: e.dma_start(out=self.d_gdn_p, in_=self.Sst[:, :, :])),
                      reads=[('S', k) for k in range(H)], writes=['d_gdn_p'])
                for e_ in Sched.ENG:
                    S.wait_all(e_)
                S.emit()

            if NS > 0:
                with contextlib.ExitStack() as sst:
                    self.st = sst
                    self.cs_st = [self.sb(f"cs_st{i}", [128, NS, CW - 1]) for i in range(1)]
                    self.cs_new = [self.sb(f"cs_new{i}", [128, NS, CW - 1]) for i in range(1)]
                    self.cs_prod = [self.sb(f"cs_prod{i}", [128, NS, CW - 1]) for i in range(1)]
                    self.c_c = [self.sb(f"c_c{i}", [128, NS]) for i in range(1)]
                    self.qs_st = [self.sb(f"qs_st{i}", [128, 3, 2, NS, SC - 1]) for i in range(1)]
                    self.qs_new = [self.sb(f"qs_new{i}", [128, 3, 2, NS, SC - 1]) for i in range(1)]
                    self.qs_prod = self.sb("qs_prod", [128, NS, SC - 1])
                    self.s_a = self.sb("s_a", [NS, H])
                    self.s_bd = self.sb("s_bd", [NS, 2, H])
                    self.s_ab = self.sb("s_ab", [128, 2, H])
                    self.s_S = self.sb("s_S", [128, H, 128])
                    self.s_q = self.sb("s_q", [128, H, NS])
                    self.s_k = self.sb("s_k", [128, H, NS])
                    self.s_v = self.sb("s_v", [128, H, NS])
                    self.s_z = self.sb("s_z", [128, H, NS])
                    self.s_o = self.sb("s_o", [128, H, NS])
                    self.s_r = self.sb("s_r", [128, H])
                    self.s_rows = self.sb("s_rows", [H, 256])
                    self.s_rbd = [self.sb(f"s_rbd{i}", [H, 2, 128]) for i in range(2)]
                    self.tile(0, True)
                    assert self.wpos == self.wtotal, (self.wpos, self.wtotal)
                    S.wait_all('sp')
                    S.emit()
        return nc


def _fm(x):
    T, C = x.shape
    return np.ascontiguousarray(x.reshape(T, C // 128, 128).transpose(2, 1, 0))


def _fm_inv(y):
    p, kc, T = y.shape
    return np.ascontiguousarray(y.transpose(2, 1, 0).reshape(T, kc * 128))


def _vec(v):
    lead = v.shape[:-1]
    C = v.shape[-1]
    r = v.reshape(lead + (C // 128, 128))
    return np.ascontiguousarray(np.moveaxis(r, -1, 0))


def make_consts():
    p = np.arange(C64)[:, None]
    x = np.arange(C64)[None, :]
    return dict(
        c_ident=np.eye(128, dtype=np.float32),
        c_tri=(p <= x).astype(np.float32),
        c_mup=np.where(x > p, NEG, 0.0).astype(np.float32),
        c_mlo=np.where(x < p, NEG, 0.0).astype(np.float32),
        c_mstrict=(x < p).astype(np.float32),
    )


def run(cfg, inp, nseq_cores=None):
    c = cfg
    NC = c.NCORES
    B = inp['x_prompt'].shape[0]
    NS = c.NS
    prog = Prog(c)
    nc = prog.build()
    f = lambda a: np.ascontiguousarray(np.asarray(a, dtype=np.float32))
    shared = dict(
        gmix=_vec(f(inp['g_mix'])), gffn=_vec(f(inp['g_ffn'])), gple=_vec(f(inp['g_ple'])),
        gfin=_vec(f(inp['g_final'])[None]),
        cwdw=np.ascontiguousarray(_vec(f(inp['conf_w_dw'][0])).transpose(0, 2, 1)),
        cbdw=_vec(f(inp['conf_b_dw'][0])), clng=_vec(f(inp['conf_ln_g'][0])), clnb=_vec(f(inp['conf_ln_b'][0])),
        gwc=np.ascontiguousarray(_vec(f(inp['gdn_w_conv'][0])).transpose(0, 2, 1)),
        alog=np.ascontiguousarray(np.broadcast_to(f(inp['gdn_a_log'][0])[None, :], (C64, c.H))),
        dtb=np.ascontiguousarray(np.broadcast_to(f(inp['gdn_dt_bias'][0])[None, :], (C64, c.H))),
        gon=np.ascontiguousarray(f(inp['gdn_g_onorm'][0])[:, None]),
    )
    shared.update(make_consts())
    for n in ('conf_w_pw1', 'conf_w_pw2', 'gdn_w_in', 'gdn_w_out', 'ffn_w_gate', 'ffn_w_up', 'ffn_w_down', 'ple_w_gate',
              'ple_w_proj'):
        shared[n] = f(inp[n])
    xp, xs = f(inp['x_prompt']), f(inp['x_sample'])
    pp, psm = f(inp['p_prompt']), f(inp['p_sample'])
    scc, scq, sg = f(inp['state_conv_conformer']), f(inp['state_conv_qkv']), f(inp['state_gdn'])
    in_maps = []
    ACT = c.ACTIVE
    assert len(ACT) >= B and NS * len(ACT) == xs.shape[0]
    zero_map = None
    for ci in range(NC):
        if ci not in ACT:
            if zero_map is None:
                zero_map = {k: np.zeros_like(v) for k, v in in_maps[0].items()}
            in_maps.append(zero_map)
            continue
        a = ACT.index(ci)
        sq = a % B
        sl = slice(a * NS, (a + 1) * NS)
        m = dict(shared)
        m['xpT'] = _fm(xp[sq])
        m['xsT'] = _fm(xs[sl, 0])
        for l in range(2):
            m[f'ppT{l}'] = _fm(pp[l, sq])
            m[f'psT{l}'] = _fm(psm[l, sl, 0])
        m['cs_in'] = np.ascontiguousarray(scc[0, sl].reshape(NS, CW - 1, c.KC, 128).transpose(3, 2, 0, 1))
        m['qs_in'] = np.ascontiguousarray(scq[0, sl].reshape(NS, SC - 1, c.QC, 128).transpose(3, 2, 0, 1))
        m['gs_in'] = np.ascontiguousarray(sg[0, sl].transpose(0, 2, 1, 3))
        in_maps.append(m)
    res = run_bass_kernel_spmd(nc, in_maps, core_ids=list(range(NC)))
    R = [res.results[ci] for ci in c.ACTIVE]
    NC = len(c.ACTIVE)
    D = c.D
    y_p = np.stack([_fm_inv(R[b]['ypT']) for b in range(B)])
    y_s = np.concatenate([_fm_inv(R[ci]['ysT']) for ci in range(NC)])[:, None, :]
    conf_p = np.stack([_fm_inv(R[b]['conf_p']) for b in range(B)])[None]
    qkv_p = np.stack([_fm_inv(R[b]['qkv_p']) for b in range(B)])[None]
    gdn_p = np.stack([np.ascontiguousarray(R[b]['gdn_p'].transpose(1, 0, 2)) for b in range(B)])[None]
    conf_s = np.concatenate([np.ascontiguousarray(R[ci]['conf_s'].transpose(2, 3, 1, 0)).reshape(NS, CW - 1, D)
                             for ci in range(NC)])[None]
    qkv_s = np.concatenate([np.ascontiguousarray(R[ci]['qkv_s'].transpose(2, 3, 1, 0)).reshape(NS, SC - 1, c.QKV)
                            for ci in range(NC)])[None]
    gdn_s = np.concatenate([np.ascontiguousarray(R[ci]['gdn_s'].transpose(0, 2, 1, 3)) for ci in range(NC)])[None]
    return (y_p, y_s, conf_p, qkv_p, gdn_p, conf_s, qkv_s, gdn_s)


def kernel(**inputs):
    cfg = Cfg(NS=32, ACTIVE=(0, 1, 4, 5))
    return run(cfg, inputs)
```

```python
import contextlib
import numpy as np
import concourse.bass as bass
import concourse.mybir as mybir
from concourse.bass_utils import run_bass_kernel_spmd

F32 = mybir.dt.float32
BF16 = mybir.dt.bfloat16
AF = mybir.ActivationFunctionType
ALU = mybir.AluOpType
AX = mybir.AxisListType

RMS_EPS = 1e-6
LN_EPS = 1e-5
L2_EPS = 1e-6
CW = 31
SC = 4
C64 = 64
NEG = 30000.0


class Cfg:
    def __init__(self, D=4096, F=11008, H=32, PLE=256, SEQ=2048, NS=16, TT=256, NCORES=8, NWB=3, ACTIVE=None):
        self.D, self.F, self.H, self.PLE, self.SEQ, self.NS, self.TT = D, F, H, PLE, SEQ, NS, TT
        self.NCORES, self.NWB = NCORES, NWB
        self.ACTIVE = list(ACTIVE) if ACTIVE is not None else list(range(NCORES))
        self.KC = D // 128
        self.FC = F // 128
        self.QC = 3 * H
        self.QKV = H * 384
        self.PROJ = self.QKV + H * 128 + 2 * H
        self.PC = PLE // 128
        self.NT = SEQ // TT
        self.NCH = TT // C64
        ng = -(-self.FC // 32)
        base = self.FC // ng
        self.FG = []
        s = 0
        for i in range(ng):
            n = base + (1 if i < self.FC - base * ng else 0)
            self.FG.append((s, n))
            s += n
        self.FGMAX = max(n for _, n in self.FG)


class Sched:
    ENG = ('pe', 'act', 'dve', 'pool', 'sp')

    def __init__(self, nc, stack):
        self.nc = nc
        self.stack = stack
        self.eng = dict(pe=nc.tensor, act=nc.scalar, dve=nc.vector, pool=nc.gpsimd, sp=nc.sync)
        self.prog = {e: [] for e in self.ENG}
        self.psem = {e: stack.enter_context(nc.semaphore(f"prog_{e}")) for e in self.ENG if e != 'sp'}
        self.pcnt = {e: 0 for e in self.ENG}
        self.seen = {e: {} for e in self.ENG}
        self.last_w = {}
        self.readers = {}
        self.dsem = {}
        self.dcnt = {}
        self.dry = False
        self.ns = None
        self.cap = None

    def _deps(self, e, reads, writes):
        best = {}
        for k in reads:
            t = self.last_w.get(k)
            if t is not None and best.get(t[0], 0) < t[1]:
                best[t[0]] = t[1]
        for k in writes:
            t = self.last_w.get(k)
            if t is not None and best.get(t[0], 0) < t[1]:
                best[t[0]] = t[1]
            for t in self.readers.get(k, ()):
                if best.get(t[0], 0) < t[1]:
                    best[t[0]] = t[1]
        seen = self.seen[e]
        for s, v in best.items():
            if seen.get(s, 0) < v:
                seen[s] = v
                self.prog[e].append(('w', s, v))

    def _commit(self, tok, reads, writes):
        for k in writes:
            self.last_w[k] = tok
            self.readers[k] = []
        for k in reads:
            if k in writes:
                continue
            self.readers.setdefault(k, []).append(tok)

    PRIV = ('knb', 'qnb', 'qrb', 'Sbf')

    def _nsk(self, k):
        if isinstance(k, str):
            return (k, 'ns', self.ns) if (k.startswith('t_') or k in self.PRIV) else k
        if isinstance(k, tuple) and isinstance(k[0], str) and k[0].startswith('t_'):
            return k + ('ns', self.ns)
        return k

    def op(self, e, fn, reads=(), writes=()):
        if self.dry:
            return
        if self.ns is not None:
            reads = [self._nsk(k) for k in reads]
            writes = [self._nsk(k) for k in writes]
        if self.cap is not None:
            self.cap.append(('op', e, fn, reads, writes))
            return
        px = [k for k in reads if isinstance(k, tuple) and k[0] == 'ps']
        if px:
            reads = [k for k in reads if not (isinstance(k, tuple) and k[0] == 'ps')]
            writes = list(writes) + [k for k in px if k not in writes]
        self._deps(e, reads, writes)
        self.pcnt[e] += 1
        tok = (('p', e), self.pcnt[e])
        self.prog[e].append(('o', fn, ('p', e)))
        self._commit(tok, reads, writes)

    def dma(self, e, sem_name, fn, reads=(), writes=()):
        if self.dry:
            return
        if self.cap is not None:
            self.cap.append(('dma', e, sem_name, fn, reads, writes))
            return
        if sem_name not in self.dsem:
            self.dsem[sem_name] = self.stack.enter_context(self.nc.semaphore(f"dma_{sem_name}"))
            self.dcnt[sem_name] = 0
        self._deps(e, reads, writes)
        self.dcnt[sem_name] += 16
        tok = (('d', sem_name), self.dcnt[sem_name])
        self.prog[e].append(('d', fn, ('d', sem_name)))
        self._commit(tok, reads, writes)

    def replay_zipped(self, caps):
        for i in range(max(len(x) for x in caps)):
            for x in caps:
                if i < len(x):
                    it = x[i]
                    if it[0] == 'op':
                        self.op(*it[1:])
                    else:
                        self.dma(*it[1:])

    def wait_all(self, e):
        allt = [(('p', x), self.pcnt[x]) for x in self.psem] + [(('d', n), c) for n, c in self.dcnt.items()]
        for s, c in allt:
            if c > 0 and self.seen[e].get(s, 0) < c:
                self.seen[e][s] = c
                self.prog[e].append(('w', s, c))

    def _sem(self, s):
        return self.psem[s[1]] if s[0] == 'p' else self.dsem[s[1]]

    def emit(self):
        nc = self.nc

        def run(e):
            eng = self.eng[e]
            for it in self.prog[e]:
                if it[0] == 'w':
                    eng.wait_ge(self._sem(it[1]), it[2])
                elif it[0] == 'o':
                    it[1](eng).then_inc(self._sem(it[2]), 1)
                else:
                    it[1](eng).then_inc(self._sem(it[2]), 16)

        with nc.Block() as block:
            @block.tensor
            def _(eng):
                run('pe')

            @block.scalar
            def _(eng):
                run('act')

            @block.vector
            def _(eng):
                run('dve')

            @block.gpsimd
            def _(eng):
                run('pool')

            @block.sync
            def _(eng):
                run('sp')
        self.prog = {e: [] for e in self.ENG}


class Prog:
    def __init__(self, cfg):
        self.c = cfg
        self.nc = bass.Bass("TRN2", target_bir_lowering=False)
        self.uid = 0
        self.aux_set = None

    def sb(self, name, shape, dt=F32):
        return self.st.enter_context(self.nc.sbuf_tensor("sb_" + name, list(shape), dt))

    def din(self, name, shape, dt=F32):
        return self.nc.dram_tensor(name, list(shape), dt, kind="ExternalInput").ap()

    def dout(self, name, shape, dt=F32):
        return self.nc.dram_tensor(name, list(shape), dt, kind="ExternalOutput").ap()

    def ps(self, kind):
        if kind == 'lin':
            r = self.ps_lin_i % 4
            self.ps_lin_i += 1
            return self.psb[r][:, 0:256], ('ps', r)
        if self.aux_set is not None:
            st_ = self.aux_set
            r = st_[self.ps_aux_i % len(st_)]
        else:
            r = self.ps_aux_i % 4
        self.ps_aux_i += 1
        return self.psb[4 + r][:, 0:256], ('ps', 4 + r)

    def tmp(self, kind):
        n = len(self.tmps)
        i = self.tmp_i % n
        self.tmp_i += 1
        return self.tmps[i], ('tmp', i)

    def wget(self, src, nk, ncols, tag, hold=0):
        c = self.c
        if self.S.dry:
            self.plan.append((src, nk, ncols, tag))
            return None, None
        i = self.wpos
        self.wpos += 1
        assert self.plan[i % len(self.plan)][3] == tag, (self.plan[i % len(self.plan)][3], tag)
        self._wissue(min(i - hold + c.NWB - 1, self.wtotal - 1))
        slot = i % c.NWB
        return self.wb[slot], ('wb', slot)

    def _wissue(self, upto):
        c = self.c
        S = self.S
        NP = len(self.plan)
        while self.wissued <= upto:
            g = self.wissued
            self.wissued += 1
            src, nk, ncols, tag = self.plan[g % NP]
            slot = g % c.NWB
            pid = g % NP
            dst = self.wb[slot][:, 0:nk, 0:ncols]
            cache = self.wcache[pid]
            cview = cache[:, 0:nk * ncols].rearrange("p (k n) -> p k n", k=nk)
            if g < NP:
                srcv = src.rearrange("(kc p) n -> p kc n", p=128)
                S.dma('pool', f'wl{slot}', (lambda e, dst=dst, srcv=srcv: e.dma_start(out=dst, in_=srcv)),
                      writes=[('wb', slot)])
                if self.wtotal > NP:
                    S.dma('sp', f'wc{slot}', (lambda e, dst=dst, cview=cview: e.dma_start(out=cview, in_=dst)),
                          reads=[('wb', slot)], writes=[('wcache', pid)])
            else:
                S.dma('sp', f'wh{slot}', (lambda e, dst=dst, cview=cview: e.dma_start(out=dst, in_=cview)),
                      reads=[('wcache', pid)], writes=[('wb', slot)])

    def mm_acc(self, psap, pskey, parts, W, extra_reads=()):
        S = self.S
        if S.dry:
            return
        seq = []
        reads = list(extra_reads)
        for (wt, wk, kcs, moff, msz, it, ikcs, ikeys) in parts:
            for a, b in zip(kcs, ikcs):
                seq.append((wt[:, a, moff:moff + msz], it[:, b, 0:W]))
            reads.append(wk)
            reads.extend(ikeys)
        n = len(seq)

        def fn(e, seq=seq, psap=psap, W=W, n=n):
            ins = None
            for i, (l, r) in enumerate(seq):
                ins = e.matmul(psap[0:l.shape[-1], 0:W], lhsT=l, rhs=r, start=(i == 0), stop=(i == n - 1))
            return ins
        S.op('pe', fn, reads=reads, writes=[pskey])

    def linear(self, Wsrc, K_chunks, col0, ncols_total, in_tile, in_key_fn, W, cb, tag, k0=0):
        npan = -(-ncols_total // 256)
        for pi in range(npan):
            c0 = col0 + pi * 256
            ncol = min(256, col0 + ncols_total - c0)
            subs = []
            kk = 0
            while kk < K_chunks:
                nk = min(32, K_chunks - kk)
                src = Wsrc[(k0 + kk) * 128:(k0 + kk + nk) * 128, c0:c0 + ncol]
                wt, wk = self.wget(src, nk, ncol, (tag, pi, kk), hold=len(subs))
                subs.append((wt, wk, kk, nk))
                kk += nk
            for mi in range(ncol // 128):
                if self.S.dry:
                    continue
                psap, pskey = self.ps('lin')
                parts = []
                for (wt, wk, kk, nk) in subs:
                    parts.append((wt, wk, list(range(nk)), mi * 128, 128, in_tile,
                                  list(range(kk, kk + nk)), [in_key_fn(k) for k in range(kk, kk + nk)]))
                self.mm_acc(psap, pskey, parts, W)
                cb(pi * 2 + mi, psap, pskey)

    def colsum_sq(self, src_fn, nchunks, W, scale_inv, eps, out_rstd, out_key):
        S = self.S
        if S.dry:
            return
        psap, pskey = self.ps('aux')
        for k in range(nchunks):
            sap, skey = src_fn(k)
            t, tk = self.tmp('sq')
            S.op('act', (lambda e, t=t, sap=sap, W=W: e.activation(out=t[:, 0:W], in_=sap, func=AF.Square)),
                 reads=[skey], writes=[tk])
            S.op('pe', (lambda e, t=t, psap=psap, W=W, k=k, n=nchunks: e.matmul(
                psap[:, 0:W], lhsT=self.ones_f[:, :], rhs=t[:, 0:W], start=(k == 0), stop=(k == n - 1))),
                 reads=[tk, 'const'], writes=[pskey])
        S.op('dve', (lambda e, psap=psap, W=W: e.tensor_scalar(out=out_rstd[:, 0:W], in0=psap[:, 0:W], scalar1=scale_inv,
                                                           scalar2=eps, op0=ALU.mult, op1=ALU.add)),
             reads=[pskey], writes=[out_key])
        self.rsqrt_(out_rstd[:, 0:W], out_key)

    def rsqrt_(self, ap, key):
        S = self.S
        S.op('act', (lambda e, ap=ap: e.activation(out=ap, in_=ap, func=AF.Sqrt)), reads=[key], writes=[key])
        S.op('dve', (lambda e, ap=ap: e.reciprocal(out=ap, in_=ap)), reads=[key], writes=[key])

    def rmsnorm(self, gtile, gl, W):
        c = self.c
        S = self.S
        if S.dry:
            return
        h, u = self.h, self.u
        self.colsum_sq(lambda k: (h[:, k, 0:W], ('h', k)), c.KC, W, 1.0 / c.D, RMS_EPS, self.rstd, 'rstd')
        for k in range(c.KC):
            S.op('dve', (lambda e, k=k, W=W: e.scalar_tensor_tensor(
                out=u[:, k, 0:W], in0=h[:, k, 0:W], scalar=gtile[:, gl, k:k + 1], in1=self.rstd[:, 0:W],
                op0=ALU.mult, op1=ALU.mult)), reads=[('h', k), 'rstd', 'const'], writes=[('u', k)])

    def conformer(self, W, sample, t0):
        c = self.c
        S = self.S
        KC = c.KC
        h, u = self.h, self.u
        Wp1 = self.w['conf_w_pw1'][0]
        Wp2 = self.w['conf_w_pw2'][0]
        cbuf = self.scr
        glub = self.scr_glu
        ps_mean = ps_var = None
        if not S.dry:
            ps_mean, km = self.ps('aux')
            ps_var, kv = self.ps('aux')
        if sample and not S.dry:
            pass
        pending_tail = None
        for j in range(KC // 2):
            a_src = Wp1[:, j * 256:(j + 1) * 256]
            g_src = Wp1[:, c.D + j * 256:c.D + (j + 1) * 256]
            wa, wak = self.wget(a_src, KC, 256, ('pw1a', j))
            wg, wgk = self.wget(g_src, KC, 256, ('pw1g', j), hold=1)
            if S.dry:
                continue
            caps = []
            splits = []
            for mi in range(2):
                if not sample:
                    S.cap = []
                    caps.append(S.cap)
                bi = 0 if sample else mi
                ch = 2 * j + mi
                gi = ch % 4
                pa, pak = self.ps('lin')
                pg, pgk = self.ps('lin')
                ukeys = [('u', k) for k in range(KC)]
                self.mm_acc(pa, pak, [(wa, wak, list(range(KC)), mi * 128, 128, u, list(range(KC)), ukeys)], W)
                self.mm_acc(pg, pgk, [(wg, wgk, list(range(KC)), mi * 128, 128, u, list(range(KC)), ukeys)], W)
                sg, sgk = self.c_a[bi], ('c_a', bi)
                S.op('act', (lambda e, sg=sg, pg=pg, W=W: e.activation(out=sg[:, 0:W], in_=pg[:, 0:W], func=AF.Sigmoid)),
                     reads=[pgk], writes=[sgk])
                acc, acck = self.c_b[bi], ('c_b', bi)
                if not sample:
                    gk = ('glu', gi)
                    S.op('act', (lambda e, gi=gi, ch=ch: e.activation(out=glub[:, gi, 0:30], in_=self.chalo[:, ch, :],
                                                                     func=AF.Copy)),
                         reads=[('chalo', ch)], writes=[gk])
                    S.op('dve', (lambda e, gi=gi, pa=pa, sg=sg, W=W: e.tensor_tensor(
                        out=glub[:, gi, 30:30 + W], in0=pa[:, 0:W], in1=sg[:, 0:W], op=ALU.mult)),
                         reads=[pak, sgk, gk], writes=[gk])
                    S.op('act', (lambda e, gi=gi, ch=ch, W=W: e.activation(out=self.chalo[:, ch, :],
                                                                          in_=glub[:, gi, W:W + 30], func=AF.Copy)),
                         reads=[gk], writes=[('chalo', ch)])
                    splits.append(len(S.cap))
                    for w in range(CW):
                        if w == 0:
                            S.op('dve', (lambda e, acc=acc, gi=gi, ch=ch, W=W: e.tensor_scalar(
                                out=acc[:, 0:W], in0=glub[:, gi, 0:W], scalar1=self.cwdw[:, ch, 0:1],
                                scalar2=self.cbdw[:, ch:ch + 1], op0=ALU.mult, op1=ALU.add)),
                                 reads=[gk, 'const'], writes=[acck])
                        else:
                            S.op('dve', (lambda e, acc=acc, gi=gi, ch=ch, W=W, w=w: e.scalar_tensor_tensor(
                                out=acc[:, 0:W], in0=glub[:, gi, w:w + W], scalar=self.cwdw[:, ch, w:w + 1],
                                in1=acc[:, 0:W], op0=ALU.mult, op1=ALU.add)),
                                 reads=[gk, acck, 'const'], writes=[acck])
                else:
                    NS = c.NS
                    st, stk = self.cs_st[0], ('csst', 0)
                    nst, nstk = self.cs_new[0], ('csnew', 0)
                    S.dma('pool', 'csin', (lambda e, st=st, ch=ch: e.dma_start(out=st[:, :, :], in_=self.d_cs_in[:, ch, :, :])),
                          writes=[stk])
                    gl, glk = self.c_c[bi], ('c_c', bi)
                    S.op('dve', (lambda e, gl=gl, pa=pa, sg=sg, W=W: e.tensor_tensor(
                        out=gl[:, 0:W], in0=pa[:, 0:W], in1=sg[:, 0:W], op=ALU.mult)),
                         reads=[pak, sgk], writes=[glk])
                    S.op('act', (lambda e, st=st, nst=nst: e.activation(out=nst[:, :, 0:29], in_=st[:, :, 1:30], func=AF.Copy)),
                         reads=[stk], writes=[nstk])
                    S.op('act', (lambda e, gl=gl, nst=nst, W=W: e.activation(out=nst[:, :, 29], in_=gl[:, 0:W], func=AF.Copy)),
                         reads=[glk, nstk], writes=[nstk])
                    S.dma('pool', 'csout', (lambda e, nst=nst, ch=ch: e.dma_start(out=self.d_conf_s[:, ch, :, :], in_=nst[:, :, :])),
                          reads=[nstk], writes=[('d_conf_s', ch)])
                    pr, prk = self.cs_prod[0], ('csprod', 0)
                    S.op('dve', (lambda e, st=st, ch=ch, pr=pr, NS=NS: e.tensor_tensor(
                        out=pr[:, :, :], in0=st[:, :, :], in1=self.cwdw[:, ch, 0:30].unsqueeze(1).to_broadcast([128, NS, 30]),
                        op=ALU.mult)), reads=[stk, 'const'], writes=[prk])
                    S.op('dve', (lambda e, acc=acc, pr=pr, W=W: e.tensor_reduce(out=acc[:, 0:W], in_=pr[:, :, :], axis=AX.X, op=ALU.add)),
                         reads=[prk], writes=[acck])
                    S.op('dve', (lambda e, acc=acc, gl=gl, ch=ch, W=W: e.scalar_tensor_tensor(
                        out=acc[:, 0:W], in0=gl[:, 0:W], scalar=self.cwdw[:, ch, 30:31], in1=acc[:, 0:W],
                        op0=ALU.mult, op1=ALU.add)), reads=[glk, acck, 'const'], writes=[acck])
                    S.op('dve', (lambda e, acc=acc, ch=ch, W=W: e.tensor_scalar(
                        out=acc[:, 0:W], in0=acc[:, 0:W], scalar1=self.cbdw[:, ch:ch + 1], scalar2=None, op0=ALU.add)),
                         reads=[acck, 'const'], writes=[acck])
                S.op('pe', (lambda e, acc=acc, W=W, ch=ch: e.matmul(ps_mean[:, 0:W], lhsT=self.ones_f[:, :], rhs=acc[:, 0:W],
                                                               start=(ch == 0), stop=(ch == KC - 1))),
                     reads=[acck, 'const'], writes=[km])
                sq, sqk = self.c_a[bi], ('c_a', bi)
                S.op('act', (lambda e, sq=sq, acc=acc, W=W: e.activation(out=sq[:, 0:W], in_=acc[:, 0:W], func=AF.Square)),
                     reads=[acck], writes=[sqk])
                S.op('pe', (lambda e, sq=sq, W=W, ch=ch: e.matmul(ps_var[:, 0:W], lhsT=self.ones_f[:, :], rhs=sq[:, 0:W],
                                                             start=(ch == 0), stop=(ch == KC - 1))),
                     reads=[sqk, 'const'], writes=[kv])
                S.op('act', (lambda e, acc=acc, W=W, ch=ch: e.activation(out=cbuf[:, ch, 0:W], in_=acc[:, 0:W], func=AF.Copy)),
                     reads=[acck], writes=[('scr', ch)])
                S.cap = None
            if caps:
                S.replay_zipped([cp[:sp] for cp, sp in zip(caps, splits)])
                if pending_tail is not None:
                    S.replay_zipped(pending_tail)
                pending_tail = [cp[sp:] for cp, sp in zip(caps, splits)]
        if pending_tail is not None:
            S.replay_zipped(pending_tail)
            pending_tail = None
        if not S.dry:
            mean, var = self.mean, self.rstd
            invD = 1.0 / c.D
            S.op('dve', (lambda e, W=W: e.tensor_scalar(out=mean[:, 0:W], in0=ps_mean[:, 0:W], scalar1=invD, scalar2=None,
                                                      op0=ALU.mult)), reads=[km], writes=['mean'])
            msq, msqk = self.tmp('msq')
            S.op('dve', (lambda e, W=W, msq=msq: e.tensor_tensor(out=msq[:, 0:W], in0=mean[:, 0:W], in1=mean[:, 0:W], op=ALU.mult)),
                 reads=['mean'], writes=[msqk])
            S.op('dve', (lambda e, W=W, msq=msq: e.scalar_tensor_tensor(out=var[:, 0:W], in0=ps_var[:, 0:W], scalar=invD,
                                                                      in1=msq[:, 0:W], op0=ALU.mult, op1=ALU.subtract)),
                 reads=[kv, msqk], writes=['rstd'])
            S.op('dve', (lambda e, W=W: e.tensor_scalar(out=var[:, 0:W], in0=var[:, 0:W], scalar1=LN_EPS, scalar2=None,
                                                      op0=ALU.add)), reads=['rstd'], writes=['rstd'])
            self.rsqrt_(var[:, 0:W], 'rstd')
            for ch in range(KC):
                t1, t1k = self.tmp('ln1')
                S.op('dve', (lambda e, t1=t1, ch=ch, W=W: e.tensor_tensor(out=t1[:, 0:W], in0=cbuf[:, ch, 0:W], in1=mean[:, 0:W],
                                                                        op=ALU.subtract)),
                     reads=[('scr', ch), 'mean'], writes=[t1k])
                S.op('dve', (lambda e, t1=t1, W=W: e.tensor_tensor(out=t1[:, 0:W], in0=t1[:, 0:W], in1=var[:, 0:W], op=ALU.mult)),
                     reads=[t1k, 'rstd'], writes=[t1k])
                S.op('act', (lambda e, t1=t1, ch=ch, W=W: e.activation(out=u[:, ch, 0:W], in_=t1[:, 0:W], func=AF.Silu,
                                                                     bias=self.clnb[:, ch:ch + 1], scale=self.clng[:, ch:ch + 1])),
                     reads=[t1k, 'const'], writes=[('u', ch)])

        def cb(mi, psap, pskey):
            S.op('dve', (lambda e, mi=mi, psap=psap, W=W: e.tensor_tensor(out=h[:, mi, 0:W], in0=h[:, mi, 0:W], in1=psap[:, 0:W],
                                                                        op=ALU.add)),
                 reads=[pskey, ('h', mi)], writes=[('h', mi)])
        self.linear(Wp2, KC, 0, c.D, u, lambda k: ('u', k), W, cb, 'pw2')

    def ffn(self, layer, W):
        c = self.c
        S = self.S
        KC = c.KC
        h, u = self.h, self.u
        Wg = self.w['ffn_w_gate'][layer]
        Wu = self.w['ffn_w_up'][layer]
        Wd = self.w['ffn_w_down'][layer]
        hid = self.scr
        ukeys = [('u', k) for k in range(KC)]
        for (f0, fn_) in c.FG:
            j = 0
            while j < fn_:
                ncol = min(2, fn_ - j) * 128
                c0 = (f0 + j) * 128
                wg, wgk = self.wget(Wg[:, c0:c0 + ncol], KC, ncol, ('ffg', layer, f0 + j))
                wu, wuk = self.wget(Wu[:, c0:c0 + ncol], KC, ncol, ('ffu', layer, f0 + j), hold=1)
                if not S.dry:
                    for mi in range(ncol // 128):
                        jj = j + mi
                        pg, pgk = self.ps('lin')
                        pu, puk = self.ps('lin')
                        self.mm_acc(pg, pgk, [(wg, wgk, list(range(KC)), mi * 128, 128, u, list(range(KC)), ukeys)], W)
                        self.mm_acc(pu, puk, [(wu, wuk, list(range(KC)), mi * 128, 128, u, list(range(KC)), ukeys)], W)
                        sg, sgk = self.tmp('silu')
                        S.op('act', (lambda e, sg=sg, pg=pg, W=W: e.activation(out=sg[:, 0:W], in_=pg[:, 0:W], func=AF.Silu)),
                             reads=[pgk], writes=[sgk])
                        S.op('dve', (lambda e, sg=sg, pu=pu, jj=jj, W=W: e.tensor_tensor(out=hid[:, jj, 0:W], in0=pu[:, 0:W],
                                                                                     in1=sg[:, 0:W], op=ALU.mult)),
                             reads=[puk, sgk], writes=[('scr', jj)])
                j += 2

            def cb(mi, psap, pskey):
                S.op('dve', (lambda e, mi=mi, psap=psap, W=W: e.tensor_tensor(out=h[:, mi, 0:W], in0=h[:, mi, 0:W],
                                                                            in1=psap[:, 0:W], op=ALU.add)),
                     reads=[pskey, ('h', mi)], writes=[('h', mi)])
            self.linear(Wd, fn_, 0, c.D, hid, lambda k: ('scr', k), W, cb, ('ffd', layer, f0), k0=f0)

    def ple(self, layer, W, sample, t0):
        c = self.c
        S = self.S
        h, u = self.h, self.u
        Wpg = self.w['ple_w_gate'][layer]
        Wpp = self.w['ple_w_proj'][layer]
        pT = self.pT
        if not S.dry:
            src = (self.d_psT[layer] if sample else self.d_ppT[layer][:, :, t0:t0 + W])
            S.dma('pool', 'pin', (lambda e, src=src, W=W: e.dma_start(out=pT[:, :, 0:W], in_=src)), writes=['pT'])
        pkeys = ['pT'] * c.PC
        for j in range(c.KC // 2):
            wg, wgk = self.wget(Wpg[:, j * 256:(j + 1) * 256], c.KC, 256, ('pleg', layer, j))
            wp, wpk = self.wget(Wpp[:, j * 256:(j + 1) * 256], c.PC, 256, ('plep', layer, j), hold=1)
            if S.dry:
                continue
            for mi in range(2):
                ch = 2 * j + mi
                pg, pgk = self.ps('lin')
                pp, ppk = self.ps('lin')
                self.mm_acc(pg, pgk, [(wg, wgk, list(range(c.KC)), mi * 128, 128, u, list(range(c.KC)),
                                       [('u', k) for k in range(c.KC)])], W)
                self.mm_acc(pp, ppk, [(wp, wpk, list(range(c.PC)), mi * 128, 128, pT, list(range(c.PC)), pkeys)], W)
                sg, sgk = self.tmp('sig')
                S.op('act', (lambda e, sg=sg, pg=pg, W=W: e.activation(out=sg[:, 0:W], in_=pg[:, 0:W], func=AF.Sigmoid)),
                     reads=[pgk], writes=[sgk])
                S.op('dve', (lambda e, sg=sg, pp=pp, W=W: e.tensor_tensor(out=sg[:, 0:W], in0=pp[:, 0:W], in1=sg[:, 0:W],
                                                                        op=ALU.mult)), reads=[ppk, sgk], writes=[sgk])
                S.op('dve', (lambda e, sg=sg, ch=ch, W=W: e.tensor_tensor(out=h[:, ch, 0:W], in0=h[:, ch, 0:W], in1=sg[:, 0:W],
                                                                        op=ALU.add)),
                     reads=[sgk, ('h', ch)], writes=[('h', ch)])

    def gdn_gates(self, W, ntok_blocks):
        c = self.c
        S = self.S
        H = c.H
        Win = self.w['gdn_w_in'][0]
        cb0 = c.QKV + H * 128
        if not S.dry:
            S.dma('pool', 'wba', (lambda e: e.dma_start(out=self.wba[:, :, :],
                                                       in_=Win[:, cb0:cb0 + 2 * H].rearrange("(kc p) n -> p kc n", p=128))),
                  writes=['wba'])
        if S.dry:
            return
        tb = min(W, C64)
        for blk in range(ntok_blocks):
            psap, pskey = self.ps('aux')
            u = self.u

            def fn(e, blk=blk, psap=psap, tb=tb):
                ins = None
                for k in range(c.KC):
                    ins = e.matmul(psap[0:tb, 0:2 * H], lhsT=u[:, k, blk * tb:(blk + 1) * tb], rhs=self.wba[:, k, :],
                                   start=(k == 0), stop=(k == c.KC - 1))
                return ins
            S.op('pe', fn, reads=['wba'] + [('u', k) for k in range(c.KC)], writes=[pskey])
            S.op('act', (lambda e, blk=blk, psap=psap, tb=tb: e.activation(out=self.g_beta[0:tb, blk, :], in_=psap[0:tb, 0:H],
                                                                         func=AF.Sigmoid)),
                 reads=[pskey], writes=['g_beta'])
            S.op('dve', (lambda e, blk=blk, psap=psap, tb=tb: e.tensor_tensor(out=self.g_g[0:tb, blk, :], in0=psap[0:tb, H:2 * H],
                                                                            in1=self.dtb[0:tb, :], op=ALU.add)),
                 reads=[pskey, 'const'], writes=['g_g'])
            S.op('act', (lambda e, blk=blk, tb=tb: e.activation(out=self.g_g[0:tb, blk, :], in_=self.g_g[0:tb, blk, :], func=AF.Exp)),
                 reads=['g_g'], writes=['g_g'])
            S.op('dve', (lambda e, blk=blk, tb=tb: e.tensor_scalar(out=self.g_g[0:tb, blk, :], in0=self.g_g[0:tb, blk, :],
                                                                 scalar1=1.0, scalar2=None, op0=ALU.add)),
                 reads=['g_g'], writes=['g_g'])
            S.op('act', (lambda e, blk=blk, tb=tb: e.activation(out=self.g_g[0:tb, blk, :], in_=self.g_g[0:tb, blk, :], func=AF.Ln)),
                 reads=['g_g'], writes=['g_g'])
            S.op('dve', (lambda e, blk=blk, tb=tb: e.tensor_tensor(out=self.g_g[0:tb, blk, :], in0=self.g_g[0:tb, blk, :],
                                                                 in1=self.nA[0:tb, :], op=ALU.mult)),
                 reads=['g_g', 'const'], writes=['g_g'])

    def gdn_proj_head(self, hd, W, which, sample):
        c = self.c
        S = self.S
        H = c.H
        KC = c.KC
        j = hd // 2
        Win = self.w['gdn_w_in'][0]
        nh = min(2, H - 2 * j)
        dsts = {'q': self.hq, 'k': self.hk, 'v': self.hv}
        par = 0
        qst = qnew = None
        if sample and not S.dry:
            qst, qnew = self.qs_st[par], self.qs_new[par]
            for idx in range(3):
                ch0 = idx * H + 2 * j
                S.dma('pool', f'qsin{par}', (lambda e, qst=qst, idx=idx, ch0=ch0: e.dma_start(
                    out=qst[:, idx, 0:nh, :, :], in_=self.d_qs_in[:, ch0:ch0 + nh, :, :])), writes=[('qs_st', par)])
            S.op('act', (lambda e, qst=qst, qnew=qnew: e.activation(out=qnew[:, :, :, :, 0:2], in_=qst[:, :, :, :, 1:3],
                                                                    func=AF.Copy)),
                 reads=[('qs_st', par)], writes=[('qs_new', par)])
        for idx, nm in enumerate(('q', 'k', 'v', 'z')):
            colbase = (idx * H * 128 if nm != 'z' else c.QKV) + j * 256
            wt, wk = self.wget(Win[:, colbase:colbase + nh * 128], KC, nh * 128, ('gin', nm, j))
            if S.dry:
                continue
            for hh in range(nh):
                head = 2 * j + hh
                psap, pskey = self.ps('lin')
                self.mm_acc(psap, pskey, [(wt, wk, list(range(KC)), hh * 128, 128, self.u, list(range(KC)),
                                           [('u', k) for k in range(KC)])], W)
                if nm == 'z':
                    S.op('act', (lambda e, hh=hh, psap=psap, W=W: e.activation(out=self.hz[hh][:, 0:W], in_=psap[:, 0:W],
                                                                             func=AF.Silu)),
                         reads=[pskey], writes=[('hz', hh)])
                    continue
                qch = idx * H + head
                dst = dsts[nm][hh]
                dkey = ('h' + nm, hh)
                if not sample:
                    cv, cvk = self.cvb[idx % 2][hh], ('cvb', idx % 2, hh)
                    S.op('act', (lambda e, cv=cv, qch=qch: e.activation(out=cv[:, 0:3], in_=self.qhalo[:, qch, :], func=AF.Copy)),
                         reads=[('qhalo', qch)], writes=[cvk])
                    S.op('act', (lambda e, cv=cv, psap=psap, W=W: e.activation(out=cv[:, 3:3 + W], in_=psap[:, 0:W], func=AF.Copy)),
                         reads=[pskey, cvk], writes=[cvk])
                    S.op('act', (lambda e, cv=cv, qch=qch, W=W: e.activation(out=self.qhalo[:, qch, :], in_=cv[:, W:W + 3],
                                                                           func=AF.Copy)),
                         reads=[cvk], writes=[('qhalo', qch)])
                    for w in range(SC):
                        if w == 0:
                            S.op('dve', (lambda e, cv=cv, dst=dst, qch=qch, W=W: e.tensor_scalar(
                                out=dst[:, 0:W], in0=cv[:, 0:W], scalar1=self.gwc[:, qch, 0:1], scalar2=None, op0=ALU.mult)),
                                 reads=[cvk, 'const'], writes=[dkey])
                        else:
                            S.op('dve', (lambda e, cv=cv, dst=dst, qch=qch, W=W, w=w: e.scalar_tensor_tensor(
                                out=dst[:, 0:W], in0=cv[:, w:w + W], scalar=self.gwc[:, qch, w:w + 1], in1=dst[:, 0:W],
                                op0=ALU.mult, op1=ALU.add)), reads=[cvk, dkey, 'const'], writes=[dkey])
                else:
                    NS = c.NS
                    S.op('act', (lambda e, qnew=qnew, idx=idx, hh=hh, psap=psap, W=W: e.activation(
                        out=qnew[:, idx, hh, :, 2], in_=psap[:, 0:W], func=AF.Copy)),
                         reads=[pskey, ('qs_new', par)], writes=[('qs_new', par)])
                    pr, prk = self.qs_prod, 'qsprod'
                    S.op('dve', (lambda e, qst=qst, idx=idx, hh=hh, qch=qch, pr=pr, NS=NS: e.tensor_tensor(
                        out=pr[:, :, :], in0=qst[:, idx, hh, :, :],
                        in1=self.gwc[:, qch, 0:3].unsqueeze(1).to_broadcast([128, NS, 3]), op=ALU.mult)),
                         reads=[('qs_st', par), 'const'], writes=[prk])
                    S.op('dve', (lambda e, dst=dst, pr=pr, W=W: e.tensor_reduce(out=dst[:, 0:W], in_=pr[:, :, :], axis=AX.X,
                                                                               op=ALU.add)),
                         reads=[prk], writes=[dkey])
                    S.op('dve', (lambda e, dst=dst, psap=psap, qch=qch, W=W: e.scalar_tensor_tensor(
                        out=dst[:, 0:W], in0=psap[:, 0:W], scalar=self.gwc[:, qch, 3:4], in1=dst[:, 0:W],
                        op0=ALU.mult, op1=ALU.add)), reads=[pskey, dkey, 'const'], writes=[dkey])
                S.op('act', (lambda e, dst=dst, W=W: e.activation(out=dst[:, 0:W], in_=dst[:, 0:W], func=AF.Silu)),
                     reads=[dkey], writes=[dkey])
                if nm in ('q', 'k'):
                    rn, rnk = self.tmp('rn')
                    self.colsum_sq(lambda k, dst=dst, dkey=dkey, W=W: (dst[:, 0:W], dkey), 1, W, 1.0, L2_EPS, rn, rnk)
                    sc = (128.0 ** -0.5) if nm == 'q' else 1.0
                    S.op('dve', (lambda e, dst=dst, rn=rn, W=W, sc=sc: e.scalar_tensor_tensor(
                        out=dst[:, 0:W], in0=dst[:, 0:W], scalar=sc, in1=rn[:, 0:W], op0=ALU.mult, op1=ALU.mult)),
                         reads=[dkey, rnk], writes=[dkey])

        if sample and not S.dry:
            for idx in range(3):
                ch0 = idx * H + 2 * j
                S.dma('pool', f'qsout{par}', (lambda e, qnew=qnew, idx=idx, ch0=ch0: e.dma_start(
                    out=self.d_qkv_s[:, ch0:ch0 + nh, :, :], in_=qnew[:, idx, 0:nh, :, :])),
                      reads=[('qs_new', par)], writes=[('d_qkv_s', idx, j)])

    def gdn_prompt_head(self, hd, hh, W):
        c = self.c
        S = self.S
        H = c.H
        NCH = W // C64
        NW = NCH * C64
        T = self.Tb[hh]
        qn, kn, vv = self.hq[hh], self.hk[hh], self.hv[hh]
        kq, kk, kv_ = ('hq', hh), ('hk', hh), ('hv', hh)
        ident = self.ident_f
        knb, qnb = T.knb, T.qnb
        S.op('act', (lambda e: e.activation(out=knb[:, 0:NW], in_=kn[:, 0:NW], func=AF.Copy)), reads=[kk], writes=['knb'])
        gcol = self.g_g[0:C64, 0:NCH, hd]
        bcol = self.g_beta[0:C64, 0:NCH, hd]
        pG, pGk = self.ps('aux')
        S.op('pe', (lambda e: e.matmul(pG[0:C64, 0:NCH], lhsT=self.tri_f[0:C64, 0:C64], rhs=gcol, start=True, stop=True)),
             reads=['g_g', 'const'], writes=[pGk])
        S.op('pe', (lambda e: e.matmul(pG[:, 64:64 + NCH], lhsT=self.ones_f[0:C64, :], rhs=gcol, start=True, stop=True)),
             reads=['g_g', 'const', pGk], writes=[pGk])
        Gc = T.t_G
        S.op('dve', (lambda e: e.tensor_copy(out=Gc[:, 0:NCH], in_=pG[0:C64, 0:NCH])), reads=[pGk], writes=['t_G'])
        S.op('act', (lambda e: e.activation(out=T.t_alast[:, 0:NCH], in_=pG[:, 64:64 + NCH], func=AF.Exp)),
             reads=[pGk], writes=['t_alast'])
        S.op('dve', (lambda e: e.tensor_tensor(out=T.t_kdsc[:, 0:NCH], in0=pG[0:C64, 64:64 + NCH], in1=Gc[:, 0:NCH],
                                               op=ALU.subtract)), reads=[pGk, 't_G'], writes=['t_kdsc'])
        S.op('act', (lambda e: e.activation(out=T.t_kdsc[:, 0:NCH], in_=T.t_kdsc[:, 0:NCH], func=AF.Exp)),
             reads=['t_kdsc'], writes=['t_kdsc'])
        S.op('act', (lambda e: e.activation(out=T.t_eG[:, 0:NCH], in_=Gc[:, 0:NCH], func=AF.Exp)), reads=['t_G'],
             writes=['t_eG'])
        S.op('dve', (lambda e: e.scalar_tensor_tensor(out=T.t_nbe[:, 0:NCH], in0=T.t_eG[:, 0:NCH], scalar=-1.0, in1=bcol,
                                                      op0=ALU.mult, op1=ALU.mult)), reads=['t_eG', 'g_beta'], writes=['t_nbe'])
        gt, gtk = T.t_X, 't_X'
        S.op('dve', (lambda e: e.tensor_tensor(out=gt[:, 0:NCH, :], in0=gcol.unsqueeze(2).to_broadcast([C64, NCH, C64]),
                                               in1=self.tri_f[0:C64, 0:C64].unsqueeze(1).to_broadcast([C64, NCH, C64]),
                                               op=ALU.mult)), reads=['g_g', 'const'], writes=[gtk])
        pR, pRk = self.ps('aux')
        S.op('pe', (lambda e: e.matmul(pR[:, 0:NW], lhsT=self.ones_f[0:C64, :], rhs=gt[:, 0:NCH, :].rearrange("p c x -> p (c x)"),
                                       start=True, stop=True)), reads=[gtk, 'const'], writes=[pRk])
        S.op('act', (lambda e: e.activation(out=T.t_eGrow[:, 0:NW], in_=pR[:, 0:NW], func=AF.Exp)), reads=[pRk],
             writes=['t_eGrow'])
        S.op('dve', (lambda e: e.tensor_tensor(out=qnb[:, 0:NW], in0=qn[:, 0:NW], in1=T.t_eGrow[:, 0:NW], op=ALU.mult)),
             reads=[kq, 't_eGrow'], writes=['qnb'])
        X, Xk = T.t_X, 't_X'
        S.op('dve', (lambda e: e.tensor_tensor(out=X[:, 0:NCH, :], in0=pR[0:C64, 0:NW].rearrange("p (c x) -> p c x", c=NCH),
                                               in1=Gc[:, 0:NCH].unsqueeze(2).to_broadcast([C64, NCH, C64]), op=ALU.subtract)),
             reads=[pRk, 't_G'], writes=[Xk])
        D, DT = T.t_D, T.t_DT
        S.op('dve', (lambda e: e.scalar_tensor_tensor(out=D[:, 0:NCH, :], in0=X[:, 0:NCH, :], scalar=-1.0,
                                                      in1=self.m_up.unsqueeze(1).to_broadcast([C64, NCH, C64]),
                                                      op0=ALU.mult, op1=ALU.subtract)), reads=[Xk, 'const'], writes=['t_D'])
        S.op('act', (lambda e: e.activation(out=D[:, 0:NCH, :], in_=D[:, 0:NCH, :], func=AF.Exp)), reads=['t_D'], writes=['t_D'])
        S.op('dve', (lambda e: e.tensor_tensor(out=DT[:, 0:NCH, :], in0=X[:, 0:NCH, :],
                                               in1=self.m_lo.unsqueeze(1).to_broadcast([C64, NCH, C64]), op=ALU.subtract)),
             reads=[Xk, 'const'], writes=['t_DT'])
        S.op('act', (lambda e: e.activation(out=DT[:, 0:NCH, :], in_=DT[:, 0:NCH, :], func=AF.Exp)), reads=['t_DT'],
             writes=['t_DT'])
        S.op('dve', (lambda e: e.tensor_tensor(out=D[:, 0:NCH, :], in0=D[:, 0:NCH, :],
                                               in1=self.m_strict.unsqueeze(1).to_broadcast([C64, NCH, C64]), op=ALU.mult)),
             reads=['t_D', 'const'], writes=['t_D'])
        S.op('dve', (lambda e: e.scalar_tensor_tensor(out=D[:, 0:NCH, :], in0=D[:, 0:NCH, :], scalar=-1.0,
                                                      in1=bcol.unsqueeze(2).to_broadcast([C64, NCH, C64]),
                                                      op0=ALU.mult, op1=ALU.mult)), reads=['t_D', 'g_beta'], writes=['t_D'])
        pA, pAk = self.ps('aux')
        pQ, pQk = self.ps('aux')

        def fa(e):
            ins = None
            for ci in range(NCH):
                sl = slice(ci * C64, (ci + 1) * C64)
                ins = e.matmul(pA[0:C64, sl], lhsT=knb[:, sl], rhs=knb[:, sl], start=True, stop=True)
            return ins
        S.op('pe', fa, reads=['knb'], writes=[pAk])
        S.op('act', (lambda e: e.activation(out=T.qrb[:, 0:NW], in_=qn[:, 0:NW], func=AF.Copy)), reads=[kq], writes=['qrb'])

        def fq(e):
            ins = None
            for ci in range(NCH):
                sl = slice(ci * C64, (ci + 1) * C64)
                ins = e.matmul(pQ[0:C64, sl], lhsT=knb[:, sl], rhs=T.qrb[:, sl], start=True, stop=True)
            return ins
        S.op('pe', fq, reads=['knb', 'qrb'], writes=[pQk])
        Nm, NmT = T.t_N, T.t_NT
        S.op('dve', (lambda e: e.tensor_tensor(out=Nm[0][:, 0:NCH, :], in0=pA[0:C64, 0:NW].rearrange("p (c x) -> p c x", c=NCH),
                                               in1=D[:, 0:NCH, :], op=ALU.mult)), reads=[pAk, 't_D'], writes=[('t_N', 0)])
        S.op('dve', (lambda e: e.tensor_tensor(out=T.t_attT[:, 0:NCH, :],
                                               in0=pQ[0:C64, 0:NW].rearrange("p (c x) -> p c x", c=NCH),
                                               in1=DT[:, 0:NCH, :], op=ALU.mult)), reads=[pQk, 't_DT'], writes=['t_attT'])
        pT_, pTk = self.ps('aux')

        def ft(e):
            ins = None
            for ci in range(NCH):
                sl = slice(ci * C64, (ci + 1) * C64)
                ins = e.transpose(pT_[0:C64, sl], Nm[0][:, ci, :], ident[0:C64, 0:C64])
            return ins
        S.op('pe', ft, reads=[('t_N', 0), 'const'], writes=[pTk])
        S.op('act', (lambda e: e.activation(out=NmT[0][:, 0:NCH, :], in_=pT_[0:C64, 0:NW].rearrange("p (c x) -> p c x", c=NCH),
                                            func=AF.Copy)), reads=[pTk], writes=[('t_NT', 0)])
        PT = T.t_PT
        S.op('dve', (lambda e: e.tensor_tensor(out=PT[0][:, 0:NCH, :], in0=NmT[0][:, 0:NCH, :],
                                               in1=ident[0:C64, 0:C64].unsqueeze(1).to_broadcast([C64, NCH, C64]), op=ALU.add)),
             reads=[('t_NT', 0), 'const'], writes=[('t_PT', 0)])
        cur = 0
        for lvl in range(1, 6):
            nxt = 1 - cur
            pM, pMk = self.ps('aux')

            def fm(e, cur=cur, pM=pM):
                ins = None
                for ci in range(NCH):
                    sl = slice(ci * C64, (ci + 1) * C64)
                    ins = e.matmul(pM[0:C64, sl], lhsT=NmT[cur][:, ci, :], rhs=Nm[cur][:, ci, :], start=True, stop=True)
                return ins
            S.op('pe', fm, reads=[('t_N', cur), ('t_NT', cur)], writes=[pMk])
            S.op('act', (lambda e, nxt=nxt, pM=pM: e.activation(out=Nm[nxt][:, 0:NCH, :],
                                                               in_=pM[0:C64, 0:NW].rearrange("p (c x) -> p c x", c=NCH),
                                                               func=AF.Copy)), reads=[pMk], writes=[('t_N', nxt)])
            if lvl < 5:
                pMT, pMTk = self.ps('aux')

                def fmt(e, cur=cur, pMT=pMT):
                    ins = None
                    for ci in range(NCH):
                        sl = slice(ci * C64, (ci + 1) * C64)
                        ins = e.matmul(pMT[0:C64, sl], lhsT=Nm[cur][:, ci, :], rhs=NmT[cur][:, ci, :], start=True, stop=True)
                    return ins
                S.op('pe', fmt, reads=[('t_N', cur), ('t_NT', cur)], writes=[pMTk])
                S.op('dve', (lambda e, nxt=nxt, pMT=pMT: e.tensor_copy(out=NmT[nxt][:, 0:NCH, :],
                                                                      in_=pMT[0:C64, 0:NW].rearrange("p (c x) -> p c x", c=NCH))),
                     reads=[pMTk], writes=[('t_NT', nxt)])
            pP, pPk = self.ps('aux')

            def fp(e, nxt=nxt, cur=cur, pP=pP):
                ins = None
                for ci in range(NCH):
                    sl = slice(ci * C64, (ci + 1) * C64)
                    ins = e.matmul(pP[0:C64, sl], lhsT=Nm[nxt][:, ci, :], rhs=PT[0][:, ci, :], start=True, stop=True)
                return ins
            S.op('pe', fp, reads=[('t_N', nxt), ('t_PT', 0)], writes=[pPk])
            S.op('dve', (lambda e, nxt=nxt, cur=cur, pP=pP: e.tensor_tensor(
                out=PT[0][:, 0:NCH, :], in0=pP[0:C64, 0:NW].rearrange("p (c x) -> p c x", c=NCH), in1=PT[0][:, 0:NCH, :],
                op=ALU.add)), reads=[pPk, ('t_PT', 0)], writes=[('t_PT', 0)])
            cur = nxt
        PTf, PTk = PT[0], ('t_PT', 0)
        for ci in range(NCH):
            sl = slice(ci * C64, (ci + 1) * C64)
            pk, pkk = self.ps('aux')
            S.op('pe', (lambda e, pk=pk, sl=sl: e.transpose(pk[0:C64, 0:128], kn[:, sl], ident[:, :])), reads=[kk, 'const'],
                 writes=[pkk])
            S.op('pe', (lambda e, pk=pk, sl=sl: e.transpose(pk[0:C64, 128:256], vv[:, sl], ident[:, :])),
                 reads=[kv_, 'const', pkk], writes=[pkk])
            S.op('dve', (lambda e, pk=pk, ci=ci: e.tensor_scalar(out=T.t_kd[:, ci, :], in0=pk[0:C64, 0:128],
                                                               scalar1=T.t_kdsc[:, ci:ci + 1], scalar2=None, op0=ALU.mult)),
                 reads=[pkk, 't_kdsc'], writes=[('t_kd', ci)])
            S.op('dve', (lambda e, pk=pk, ci=ci: e.tensor_scalar(out=T.t_bv[:, ci, :], in0=pk[0:C64, 128:256],
                                                               scalar1=self.g_beta[0:C64, ci, hd:hd + 1], scalar2=None,
                                                               op0=ALU.mult)),
                 reads=[pkk, 'g_beta'], writes=[('t_bv', ci)])
        Sf = self.Sst[:, hd, :]
        Sk = ('S', hd)
        Sb = T.Sbf
        S.op('act', (lambda e: e.activation(out=Sb[:, :], in_=Sf, func=AF.Copy)), reads=[Sk], writes=['Sbf'])
        for ci in range(NCH):
            sl = slice(ci * C64, (ci + 1) * C64)
            p1, p1k = self.ps('aux')
            S.op('pe', (lambda e, p1=p1, sl=sl: e.matmul(p1[0:C64, 0:128], lhsT=knb[:, sl], rhs=Sb[:, :], start=True, stop=True)),
                 reads=['knb', 'Sbf'], writes=[p1k])
            S.op('dve', (lambda e, p1=p1, ci=ci: e.scalar_tensor_tensor(out=T.t_R[:, :], in0=p1[0:C64, 0:128],
                                                                      scalar=T.t_nbe[:, ci:ci + 1], in1=T.t_bv[:, ci, :],
                                                                      op0=ALU.mult, op1=ALU.add)),
                 reads=[p1k, 't_nbe', ('t_bv', ci)], writes=['t_R'])
            S.op('pe', (lambda e, p1=p1, ci=ci: e.matmul(p1[0:C64, 128:256], lhsT=PTf[:, ci, :], rhs=T.t_R[:, :],
                                                       start=True, stop=True)), reads=[PTk, 't_R', p1k], writes=[p1k])
            S.op('act', (lambda e, p1=p1: e.activation(out=T.t_ub[:, :], in_=p1[0:C64, 128:256], func=AF.Copy)),
                 reads=[p1k], writes=['t_ub'])
            p2, p2k = self.ps('aux')

            def fo(e, p2=p2, sl=sl, ci=ci):
                e.matmul(p2[0:C64, 0:128], lhsT=qnb[:, sl], rhs=Sb[:, :], start=True, stop=False)
                return e.matmul(p2[0:C64, 0:128], lhsT=T.t_attT[:, ci, :], rhs=T.t_ub[:, :], start=False, stop=True)
            S.op('pe', fo, reads=['qnb', 'Sbf', 't_attT', 't_ub'], writes=[p2k])
            S.op('act', (lambda e, p2=p2, ci=ci: e.activation(out=T.t_o[:, ci, :], in_=p2[0:C64, 0:128], func=AF.Copy)),
                 reads=[p2k], writes=[('t_o', ci)])
            p3, p3k = self.ps('aux')
            S.op('pe', (lambda e, p3=p3, ci=ci: e.matmul(p3[:, 0:128], lhsT=T.t_kd[:, ci, :], rhs=T.t_ub[:, :],
                                                       start=True, stop=True)), reads=[('t_kd', ci), 't_ub'], writes=[p3k])
            S.op('dve', (lambda e, p3=p3, ci=ci: e.scalar_tensor_tensor(out=Sf, in0=Sf, scalar=T.t_alast[:, ci:ci + 1],
                                                                      in1=p3[:, 0:128], op0=ALU.mult, op1=ALU.add)),
                 reads=[p3k, 't_alast', Sk], writes=[Sk])
            if ci < NCH - 1:
                S.op('act', (lambda e: e.activation(out=Sb[:, :], in_=Sf, func=AF.Copy)), reads=[Sk], writes=['Sbf'])
        sq = T.t_bv
        sqkeys = [('t_bv', ci) for ci in range(NCH)]
        S.op('dve', (lambda e: e.tensor_tensor(out=sq[:, 0:NCH, :], in0=T.t_o[:, 0:NCH, :], in1=T.t_o[:, 0:NCH, :],
                                               op=ALU.mult)), reads=[('t_o', ci) for ci in range(NCH)], writes=sqkeys)
        S.op('dve', (lambda e: e.tensor_reduce(out=T.t_oss[:, 0:NCH], in_=sq[:, 0:NCH, :], axis=AX.X, op=ALU.add)),
             reads=sqkeys, writes=['t_oss'])
        S.op('dve', (lambda e: e.tensor_scalar(out=T.t_oss[:, 0:NCH], in0=T.t_oss[:, 0:NCH], scalar1=1.0 / 128,
                                               scalar2=RMS_EPS, op0=ALU.mult, op1=ALU.add)), reads=['t_oss'], writes=['t_oss'])
        self.rsqrt_(T.t_oss[:, 0:NCH], 't_oss')
        S.op('dve', (lambda e: e.tensor_tensor(out=sq[:, 0:NCH, :], in0=T.t_o[:, 0:NCH, :],
                                               in1=T.t_oss[:, 0:NCH].unsqueeze(2).to_broadcast([C64, NCH, 128]), op=ALU.mult)),
             reads=[('t_o', ci) for ci in range(NCH)] + ['t_oss'] + sqkeys, writes=sqkeys)
        pO, pOk = self.ps('aux')

        def fto(e):
            ins = None
            for ci in range(NCH):
                ins = e.transpose(pO[:, ci * C64:(ci + 1) * C64], sq[:, ci, :], ident[0:C64, 0:C64])
            return ins
        S.op('pe', fto, reads=sqkeys + ['const'], writes=[pOk])
        S.op('dve', (lambda e: e.scalar_tensor_tensor(out=self.scr[:, hd, 0:NW], in0=pO[:, 0:NW], scalar=self.gon[:, 0:1],
                                                      in1=self.hz[hh][:, 0:NW], op0=ALU.mult, op1=ALU.mult)),
             reads=[pOk, ('hz', hh), 'const'], writes=[('scr', hd)])

    def gdn_sample_states(self, W):
        c = self.c
        S = self.S
        H, NS = c.H, c.NS
        ident = self.ident_f
        S.op('act', (lambda e: e.activation(out=self.s_a[0:NS, :], in_=self.g_g[0:NS, 0, :], func=AF.Exp)),
             reads=['g_g'], writes=['s_a'])
        for b in range(NS):
            Sb_, Sbk = self.s_S, 's_S'
            S.dma('pool', 'sin', (lambda e, Sb_=Sb_, b=b: e.dma_start(out=Sb_[:, :, :], in_=self.d_gs_in[b])), writes=[Sbk])
            S.op('dve', (lambda e, b=b: e.tensor_scalar(out=self.s_bd[0:NS, 0, :], in0=self.s_a[0:NS, :],
                                                       scalar1=ident[0:NS, b:b + 1], scalar2=None, op0=ALU.mult)),
                 reads=['s_a', 'const'], writes=['s_bd'])
            S.op('dve', (lambda e, b=b: e.tensor_scalar(out=self.s_bd[0:NS, 1, :], in0=self.g_beta[0:NS, 0, :],
                                                       scalar1=ident[0:NS, b:b + 1], scalar2=None, op0=ALU.mult)),
                 reads=['g_beta', 'const', 's_bd'], writes=['s_bd'])
            pb, pbk = self.ps('aux')
            S.op('pe', (lambda e, pb=pb: e.matmul(pb[:, 0:2 * H], lhsT=self.ones_f[0:NS, :],
                                                  rhs=self.s_bd[0:NS, :, :].rearrange("p a h -> p (a h)"),
                                                  start=True, stop=True)), reads=['s_bd', 'const'], writes=[pbk])
            S.op('act', (lambda e, pb=pb: e.activation(out=self.s_ab[:, :, :],
                                                       in_=pb[:, 0:2 * H].rearrange("p (a h) -> p a h", a=2), func=AF.Copy)),
                 reads=[pbk], writes=['s_ab'])
            pk, pkk = self.ps('aux')

            def fks(e, Sb_=Sb_, pk=pk, b=b):
                ins = None
                for hd in range(H):
                    ins = e.matmul(pk[:, hd:hd + 1], lhsT=Sb_[:, hd, :], rhs=self.s_k[:, hd, b:b + 1], start=True, stop=True)
                return ins
            S.op('pe', fks, reads=[Sbk, 's_k'], writes=[pkk])
            r, rk = self.s_r, 's_r'
            S.op('dve', (lambda e, pk=pk, b=b: e.tensor_tensor(out=r[:, :], in0=pk[:, 0:H], in1=self.s_ab[:, 0, :], op=ALU.mult)),
                 reads=[pkk, 's_ab'], writes=[rk])
            S.op('dve', (lambda e, b=b: e.tensor_tensor(out=r[:, :], in0=self.s_v[:, :, b], in1=r[:, :], op=ALU.subtract)),
                 reads=[rk, 's_v'], writes=[rk])
            S.op('dve', (lambda e, b=b: e.tensor_tensor(out=r[:, :], in0=r[:, :], in1=self.s_ab[:, 1, :], op=ALU.mult)),
                 reads=[rk, 's_ab'], writes=[rk])
            pt, ptk = self.ps('aux')
            S.op('pe', (lambda e, pt=pt, b=b: e.transpose(pt[0:H, 0:128], self.s_k[:, :, b], ident[:, :])), reads=['s_k', 'const'],
                 writes=[ptk])
            S.op('pe', (lambda e, pt=pt: e.transpose(pt[0:H, 128:256], r[:, :], ident[:, :])), reads=[rk, 'const', ptk],
                 writes=[ptk])
            S.op('act', (lambda e, pt=pt: e.activation(out=self.s_rows[0:H, :], in_=pt[0:H, 0:256], func=AF.Copy)), reads=[ptk],
                 writes=['s_rows'])
            for g0 in range(0, H, 2):
                ng = min(2, H - g0)
                po, pok = self.ps('aux')
                rb, rbk = self.s_rbd[(g0 // 2) % 2], ('s_rbd', (g0 // 2) % 2)
                S.op('dve', (lambda e, rb=rb, g0=g0, ng=ng: e.tensor_tensor(
                    out=rb[0:H, 0:ng, :], in0=self.s_rows[0:H, 128:256].unsqueeze(1).to_broadcast([H, ng, 128]),
                    in1=ident[0:H, g0:g0 + ng].unsqueeze(2).to_broadcast([H, ng, 128]), op=ALU.mult)),
                     reads=['s_rows', 'const'], writes=[rbk])
                S.op('pe', (lambda e, po=po, rb=rb, ng=ng: e.matmul(
                    po[:, 0:ng * 128], lhsT=self.s_rows[0:H, 0:128],
                    rhs=rb[0:H, 0:ng, :].rearrange("p h d -> p (h d)"), start=True, stop=True)),
                     reads=['s_rows', rbk], writes=[pok])
                S.op('dve', (lambda e, Sb_=Sb_, b=b, g0=g0, ng=ng: e.tensor_tensor(
                    out=Sb_[:, g0:g0 + ng, :], in0=Sb_[:, g0:g0 + ng, :],
                    in1=self.s_ab[:, 0, g0:g0 + ng].unsqueeze(2).to_broadcast([128, ng, 128]), op=ALU.mult)),
                     reads=[Sbk, 's_ab'], writes=[Sbk])
                S.op('dve', (lambda e, Sb_=Sb_, po=po, g0=g0, ng=ng: e.tensor_tensor(
                    out=Sb_[:, g0:g0 + ng, :], in0=Sb_[:, g0:g0 + ng, :],
                    in1=po[:, 0:ng * 128].rearrange("p (h d) -> p h d", h=ng), op=ALU.add)),
                     reads=[Sbk, pok], writes=[Sbk])
            pq, pqk = self.ps('aux')

            def foq(e, Sb_=Sb_, pq=pq, b=b):
                ins = None
                for hd in range(H):
                    ins = e.matmul(pq[:, hd:hd + 1], lhsT=Sb_[:, hd, :], rhs=self.s_q[:, hd, b:b + 1], start=True, stop=True)
                return ins
            S.op('pe', foq, reads=[Sbk, 's_q'], writes=[pqk])
            S.op('act', (lambda e, pq=pq, b=b: e.activation(out=self.s_o[:, :, b], in_=pq[:, 0:H], func=AF.Copy)), reads=[pqk],
                 writes=['s_o'])
            S.dma('pool', 'sout', (lambda e, Sb_=Sb_, b=b: e.dma_start(out=self.d_gdn_s[b], in_=Sb_[:, :, :])),
                  reads=[Sbk], writes=[('d_gdn_s', b)])
        HB = H * NS
        of = self.s_o[:, :, :].rearrange("p h b -> p (h b)")
        hstep = max(1, 256 // NS)
        for h0 in range(0, H, hstep):
            nh = min(hstep, H - h0)
            o0, n = h0 * NS, nh * NS
            rn, rnk = self.tmp('rn')
            self.colsum_sq(lambda k, o0=o0, n=n: (of[:, o0:o0 + n], 's_o'), 1, n, 1.0 / 128, RMS_EPS, rn, rnk)
            t, tk = self.tmp('on')
            S.op('dve', (lambda e, t=t, rn=rn, o0=o0, n=n: e.scalar_tensor_tensor(out=t[:, 0:n], in0=of[:, o0:o0 + n],
                                                                               scalar=self.gon[:, 0:1], in1=rn[:, 0:n],
                                                                               op0=ALU.mult, op1=ALU.mult)),
                 reads=['s_o', rnk, 'const'], writes=[tk])
            S.op('dve', (lambda e, t=t, n=n, h0=h0, nh=nh: e.tensor_tensor(
                out=self.scr[:, h0:h0 + nh, 0:NS], in0=t[:, 0:n].rearrange("p (h b) -> p h b", h=nh),
                in1=self.s_z[:, h0:h0 + nh, :], op=ALU.mult)),
                 reads=[tk, 's_z'], writes=[('scr', hx) for hx in range(h0, h0 + nh)])

    def gdn(self, W, sample, t0):
        c = self.c
        S = self.S
        H, NS = c.H, c.NS
        h = self.h
        self.gdn_gates(W, 1 if sample else W // C64)
        for j in range((H + 1) // 2):
            self.gdn_proj_head(2 * j, W, None, sample)
            if S.dry:
                continue
            if not sample:
                caps = []
                for hh in range(min(2, H - 2 * j)):
                    S.cap = []
                    S.ns = hh
                    self.aux_set = (2 * hh, 2 * hh + 1)
                    self.gdn_prompt_head(2 * j + hh, hh, W)
                    caps.append(S.cap)
                    S.cap = None
                    S.ns = None
                    self.aux_set = None
                S.replay_zipped(caps)
                continue
            for hh in range(min(2, H - 2 * j)):
                hd = 2 * j + hh
                if True:
                    for nm, src, dst in (('q', self.hq, self.s_q), ('k', self.hk, self.s_k), ('v', self.hv, self.s_v),
                                         ('z', self.hz, self.s_z)):
                        S.op('act', (lambda e, src=src, dst=dst, hh=hh, hd=hd: e.activation(out=dst[:, hd, :], in_=src[hh][:, 0:NS],
                                                                                            func=AF.Copy)),
                             reads=[('h' + nm, hh)], writes=['s_' + nm])
        if sample and not S.dry:
            self.gdn_sample_states(W)

        def cb(mi, psap, pskey):
            S.op('dve', (lambda e, mi=mi, psap=psap, W=W: e.tensor_tensor(out=h[:, mi, 0:W], in0=h[:, mi, 0:W], in1=psap[:, 0:W],
                                                                        op=ALU.add)),
                 reads=[pskey, ('h', mi)], writes=[('h', mi)])
        self.linear(self.w['gdn_w_out'][0], c.KC, 0, c.D, self.scr, lambda k: ('scr', k), W, cb, 'gout')

    def tile(self, ti, sample):
        c = self.c
        S = self.S
        W = c.NS if sample else c.TT
        t0 = 0 if sample else ti * c.TT
        h = self.h
        if not S.dry:
            src = self.d_xsT if sample else self.d_xpT[:, :, t0:t0 + W]
            S.dma('pool', 'xin', (lambda e, src=src, W=W: e.dma_start(out=h[:, :, 0:W], in_=src)),
                  writes=[('h', k) for k in range(c.KC)])
        for layer in range(2):
            self.rmsnorm(self.gmix, layer, W)
            if layer == 0:
                self.conformer(W, sample, t0)
            else:
                self.gdn(W, sample, t0)
            self.rmsnorm(self.gffn, layer, W)
            self.ffn(layer, W)
            self.rmsnorm(self.gple, layer, W)
            self.ple(layer, W, sample, t0)
        if S.dry:
            return
        self.colsum_sq(lambda k: (h[:, k, 0:W], ('h', k)), c.KC, W, 1.0 / c.D, RMS_EPS, self.rstd, 'rstd')
        for k in range(c.KC):
            t, tk = self.tmp('y')
            S.op('dve', (lambda e, t=t, k=k, W=W: e.scalar_tensor_tensor(out=t[:, 0:W], in0=h[:, k, 0:W],
                                                                       scalar=self.gfin[:, 0, k:k + 1], in1=self.rstd[:, 0:W],
                                                                       op0=ALU.mult, op1=ALU.mult)),
                 reads=[('h', k), 'rstd', 'const'], writes=[tk])
            dst = self.d_ysT[:, k, :] if sample else self.d_ypT[:, k, t0:t0 + W]
            S.dma('pool', f'yout{tk[1]}', (lambda e, t=t, dst=dst, W=W: e.dma_start(out=dst, in_=t[:, 0:W])),
                  reads=[tk], writes=[('d_y', sample, ti, k)])

    def build(self):
        c = self.c
        nc = self.nc
        D, F, H, KC, QC, NS, TT, SEQ = c.D, c.F, c.H, c.KC, c.QC, c.NS, c.TT, c.SEQ
        self.d_xpT = self.din("xpT", [128, KC, SEQ])
        self.d_xsT = self.din("xsT", [128, KC, NS])
        self.d_ppT = [self.din(f"ppT{l}", [128, c.PC, SEQ]) for l in range(2)]
        self.d_psT = [self.din(f"psT{l}", [128, c.PC, NS]) for l in range(2)]
        self.d_cs_in = self.din("cs_in", [128, KC, NS, CW - 1])
        self.d_qs_in = self.din("qs_in", [128, QC, NS, SC - 1])
        self.d_gs_in = self.din("gs_in", [NS, 128, H, 128])
        d_vec = {n: self.din(n, s) for n, s in dict(
            gmix=[128, 2, KC], gffn=[128, 2, KC], gple=[128, 2, KC], gfin=[128, 1, KC],
            cwdw=[128, KC, CW], cbdw=[128, KC], clng=[128, KC], clnb=[128, KC], gwc=[128, QC, SC],
            alog=[C64, H], dtb=[C64, H], gon=[128, 1],
            c_ident=[128, 128], c_tri=[C64, C64], c_mup=[C64, C64], c_mlo=[C64, C64], c_mstrict=[C64, C64]).items()}
        self.w = {}
        for n, s in dict(conf_w_pw1=[1, D, 2 * D], conf_w_pw2=[1, D, D], gdn_w_in=[1, D, c.PROJ], gdn_w_out=[1, D, D],
                         ffn_w_gate=[2, D, F], ffn_w_up=[2, D, F], ffn_w_down=[2, F, D], ple_w_gate=[2, D, D],
                         ple_w_proj=[2, c.PLE, D]).items():
            ap = self.din(n, s)
            self.w[n] = [ap[l] for l in range(s[0])]
        self.d_ypT = self.dout("ypT", [128, KC, SEQ])
        self.d_ysT = self.dout("ysT", [128, KC, NS])
        self.d_conf_p = self.dout("conf_p", [128, KC, CW - 1])
        self.d_qkv_p = self.dout("qkv_p", [128, QC, SC - 1])
        self.d_gdn_p = self.dout("gdn_p", [128, H, 128])
        self.d_conf_s = self.dout("conf_s", [128, KC, NS, CW - 1])
        self.d_qkv_s = self.dout("qkv_s", [128, QC, NS, SC - 1])
        self.d_gdn_s = self.dout("gdn_s", [NS, 128, H, 128])

        NCH = c.NCH
        with contextlib.ExitStack() as st:
            self.st = st
            S = self.S = Sched(nc, st)
            self.h = self.sb("h", [128, KC, TT])
            self.u = self.sb("u", [128, KC, TT], BF16)
            self.wb = [self.sb(f"wb{i}", [128, 32, 256], BF16) for i in range(c.NWB)]
            self.psb = [st.enter_context(nc.psum_tensor(f"psb{i}", [128, 512], F32)) for i in range(8)]
            self.tmps = [self.sb(f"tmp{i}", [128, 256]) for i in range(3)]
            self.c_a = [self.sb(f"c_a{i}", [128, 256]) for i in range(2)]
            self.c_b = [self.sb(f"c_b{i}", [128, 256]) for i in range(2)]
            self.rstd = self.sb("rstd", [128, TT])
            self.mean = self.sb("mean", [128, TT])
            self.pT = self.sb("pT", [128, c.PC, TT], BF16)
            self.scr = self.sb("scr", [128, max(c.FGMAX, KC, H), TT], BF16)
            self.wba = self.sb("wba", [128, KC, 2 * H], BF16)
            self.g_beta = self.sb("g_beta", [C64, NCH, H])
            self.g_g = self.sb("g_g", [C64, NCH, H])
            self.hq = [self.sb(f"hq{i}", [128, TT]) for i in range(2)]
            self.hk = [self.sb(f"hk{i}", [128, TT]) for i in range(2)]
            self.hv = [self.sb(f"hv{i}", [128, TT]) for i in range(2)]
            self.hz = [self.sb(f"hz{i}", [128, TT]) for i in range(2)]
            self.ones_f = self.sb("ones_f", [128, 128])
            self.ident_f = self.sb("ident_f", [128, 128])
            self.tri_f = self.sb("tri_f", [C64, C64])
            self.m_up_t = self.sb("m_up", [C64, C64])
            self.m_lo_t = self.sb("m_lo", [C64, C64])
            self.m_strict_t = self.sb("m_strict", [C64, C64])
            self.m_up, self.m_lo, self.m_strict = self.m_up_t[:, :], self.m_lo_t[:, :], self.m_strict_t[:, :]
            self.gmix = self.sb("gmix", [128, 2, KC])
            self.gffn = self.sb("gffn", [128, 2, KC])
            self.gple = self.sb("gple", [128, 2, KC])
            self.gfin = self.sb("gfin", [128, 1, KC])
            self.cwdw = self.sb("cwdw", [128, KC, CW])
            self.cbdw = self.sb("cbdw", [128, KC])
            self.clng = self.sb("clng", [128, KC])
            self.clnb = self.sb("clnb", [128, KC])
            self.gwc = self.sb("gwc", [128, QC, SC])
            self.nA = self.sb("nA", [C64, H])
            self.dtb = self.sb("dtb", [C64, H])
            self.gon = self.sb("gon", [128, 1])
            self.ps_lin_i = self.ps_aux_i = self.tmp_i = 0

            with contextlib.ExitStack() as pst:
                self.st = pst
                self.scr_glu = self.sb("glub", [128, 4, CW - 1 + TT])
                self.chalo = self.sb("chalo", [128, KC, CW - 1])
                self.qhalo = self.sb("qhalo", [128, QC, SC - 1])
                self.Sst = self.sb("Sst", [128, H, 128])
                self.cvb = [[self.sb(f"cvb{a}{b}", [128, SC - 1 + TT]) for b in range(2)] for a in range(2)]
                class _NS:
                    pass
                self.Tb = []
                for p_ in range(2):
                    T = _NS()
                    sfx = f"_{p_}"
                    T.knb = self.sb("knb" + sfx, [128, TT], BF16)
                    T.qnb = self.sb("qnb" + sfx, [128, TT], BF16)
                    T.qrb = self.sb("qrb" + sfx, [128, TT], BF16)
                    T.t_G = self.sb("t_G" + sfx, [C64, NCH])
                    T.t_alast = self.sb("t_alast" + sfx, [128, NCH])
                    T.t_kdsc = self.sb("t_kdsc" + sfx, [C64, NCH])
                    T.t_eG = self.sb("t_eG" + sfx, [C64, NCH])
                    T.t_nbe = self.sb("t_nbe" + sfx, [C64, NCH])
                    T.t_eGrow = self.sb("t_eGrow" + sfx, [128, TT])
                    T.t_X = self.sb("t_X" + sfx, [C64, NCH, C64])
                    T.t_D = self.sb("t_D" + sfx, [C64, NCH, C64])
                    T.t_DT = self.sb("t_DT" + sfx, [C64, NCH, C64])
                    T.t_attT = self.sb("t_attT" + sfx, [C64, NCH, C64], BF16)
                    T.t_N = [self.sb(f"t_N{i}" + sfx, [C64, NCH, C64]) for i in range(2)]
                    T.t_NT = [self.sb(f"t_NT{i}" + sfx, [C64, NCH, C64]) for i in range(2)]
                    T.t_PT = [self.sb(f"t_PT{i}" + sfx, [C64, NCH, C64]) for i in range(1)]
                    T.t_kd = self.sb("t_kd" + sfx, [C64, NCH, 128], BF16)
                    T.t_bv = self.sb("t_bv" + sfx, [C64, NCH, 128])
                    T.t_R = self.sb("t_R" + sfx, [C64, 128])
                    T.t_ub = self.sb("t_ub" + sfx, [C64, 128], BF16)
                    T.t_o = self.sb("t_o" + sfx, [C64, NCH, 128])
                    T.t_oss = self.sb("t_oss" + sfx, [C64, NCH])
                    T.Sbf = self.sb("Sbf" + sfx, [128, 128], BF16)
                    self.Tb.append(T)

                self.plan = []
                S.dry = True
                self.tile(0, False)
                S.dry = False
                NP = len(self.plan)
                ntiles = c.NT + (1 if NS > 0 else 0)
                self.wtotal = NP * ntiles
                self.wpos = 0
                self.wissued = 0
                self.wcache = []
                for i0 in range(0, NP, 64):
                    n = min(64, NP - i0)
                    wc = nc.dram_tensor(f"wcache{i0 // 64}", [n, 128, 32 * 256], BF16, kind="Internal").ap()
                    self.wcache.extend(wc[i] for i in range(n))

                S.op('dve', lambda e: e.memset(self.ones_f[:, :], 1.0), writes=['const'])
                lst = [(self.ident_f, 'c_ident'), (self.tri_f, 'c_tri'), (self.m_up_t, 'c_mup'),
                       (self.m_lo_t, 'c_mlo'), (self.m_strict_t, 'c_mstrict'), (self.gmix, 'gmix'),
                       (self.gffn, 'gffn'), (self.gple, 'gple'), (self.gfin, 'gfin'), (self.cwdw, 'cwdw'),
                       (self.cbdw, 'cbdw'), (self.clng, 'clng'), (self.clnb, 'clnb'), (self.gwc, 'gwc'),
                       (self.nA, 'alog'), (self.dtb, 'dtb'), (self.gon, 'gon')]
                for i, (t, n) in enumerate(lst):
                    S.dma('sp', f'cst{i % 4}', (lambda e, t=t, n=n: e.dma_start(out=t[:], in_=d_vec[n])), writes=[('cst', i)])
                S.op('act', lambda e: e.activation(out=self.nA[:, :], in_=self.nA[:, :], func=AF.Exp),
                     reads=[('cst', i) for i in range(len(lst))], writes=['const'])
                S.op('dve', lambda e: e.tensor_scalar(out=self.nA[:, :], in0=self.nA[:, :], scalar1=-1.0, scalar2=None,
                                                      op0=ALU.mult), reads=['const'], writes=['const'])
                S.op('dve', lambda e: e.memset(self.chalo[:, :, :], 0.0), writes=[('chalo', k) for k in range(KC)])
                S.op('dve', lambda e: e.memset(self.qhalo[:, :, :], 0.0), writes=[('qhalo', k) for k in range(QC)])
                S.op('dve', lambda e: e.memset(self.Sst[:, :, :], 0.0), writes=[('S', k) for k in range(H)])

                for ti in range(c.NT):
                    self.tile(ti, False)
                S.dma('pool', 'po0', (lambda e: e.dma_start(out=self.d_conf_p, in_=self.chalo[:, :, :])),
                      reads=[('chalo', k) for k in range(KC)], writes=['d_conf_p'])
                S.dma('pool', 'po1', (lambda e: e.dma_start(out=self.d_qkv_p, in_=self.qhalo[:, :, :])),
                      reads=[('qhalo', k) for k in range(QC)], writes=['d_qkv_p'])
                S.dma('pool', 'po2', (lambda e: e.dma_start(out=self.d_gdn_p, in_=self.Sst[:, :, :])),
                      reads=[('S', k) for k in range(H)], writes=['d_gdn_p'])
                for e_ in Sched.ENG:
                    S.wait_all(e_)
                S.emit()

            if NS > 0:
                with contextlib.ExitStack() as sst:
                    self.st = sst
                    self.cs_st = [self.sb(f"cs_st{i}", [128, NS, CW - 1]) for i in range(1)]
                    self.cs_new = [self.sb(f"cs_new{i}", [128, NS, CW - 1]) for i in range(1)]
                    self.cs_prod = [self.sb(f"cs_prod{i}", [128, NS, CW - 1]) for i in range(1)]
                    self.c_c = [self.sb(f"c_c{i}", [128, NS]) for i in range(1)]
                    self.qs_st = [self.sb(f"qs_st{i}", [128, 3, 2, NS, SC - 1]) for i in range(1)]
                    self.qs_new = [self.sb(f"qs_new{i}", [128, 3, 2, NS, SC - 1]) for i in range(1)]
                    self.qs_prod = self.sb("qs_prod", [128, NS, SC - 1])
                    self.s_a = self.sb("s_a", [NS, H])
                    self.s_bd = self.sb("s_bd", [NS, 2, H])
                    self.s_ab = self.sb("s_ab", [128, 2, H])
                    self.s_S = self.sb("s_S", [128, H, 128])
                    self.s_q = self.sb("s_q", [128, H, NS])
                    self.s_k = self.sb("s_k", [128, H, NS])
                    self.s_v = self.sb("s_v", [128, H, NS])
                    self.s_z = self.sb("s_z", [128, H, NS])
                    self.s_o = self.sb("s_o", [128, H, NS])
                    self.s_r = self.sb("s_r", [128, H])
                    self.s_rows = self.sb("s_rows", [H, 256])
                    self.s_rbd = [self.sb(f"s_rbd{i}", [H, 2, 128]) for i in range(2)]
                    self.tile(0, True)
                    assert self.wpos == self.wtotal, (self.wpos, self.wtotal)
                    S.wait_all('sp')
                    S.emit()
        return nc


def _fm(x):
    T, C = x.shape
    return np.ascontiguousarray(x.reshape(T, C // 128, 128).transpose(2, 1, 0))


def _fm_inv(y):
    p, kc, T = y.shape
    return np.ascontiguousarray(y.transpose(2, 1, 0).reshape(T, kc * 128))


def _vec(v):
    lead = v.shape[:-1]
    C = v.shape[-1]
    r = v.reshape(lead + (C // 128, 128))
    return np.ascontiguousarray(np.moveaxis(r, -1, 0))


def make_consts():
    p = np.arange(C64)[:, None]
    x = np.arange(C64)[None, :]
    return dict(
        c_ident=np.eye(128, dtype=np.float32),
        c_tri=(p <= x).astype(np.float32),
        c_mup=np.where(x > p, NEG, 0.0).astype(np.float32),
        c_mlo=np.where(x < p, NEG, 0.0).astype(np.float32),
        c_mstrict=(x < p).astype(np.float32),
    )


def run(cfg, inp, nseq_cores=None):
    c = cfg
    NC = c.NCORES
    B = inp['x_prompt'].shape[0]
    NS = c.NS
    prog = Prog(c)
    nc = prog.build()
    f = lambda a: np.ascontiguousarray(np.asarray(a, dtype=np.float32))
    shared = dict(
        gmix=_vec(f(inp['g_mix'])), gffn=_vec(f(inp['g_ffn'])), gple=_vec(f(inp['g_ple'])),
        gfin=_vec(f(inp['g_final'])[None]),
        cwdw=np.ascontiguousarray(_vec(f(inp['conf_w_dw'][0])).transpose(0, 2, 1)),
        cbdw=_vec(f(inp['conf_b_dw'][0])), clng=_vec(f(inp['conf_ln_g'][0])), clnb=_vec(f(inp['conf_ln_b'][0])),
        gwc=np.ascontiguousarray(_vec(f(inp['gdn_w_conv'][0])).transpose(0, 2, 1)),
        alog=np.ascontiguousarray(np.broadcast_to(f(inp['gdn_a_log'][0])[None, :], (C64, c.H))),
        dtb=np.ascontiguousarray(np.broadcast_to(f(inp['gdn_dt_bias'][0])[None, :], (C64, c.H))),
        gon=np.ascontiguousarray(f(inp['gdn_g_onorm'][0])[:, None]),
    )
    shared.update(make_consts())
    for n in ('conf_w_pw1', 'conf_w_pw2', 'gdn_w_in', 'gdn_w_out', 'ffn_w_gate', 'ffn_w_up', 'ffn_w_down', 'ple_w_gate',
              'ple_w_proj'):
        shared[n] = f(inp[n])
    xp, xs = f(inp['x_prompt']), f(inp['x_sample'])
    pp, psm = f(inp['p_prompt']), f(inp['p_sample'])
    scc, scq, sg = f(inp['state_conv_conformer']), f(inp['state_conv_qkv']), f(inp['state_gdn'])
    in_maps = []
    ACT = c.ACTIVE
    assert len(ACT) >= B and NS * len(ACT) == xs.shape[0]
    zero_map = None
    for ci in range(NC):
        if ci not in ACT:
            if zero_map is None:
                zero_map = {k: np.zeros_like(v) for k, v in in_maps[0].items()}
            in_maps.append(zero_map)
            continue
        a = ACT.index(ci)
        sq = a % B
        sl = slice(a * NS, (a + 1) * NS)
        m = dict(shared)
        m['xpT'] = _fm(xp[sq])
        m['xsT'] = _fm(xs[sl, 0])
        for l in range(2):
            m[f'ppT{l}'] = _fm(pp[l, sq])
            m[f'psT{l}'] = _fm(psm[l, sl, 0])
        m['cs_in'] = np.ascontiguousarray(scc[0, sl].reshape(NS, CW - 1, c.KC, 128).transpose(3, 2, 0, 1))
        m['qs_in'] = np.ascontiguousarray(scq[0, sl].reshape(NS, SC - 1, c.QC, 128).transpose(3, 2, 0, 1))
        m['gs_in'] = np.ascontiguousarray(sg[0, sl].transpose(0, 2, 1, 3))
        in_maps.append(m)
    res = run_bass_kernel_spmd(nc, in_maps, core_ids=list(range(NC)))
    R = [res.results[ci] for ci in c.ACTIVE]
    NC = len(c.ACTIVE)
    D = c.D
    y_p = np.stack([_fm_inv(R[b]['ypT']) for b in range(B)])
    y_s = np.concatenate([_fm_inv(R[ci]['ysT']) for ci in range(NC)])[:, None, :]
    conf_p = np.stack([_fm_inv(R[b]['conf_p']) for b in range(B)])[None]
    qkv_p = np.stack([_fm_inv(R[b]['qkv_p']) for b in range(B)])[None]
    gdn_p = np.stack([np.ascontiguousarray(R[b]['gdn_p'].transpose(1, 0, 2)) for b in range(B)])[None]
    conf_s = np.concatenate([np.ascontiguousarray(R[ci]['conf_s'].transpose(2, 3, 1, 0)).reshape(NS, CW - 1, D)
                             for ci in range(NC)])[None]
    qkv_s = np.concatenate([np.ascontiguousarray(R[ci]['qkv_s'].transpose(2, 3, 1, 0)).reshape(NS, SC - 1, c.QKV)
                            for ci in range(NC)])[None]
    gdn_s = np.concatenate([np.ascontiguousarray(R[ci]['gdn_s'].transpose(0, 2, 1, 3)) for ci in range(NC)])[None]
    return (y_p, y_s, conf_p, qkv_p, gdn_p, conf_s, qkv_s, gdn_s)


def kernel(**inputs):
    cfg = Cfg(NS=32, ACTIVE=(0, 1, 4, 5))
    return run(cfg, inputs)
```

```python
import contextlib
import numpy as np
import concourse.bass as bass
import concourse.mybir as mybir
from concourse.bass_utils import run_bass_kernel_spmd

F32 = mybir.dt.float32
BF16 = mybir.dt.bfloat16
AF = mybir.ActivationFunctionType
ALU = mybir.AluOpType
AX = mybir.AxisListType

RMS_EPS = 1e-6
LN_EPS = 1e-5
L2_EPS = 1e-6
CW = 31
SC = 4
C64 = 64
NEG = 30000.0


class Cfg:
    def __init__(self, D=4096, F=11008, H=32, PLE=256, SEQ=2048, NS=16, TT=256, NCORES=8, NWB=3, ACTIVE=None):
        self.D, self.F, self.H, self.PLE, self.SEQ, self.NS, self.TT = D, F, H, PLE, SEQ, NS, TT
        self.NCORES, self.NWB = NCORES, NWB
        self.ACTIVE = list(ACTIVE) if ACTIVE is not None else list(range(NCORES))
        self.KC = D // 128
        self.FC = F // 128
        self.QC = 3 * H
        self.QKV = H * 384
        self.PROJ = self.QKV + H * 128 + 2 * H
        self.PC = PLE // 128
        self.NT = SEQ // TT
        self.NCH = TT // C64
        ng = -(-self.FC // 32)
        base = self.FC // ng
        self.FG = []
        s = 0
        for i in range(ng):
            n = base + (1 if i < self.FC - base * ng else 0)
            self.FG.append((s, n))
            s += n
        self.FGMAX = max(n for _, n in self.FG)


class Sched:
    ENG = ('pe', 'act', 'dve', 'pool', 'sp')

    def __init__(self, nc, stack):
        self.nc = nc
        self.stack = stack
        self.eng = dict(pe=nc.tensor, act=nc.scalar, dve=nc.vector, pool=nc.gpsimd, sp=nc.sync)
        self.prog = {e: [] for e in self.ENG}
        self.psem = {e: stack.enter_context(nc.semaphore(f"prog_{e}")) for e in self.ENG if e != 'sp'}
        self.pcnt = {e: 0 for e in self.ENG}
        self.seen = {e: {} for e in self.ENG}
        self.last_w = {}
        self.readers = {}
        self.dsem = {}
        self.dcnt = {}
        self.dry = False
        self.ns = None
        self.cap = None

    def _deps(self, e, reads, writes):
        best = {}
        for k in reads:
            t = self.last_w.get(k)
            if t is not None and best.get(t[0], 0) < t[1]:
                best[t[0]] = t[1]
        for k in writes:
            t = self.last_w.get(k)
            if t is not None and best.get(t[0], 0) < t[1]:
                best[t[0]] = t[1]
            for t in self.readers.get(k, ()):
                if best.get(t[0], 0) < t[1]:
                    best[t[0]] = t[1]
        seen = self.seen[e]
        for s, v in best.items():
            if seen.get(s, 0) < v:
                seen[s] = v
                self.prog[e].append(('w', s, v))

    def _commit(self, tok, reads, writes):
        for k in writes:
            self.last_w[k] = tok
            self.readers[k] = []
        for k in reads:
            if k in writes:
                continue
            self.readers.setdefault(k, []).append(tok)

    PRIV = ('knb', 'qnb', 'qrb', 'Sbf')

    def _nsk(self, k):
        if isinstance(k, str):
            return (k, 'ns', self.ns) if (k.startswith('t_') or k in self.PRIV) else k
        if isinstance(k, tuple) and isinstance(k[0], str) and k[0].startswith('t_'):
            return k + ('ns', self.ns)
        return k

    def op(self, e, fn, reads=(), writes=()):
        if self.dry:
            return
        if self.ns is not None:
            reads = [self._nsk(k) for k in reads]
            writes = [self._nsk(k) for k in writes]
        if self.cap is not None:
            self.cap.append(('op', e, fn, reads, writes))
            return
        px = [k for k in reads if isinstance(k, tuple) and k[0] == 'ps']
        if px:
            reads = [k for k in reads if not (isinstance(k, tuple) and k[0] == 'ps')]
            writes = list(writes) + [k for k in px if k not in writes]
        self._deps(e, reads, writes)
        self.pcnt[e] += 1
        tok = (('p', e), self.pcnt[e])
        self.prog[e].append(('o', fn, ('p', e)))
        self._commit(tok, reads, writes)

    def dma(self, e, sem_name, fn, reads=(), writes=()):
        if self.dry:
            return
        if self.cap is not None:
            self.cap.append(('dma', e, sem_name, fn, reads, writes))
            return
        if sem_name not in self.dsem:
            self.dsem[sem_name] = self.stack.enter_context(self.nc.semaphore(f"dma_{sem_name}"))
            self.dcnt[sem_name] = 0
        self._deps(e, reads, writes)
        self.dcnt[sem_name] += 16
        tok = (('d', sem_name), self.dcnt[sem_name])
        self.prog[e].append(('d', fn, ('d', sem_name)))
        self._commit(tok, reads, writes)

    def replay_zipped(self, caps):
        for i in range(max(len(x) for x in caps)):
            for x in caps:
                if i < len(x):
                    it = x[i]
                    if it[0] == 'op':
                        self.op(*it[1:])
                    else:
                        self.dma(*it[1:])

    def wait_all(self, e):
        allt = [(('p', x), self.pcnt[x]) for x in self.psem] + [(('d', n), c) for n, c in self.dcnt.items()]
        for s, c in allt:
            if c > 0 and self.seen[e].get(s, 0) < c:
                self.seen[e][s] = c
                self.prog[e].append(('w', s, c))

    def _sem(self, s):
        return self.psem[s[1]] if s[0] == 'p' else self.dsem[s[1]]

    def emit(self):
        nc = self.nc

        def run(e):
            eng = self.eng[e]
            for it in self.prog[e]:
                if it[0] == 'w':
                    eng.wait_ge(self._sem(it[1]), it[2])
                elif it[0] == 'o':
                    it[1](eng).then_inc(self._sem(it[2]), 1)
                else:
                    it[1](eng).then_inc(self._sem(it[2]), 16)

        with nc.Block() as block:
            @block.tensor
            def _(eng):
                run('pe')

            @block.scalar
            def _(eng):
                run('act')

            @block.vector
            def _(eng):
                run('dve')

            @block.gpsimd
            def _(eng):
                run('pool')

            @block.sync
            def _(eng):
                run('sp')
        self.prog = {e: [] for e in self.ENG}


class Prog:
    def __init__(self, cfg):
        self.c = cfg
        self.nc = bass.Bass("TRN2", target_bir_lowering=False)
        self.uid = 0
        self.aux_set = None

    def sb(self, name, shape, dt=F32):
        return self.st.enter_context(self.nc.sbuf_tensor("sb_" + name, list(shape), dt))

    def din(self, name, shape, dt=F32):
        return self.nc.dram_tensor(name, list(shape), dt, kind="ExternalInput").ap()

    def dout(self, name, shape, dt=F32):
        return self.nc.dram_tensor(name, list(shape), dt, kind="ExternalOutput").ap()

    def ps(self, kind):
        if kind == 'lin':
            r = self.ps_lin_i % 4
            self.ps_lin_i += 1
            return self.psb[r][:, 0:256], ('ps', r)
        if self.aux_set is not None:
            st_ = self.aux_set
            r = st_[self.ps_aux_i % len(st_)]
        else:
            r = self.ps_aux_i % 4
        self.ps_aux_i += 1
        return self.psb[4 + r][:, 0:256], ('ps', 4 + r)

    def tmp(self, kind):
        n = len(self.tmps)
        i = self.tmp_i % n
        self.tmp_i += 1
        return self.tmps[i], ('tmp', i)

    def wget(self, src, nk, ncols, tag, hold=0):
        c = self.c
        if self.S.dry:
            self.plan.append((src, nk, ncols, tag))
            return None, None
        i = self.wpos
        self.wpos += 1
        assert self.plan[i % len(self.plan)][3] == tag, (self.plan[i % len(self.plan)][3], tag)
        self._wissue(min(i - hold + c.NWB - 1, self.wtotal - 1))
        slot = i % c.NWB
        return self.wb[slot], ('wb', slot)

    def _wissue(self, upto):
        c = self.c
        S = self.S
        NP = len(self.plan)
        while self.wissued <= upto:
            g = self.wissued
            self.wissued += 1
            src, nk, ncols, tag = self.plan[g % NP]
            slot = g % c.NWB
            pid = g % NP
            dst = self.wb[slot][:, 0:nk, 0:ncols]
            cache = self.wcache[pid]
            cview = cache[:, 0:nk * ncols].rearrange("p (k n) -> p k n", k=nk)
            if g < NP:
                srcv = src.rearrange("(kc p) n -> p kc n", p=128)
                S.dma('pool', f'wl{slot}', (lambda e, dst=dst, srcv=srcv: e.dma_start(out=dst, in_=srcv)),
                      writes=[('wb', slot)])
                if self.wtotal > NP:
                    S.dma('sp', f'wc{slot}', (lambda e, dst=dst, cview=cview: e.dma_start(out=cview, in_=dst)),
                          reads=[('wb', slot)], writes=[('wcache', pid)])
            else:
                S.dma('sp', f'wh{slot}', (lambda e, dst=dst, cview=cview: e.dma_start(out=dst, in_=cview)),
                      reads=[('wcache', pid)], writes=[('wb', slot)])

    def mm_acc(self, psap, pskey, parts, W, extra_reads=()):
        S = self.S
        if S.dry:
            return
        seq = []
        reads = list(extra_reads)
        for (wt, wk, kcs, moff, msz, it, ikcs, ikeys) in parts:
            for a, b in zip(kcs, ikcs):
                seq.append((wt[:, a, moff:moff + msz], it[:, b, 0:W]))
            reads.append(wk)
            reads.extend(ikeys)
        n = len(seq)

        def fn(e, seq=seq, psap=psap, W=W, n=n):
            ins = None
            for i, (l, r) in enumerate(seq):
                ins = e.matmul(psap[0:l.shape[-1], 0:W], lhsT=l, rhs=r, start=(i == 0), stop=(i == n - 1))
            return ins
        S.op('pe', fn, reads=reads, writes=[pskey])

    def linear(self, Wsrc, K_chunks, col0, ncols_total, in_tile, in_key_fn, W, cb, tag, k0=0):
        npan = -(-ncols_total // 256)
        for pi in range(npan):
            c0 = col0 + pi * 256
            ncol = min(256, col0 + ncols_total - c0)
            subs = []
            kk = 0
            while kk < K_chunks:
                nk = min(32, K_chunks - kk)
                src = Wsrc[(k0 + kk) * 128:(k0 + kk + nk) * 128, c0:c0 + ncol]
                wt, wk = self.wget(src, nk, ncol, (tag, pi, kk), hold=len(subs))
                subs.append((wt, wk, kk, nk))
                kk += nk
            for mi in range(ncol // 128):
                if self.S.dry:
                    continue
                psap, pskey = self.ps('lin')
                parts = []
                for (wt, wk, kk, nk) in subs:
                    parts.append((wt, wk, list(range(nk)), mi * 128, 128, in_tile,
                                  list(range(kk, kk + nk)), [in_key_fn(k) for k in range(kk, kk + nk)]))
                self.mm_acc(psap, pskey, parts, W)
                cb(pi * 2 + mi, psap, pskey)

    def colsum_sq(self, src_fn, nchunks, W, scale_inv, eps, out_rstd, out_key):
        S = self.S
        if S.dry:
            return
        psap, pskey = self.ps('aux')
        for k in range(nchunks):
            sap, skey = src_fn(k)
            t, tk = self.tmp('sq')
            S.op('act', (lambda e, t=t, sap=sap, W=W: e.activation(out=t[:, 0:W], in_=sap, func=AF.Square)),
                 reads=[skey], writes=[tk])
            S.op('pe', (lambda e, t=t, psap=psap, W=W, k=k, n=nchunks: e.matmul(
                psap[:, 0:W], lhsT=self.ones_f[:, :], rhs=t[:, 0:W], start=(k == 0), stop=(k == n - 1))),
                 reads=[tk, 'const'], writes=[pskey])
        S.op('dve', (lambda e, psap=psap, W=W: e.tensor_scalar(out=out_rstd[:, 0:W], in0=psap[:, 0:W], scalar1=scale_inv,
                                                           scalar2=eps, op0=ALU.mult, op1=ALU.add)),
             reads=[pskey], writes=[out_key])
        self.rsqrt_(out_rstd[:, 0:W], out_key)

    def rsqrt_(self, ap, key):
        S = self.S
        S.op('act', (lambda e, ap=ap: e.activation(out=ap, in_=ap, func=AF.Sqrt)), reads=[key], writes=[key])
        S.op('dve', (lambda e, ap=ap: e.reciprocal(out=ap, in_=ap)), reads=[key], writes=[key])

    def rmsnorm(self, gtile, gl, W):
        c = self.c
        S = self.S
        if S.dry:
            return
        h, u = self.h, self.u
        self.colsum_sq(lambda k: (h[:, k, 0:W], ('h', k)), c.KC, W, 1.0 / c.D, RMS_EPS, self.rstd, 'rstd')
        for k in range(c.KC):
            S.op('dve', (lambda e, k=k, W=W: e.scalar_tensor_tensor(
                out=u[:, k, 0:W], in0=h[:, k, 0:W], scalar=gtile[:, gl, k:k + 1], in1=self.rstd[:, 0:W],
                op0=ALU.mult, op1=ALU.mult)), reads=[('h', k), 'rstd', 'const'], writes=[('u', k)])

    def conformer(self, W, sample, t0):
        c = self.c
        S = self.S
        KC = c.KC
        h, u = self.h, self.u
        Wp1 = self.w['conf_w_pw1'][0]
        Wp2 = self.w['conf_w_pw2'][0]
        cbuf = self.scr
        glub = self.scr_glu
        ps_mean = ps_var = None
        if not S.dry:
            ps_mean, km = self.ps('aux')
            ps_var, kv = self.ps('aux')
        if sample and not S.dry:
            pass
        pending_tail = None
        for j in range(KC // 2):
            a_src = Wp1[:, j * 256:(j + 1) * 256]
            g_src = Wp1[:, c.D + j * 256:c.D + (j + 1) * 256]
            wa, wak = self.wget(a_src, KC, 256, ('pw1a', j))
            wg, wgk = self.wget(g_src, KC, 256, ('pw1g', j), hold=1)
            if S.dry:
                continue
            caps = []
            splits = []
            for mi in range(2):
                if not sample:
                    S.cap = []
                    caps.append(S.cap)
                bi = 0 if sample else mi
                ch = 2 * j + mi
                gi = ch % 4
                pa, pak = self.ps('lin')
                pg, pgk = self.ps('lin')
                ukeys = [('u', k) for k in range(KC)]
                self.mm_acc(pa, pak, [(wa, wak, list(range(KC)), mi * 128, 128, u, list(range(KC)), ukeys)], W)
                self.mm_acc(pg, pgk, [(wg, wgk, list(range(KC)), mi * 128, 128, u, list(range(KC)), ukeys)], W)
                sg, sgk = self.c_a[bi], ('c_a', bi)
                S.op('act', (lambda e, sg=sg, pg=pg, W=W: e.activation(out=sg[:, 0:W], in_=pg[:, 0:W], func=AF.Sigmoid)),
                     reads=[pgk], writes=[sgk])
                acc, acck = self.c_b[bi], ('c_b', bi)
                if not sample:
                    gk = ('glu', gi)
                    S.op('act', (lambda e, gi=gi, ch=ch: e.activation(out=glub[:, gi, 0:30], in_=self.chalo[:, ch, :],
                                                                     func=AF.Copy)),
                         reads=[('chalo', ch)], writes=[gk])
                    S.op('dve', (lambda e, gi=gi, pa=pa, sg=sg, W=W: e.tensor_tensor(
                        out=glub[:, gi, 30:30 + W], in0=pa[:, 0:W], in1=sg[:, 0:W], op=ALU.mult)),
                         reads=[pak, sgk, gk], writes=[gk])
                    S.op('act', (lambda e, gi=gi, ch=ch, W=W: e.activation(out=self.chalo[:, ch, :],
                                                                          in_=glub[:, gi, W:W + 30], func=AF.Copy)),
                         reads=[gk], writes=[('chalo', ch)])
                    splits.append(len(S.cap))
                    for w in range(CW):
                        if w == 0:
                            S.op('dve', (lambda e, acc=acc, gi=gi, ch=ch, W=W: e.tensor_scalar(
                                out=acc[:, 0:W], in0=glub[:, gi, 0:W], scalar1=self.cwdw[:, ch, 0:1],
                                scalar2=self.cbdw[:, ch:ch + 1], op0=ALU.mult, op1=ALU.add)),
                                 reads=[gk, 'const'], writes=[acck])
                        else:
                            S.op('dve', (lambda e, acc=acc, gi=gi, ch=ch, W=W, w=w: e.scalar_tensor_tensor(
                                out=acc[:, 0:W], in0=glub[:, gi, w:w + W], scalar=self.cwdw[:, ch, w:w + 1],
                                in1=acc[:, 0:W], op0=ALU.mult, op1=ALU.add)),
                                 reads=[gk, acck, 'const'], writes=[acck])
                else:
                    NS = c.NS
                    st, stk = self.cs_st[0], ('csst', 0)
                    nst, nstk = self.cs_new[0], ('csnew', 0)
                    S.dma('pool', 'csin', (lambda e, st=st, ch=ch: e.dma_start(out=st[:, :, :], in_=self.d_cs_in[:, ch, :, :])),
                          writes=[stk])
                    gl, glk = self.c_c[bi], ('c_c', bi)
                    S.op('dve', (lambda e, gl=gl, pa=pa, sg=sg, W=W: e.tensor_tensor(
                        out=gl[:, 0:W], in0=pa[:, 0:W], in1=sg[:, 0:W], op=ALU.mult)),
                         reads=[pak, sgk], writes=[glk])
                    S.op('act', (lambda e, st=st, nst=nst: e.activation(out=nst[:, :, 0:29], in_=st[:, :, 1:30], func=AF.Copy)),
                         reads=[stk], writes=[nstk])
                    S.op('act', (lambda e, gl=gl, nst=nst, W=W: e.activation(out=nst[:, :, 29], in_=gl[:, 0:W], func=AF.Copy)),
                         reads=[glk, nstk], writes=[nstk])
                    S.dma('pool', 'csout', (lambda e, nst=nst, ch=ch: e.dma_start(out=self.d_conf_s[:, ch, :, :], in_=nst[:, :, :])),
                          reads=[nstk], writes=[('d_conf_s', ch)])
                    pr, prk = self.cs_prod[0], ('csprod', 0)
                    S.op('dve', (lambda e, st=st, ch=ch, pr=pr, NS=NS: e.tensor_tensor(
                        out=pr[:, :, :], in0=st[:, :, :], in1=self.cwdw[:, ch, 0:30].unsqueeze(1).to_broadcast([128, NS, 30]),
                        op=ALU.mult)), reads=[stk, 'const'], writes=[prk])
                    S.op('dve', (lambda e, acc=acc, pr=pr, W=W: e.tensor_reduce(out=acc[:, 0:W], in_=pr[:, :, :], axis=AX.X, op=ALU.add)),
                         reads=[prk], writes=[acck])
                    S.op('dve', (lambda e, acc=acc, gl=gl, ch=ch, W=W: e.scalar_tensor_tensor(
                        out=acc[:, 0:W], in0=gl[:, 0:W], scalar=self.cwdw[:, ch, 30:31], in1=acc[:, 0:W],
                        op0=ALU.mult, op1=ALU.add)), reads=[glk, acck, 'const'], writes=[acck])
                    S.op('dve', (lambda e, acc=acc, ch=ch, W=W: e.tensor_scalar(
                        out=acc[:, 0:W], in0=acc[:, 0:W], scalar1=self.cbdw[:, ch:ch + 1], scalar2=None, op0=ALU.add)),
                         reads=[acck, 'const'], writes=[acck])
                S.op('act', (lambda e, acc=acc, W=W, ch=ch: e.activation(out=cbuf[:, ch, 0:W], in_=acc[:, 0:W], func=AF.Copy)),
                     reads=[acck], writes=[('scr', ch)])
                S.cap = None
            if caps:
                S.replay_zipped([cp[:sp] for cp, sp in zip(caps, splits)])
                if pending_tail is not None:
                    S.replay_zipped(pending_tail)
                pending_tail = [cp[sp:] for cp, sp in zip(caps, splits)]
        if pending_tail is not None:
            S.replay_zipped(pending_tail)
            pending_tail = None
        if not S.dry:
            for ch in range(KC):
                t, tk = self.tmp('lnm')
                S.op('act', (lambda e, t=t, ch=ch, W=W: e.activation(out=t[:, 0:W], in_=cbuf[:, ch, 0:W], func=AF.Copy)),
                     reads=[('scr', ch)], writes=[tk])
                S.op('pe', (lambda e, t=t, W=W, ch=ch: e.matmul(ps_mean[:, 0:W], lhsT=self.ones_f[:, :], rhs=t[:, 0:W],
                                                             start=(ch == 0), stop=(ch == KC - 1))),
                     reads=[tk, 'const'], writes=[km])
                t2, t2k = self.tmp('lnv')
                S.op('act', (lambda e, t2=t2, ch=ch, W=W: e.activation(out=t2[:, 0:W], in_=cbuf[:, ch, 0:W], func=AF.Square)),
                     reads=[('scr', ch)], writes=[t2k])
                S.op('pe', (lambda e, t2=t2, W=W, ch=ch: e.matmul(ps_var[:, 0:W], lhsT=self.ones_f[:, :], rhs=t2[:, 0:W],
                                                               start=(ch == 0), stop=(ch == KC - 1))),
                     reads=[t2k, 'const'], writes=[kv])
        if not S.dry:
            mean, var = self.mean, self.rstd
            invD = 1.0 / c.D
            S.op('dve', (lambda e, W=W: e.tensor_scalar(out=mean[:, 0:W], in0=ps_mean[:, 0:W], scalar1=invD, scalar2=None,
                                                      op0=ALU.mult)), reads=[km], writes=['mean'])
            msq, msqk = self.tmp('msq')
            S.op('dve', (lambda e, W=W, msq=msq: e.tensor_tensor(out=msq[:, 0:W], in0=mean[:, 0:W], in1=mean[:, 0:W], op=ALU.mult)),
                 reads=['mean'], writes=[msqk])
            S.op('dve', (lambda e, W=W, msq=msq: e.scalar_tensor_tensor(out=var[:, 0:W], in0=ps_var[:, 0:W], scalar=invD,
                                                                      in1=msq[:, 0:W], op0=ALU.mult, op1=ALU.subtract)),
                 reads=[kv, msqk], writes=['rstd'])
            S.op('dve', (lambda e, W=W: e.tensor_scalar(out=var[:, 0:W], in0=var[:, 0:W], scalar1=LN_EPS, scalar2=None,
                                                      op0=ALU.add)), reads=['rstd'], writes=['rstd'])
            self.rsqrt_(var[:, 0:W], 'rstd')
            for ch in range(KC):
                t1, t1k = self.tmp('ln1')
                S.op('dve', (lambda e, t1=t1, ch=ch, W=W: e.tensor_tensor(out=t1[:, 0:W], in0=cbuf[:, ch, 0:W], in1=mean[:, 0:W],
                                                                        op=ALU.subtract)),
                     reads=[('scr', ch), 'mean'], writes=[t1k])
                S.op('dve', (lambda e, t1=t1, W=W: e.tensor_tensor(out=t1[:, 0:W], in0=t1[:, 0:W], in1=var[:, 0:W], op=ALU.mult)),
                     reads=[t1k, 'rstd'], writes=[t1k])
                S.op('act', (lambda e, t1=t1, ch=ch, W=W: e.activation(out=u[:, ch, 0:W], in_=t1[:, 0:W], func=AF.Silu,
                                                                     bias=self.clnb[:, ch:ch + 1], scale=self.clng[:, ch:ch + 1])),
                     reads=[t1k, 'const'], writes=[('u', ch)])

        def cb(mi, psap, pskey):
            S.op('dve', (lambda e, mi=mi, psap=psap, W=W: e.tensor_tensor(out=h[:, mi, 0:W], in0=h[:, mi, 0:W], in1=psap[:, 0:W],
                                                                        op=ALU.add)),
                 reads=[pskey, ('h', mi)], writes=[('h', mi)])
        self.linear(Wp2, KC, 0, c.D, u, lambda k: ('u', k), W, cb, 'pw2')

    def ffn(self, layer, W):
        c = self.c
        S = self.S
        KC = c.KC
        h, u = self.h, self.u
        Wg = self.w['ffn_w_gate'][layer]
        Wu = self.w['ffn_w_up'][layer]
        Wd = self.w['ffn_w_down'][layer]
        hid = self.scr
        ukeys = [('u', k) for k in range(KC)]
        for (f0, fn_) in c.FG:
            j = 0
            while j < fn_:
                ncol = min(2, fn_ - j) * 128
                c0 = (f0 + j) * 128
                wg, wgk = self.wget(Wg[:, c0:c0 + ncol], KC, ncol, ('ffg', layer, f0 + j))
                wu, wuk = self.wget(Wu[:, c0:c0 + ncol], KC, ncol, ('ffu', layer, f0 + j), hold=1)
                if not S.dry:
                    for mi in range(ncol // 128):
                        jj = j + mi
                        pg, pgk = self.ps('lin')
                        pu, puk = self.ps('lin')
                        self.mm_acc(pg, pgk, [(wg, wgk, list(range(KC)), mi * 128, 128, u, list(range(KC)), ukeys)], W)
                        self.mm_acc(pu, puk, [(wu, wuk, list(range(KC)), mi * 128, 128, u, list(range(KC)), ukeys)], W)
                        sg, sgk = self.tmp('silu')
                        S.op('act', (lambda e, sg=sg, pg=pg, W=W: e.activation(out=sg[:, 0:W], in_=pg[:, 0:W], func=AF.Silu)),
                             reads=[pgk], writes=[sgk])
                        S.op('dve', (lambda e, sg=sg, pu=pu, jj=jj, W=W: e.tensor_tensor(out=hid[:, jj, 0:W], in0=pu[:, 0:W],
                                                                                     in1=sg[:, 0:W], op=ALU.mult)),
                             reads=[puk, sgk], writes=[('scr', jj)])
                j += 2

            def cb(mi, psap, pskey):
                S.op('dve', (lambda e, mi=mi, psap=psap, W=W: e.tensor_tensor(out=h[:, mi, 0:W], in0=h[:, mi, 0:W],
                                                                            in1=psap[:, 0:W], op=ALU.add)),
                     reads=[pskey, ('h', mi)], writes=[('h', mi)])
            self.linear(Wd, fn_, 0, c.D, hid, lambda k: ('scr', k), W, cb, ('ffd', layer, f0), k0=f0)

    def ple(self, layer, W, sample, t0):
        c = self.c
        S = self.S
        h, u = self.h, self.u
        Wpg = self.w['ple_w_gate'][layer]
        Wpp = self.w['ple_w_proj'][layer]
        pT = self.pT
        if not S.dry:
            src = (self.d_psT[layer] if sample else self.d_ppT[layer][:, :, t0:t0 + W])
            S.dma('pool', 'pin', (lambda e, src=src, W=W: e.dma_start(out=pT[:, :, 0:W], in_=src)), writes=['pT'])
        pkeys = ['pT'] * c.PC
        for j in range(c.KC // 2):
            wg, wgk = self.wget(Wpg[:, j * 256:(j + 1) * 256], c.KC, 256, ('pleg', layer, j))
            wp, wpk = self.wget(Wpp[:, j * 256:(j + 1) * 256], c.PC, 256, ('plep', layer, j), hold=1)
            if S.dry:
                continue
            for mi in range(2):
                ch = 2 * j + mi
                pg, pgk = self.ps('lin')
                pp, ppk = self.ps('lin')
                self.mm_acc(pg, pgk, [(wg, wgk, list(range(c.KC)), mi * 128, 128, u, list(range(c.KC)),
                                       [('u', k) for k in range(c.KC)])], W)
                self.mm_acc(pp, ppk, [(wp, wpk, list(range(c.PC)), mi * 128, 128, pT, list(range(c.PC)), pkeys)], W)
                sg, sgk = self.tmp('sig')
                S.op('act', (lambda e, sg=sg, pg=pg, W=W: e.activation(out=sg[:, 0:W], in_=pg[:, 0:W], func=AF.Sigmoid)),
                     reads=[pgk], writes=[sgk])
                S.op('dve', (lambda e, sg=sg, pp=pp, W=W: e.tensor_tensor(out=sg[:, 0:W], in0=pp[:, 0:W], in1=sg[:, 0:W],
                                                                        op=ALU.mult)), reads=[ppk, sgk], writes=[sgk])
                S.op('dve', (lambda e, sg=sg, ch=ch, W=W: e.tensor_tensor(out=h[:, ch, 0:W], in0=h[:, ch, 0:W], in1=sg[:, 0:W],
                                                                        op=ALU.add)),
                     reads=[sgk, ('h', ch)], writes=[('h', ch)])

    def gdn_gates(self, W, ntok_blocks):
        c = self.c
        S = self.S
        H = c.H
        Win = self.w['gdn_w_in'][0]
        cb0 = c.QKV + H * 128
        if not S.dry:
            S.dma('pool', 'wba', (lambda e: e.dma_start(out=self.wba[:, :, :],
                                                       in_=Win[:, cb0:cb0 + 2 * H].rearrange("(kc p) n -> p kc n", p=128))),
                  writes=['wba'])
        if S.dry:
            return
        tb = min(W, C64)
        for blk in range(ntok_blocks):
            psap, pskey = self.ps('aux')
            u = self.u

            def fn(e, blk=blk, psap=psap, tb=tb):
                ins = None
                for k in range(c.KC):
                    ins = e.matmul(psap[0:tb, 0:2 * H], lhsT=u[:, k, blk * tb:(blk + 1) * tb], rhs=self.wba[:, k, :],
                                   start=(k == 0), stop=(k == c.KC - 1))
                return ins
            S.op('pe', fn, reads=['wba'] + [('u', k) for k in range(c.KC)], writes=[pskey])
            S.op('act', (lambda e, blk=blk, psap=psap, tb=tb: e.activation(out=self.g_beta[0:tb, blk, :], in_=psap[0:tb, 0:H],
                                                                         func=AF.Sigmoid)),
                 reads=[pskey], writes=['g_beta'])
            S.op('dve', (lambda e, blk=blk, psap=psap, tb=tb: e.tensor_tensor(out=self.g_g[0:tb, blk, :], in0=psap[0:tb, H:2 * H],
                                                                            in1=self.dtb[0:tb, :], op=ALU.add)),
                 reads=[pskey, 'const'], writes=['g_g'])
            S.op('act', (lambda e, blk=blk, tb=tb: e.activation(out=self.g_g[0:tb, blk, :], in_=self.g_g[0:tb, blk, :], func=AF.Exp)),
                 reads=['g_g'], writes=['g_g'])
            S.op('dve', (lambda e, blk=blk, tb=tb: e.tensor_scalar(out=self.g_g[0:tb, blk, :], in0=self.g_g[0:tb, blk, :],
                                                                 scalar1=1.0, scalar2=None, op0=ALU.add)),
                 reads=['g_g'], writes=['g_g'])
            S.op('act', (lambda e, blk=blk, tb=tb: e.activation(out=self.g_g[0:tb, blk, :], in_=self.g_g[0:tb, blk, :], func=AF.Ln)),
                 reads=['g_g'], writes=['g_g'])
            S.op('dve', (lambda e, blk=blk, tb=tb: e.tensor_tensor(out=self.g_g[0:tb, blk, :], in0=self.g_g[0:tb, blk, :],
                                                                 in1=self.nA[0:tb, :], op=ALU.mult)),
                 reads=['g_g', 'const'], writes=['g_g'])

    def gdn_proj_head(self, hd, W, which, sample):
        c = self.c
        S = self.S
        H = c.H
        KC = c.KC
        j = hd // 2
        Win = self.w['gdn_w_in'][0]
        nh = min(2, H - 2 * j)
        dsts = {'q': self.hq, 'k': self.hk, 'v': self.hv}
        par = 0
        qst = qnew = None
        if sample and not S.dry:
            qst, qnew = self.qs_st[par], self.qs_new[par]
            for idx in range(3):
                ch0 = idx * H + 2 * j
                S.dma('pool', f'qsin{par}', (lambda e, qst=qst, idx=idx, ch0=ch0: e.dma_start(
                    out=qst[:, idx, 0:nh, :, :], in_=self.d_qs_in[:, ch0:ch0 + nh, :, :])), writes=[('qs_st', par)])
            S.op('act', (lambda e, qst=qst, qnew=qnew: e.activation(out=qnew[:, :, :, :, 0:2], in_=qst[:, :, :, :, 1:3],
                                                                    func=AF.Copy)),
                 reads=[('qs_st', par)], writes=[('qs_new', par)])
        for idx, nm in enumerate(('q', 'k', 'v', 'z')):
            colbase = (idx * H * 128 if nm != 'z' else c.QKV) + j * 256
            wt, wk = self.wget(Win[:, colbase:colbase + nh * 128], KC, nh * 128, ('gin', nm, j))
            if S.dry:
                continue
            for hh in range(nh):
                head = 2 * j + hh
                psap, pskey = self.ps('lin')
                self.mm_acc(psap, pskey, [(wt, wk, list(range(KC)), hh * 128, 128, self.u, list(range(KC)),
                                           [('u', k) for k in range(KC)])], W)
                if nm == 'z':
                    S.op('act', (lambda e, hh=hh, psap=psap, W=W: e.activation(out=self.hz[hh][:, 0:W], in_=psap[:, 0:W],
                                                                             func=AF.Silu)),
                         reads=[pskey], writes=[('hz', hh)])
                    continue
                qch = idx * H + head
                dst = dsts[nm][hh]
                dkey = ('h' + nm, hh)
                if not sample:
                    cv, cvk = self.cvb[idx % 2][hh], ('cvb', idx % 2, hh)
                    S.op('act', (lambda e, cv=cv, qch=qch: e.activation(out=cv[:, 0:3], in_=self.qhalo[:, qch, :], func=AF.Copy)),
                         reads=[('qhalo', qch)], writes=[cvk])
                    S.op('act', (lambda e, cv=cv, psap=psap, W=W: e.activation(out=cv[:, 3:3 + W], in_=psap[:, 0:W], func=AF.Copy)),
                         reads=[pskey, cvk], writes=[cvk])
                    S.op('act', (lambda e, cv=cv, qch=qch, W=W: e.activation(out=self.qhalo[:, qch, :], in_=cv[:, W:W + 3],
                                                                           func=AF.Copy)),
                         reads=[cvk], writes=[('qhalo', qch)])
                    for w in range(SC):
                        if w == 0:
                            S.op('dve', (lambda e, cv=cv, dst=dst, qch=qch, W=W: e.tensor_scalar(
                                out=dst[:, 0:W], in0=cv[:, 0:W], scalar1=self.gwc[:, qch, 0:1], scalar2=None, op0=ALU.mult)),
                                 reads=[cvk, 'const'], writes=[dkey])
                        else:
                            S.op('dve', (lambda e, cv=cv, dst=dst, qch=qch, W=W, w=w: e.scalar_tensor_tensor(
                                out=dst[:, 0:W], in0=cv[:, w:w + W], scalar=self.gwc[:, qch, w:w + 1], in1=dst[:, 0:W],
                                op0=ALU.mult, op1=ALU.add)), reads=[cvk, dkey, 'const'], writes=[dkey])
                else:
                    NS = c.NS
                    S.op('act', (lambda e, qnew=qnew, idx=idx, hh=hh, psap=psap, W=W: e.activation(
                        out=qnew[:, idx, hh, :, 2], in_=psap[:, 0:W], func=AF.Copy)),
                         reads=[pskey, ('qs_new', par)], writes=[('qs_new', par)])
                    pr, prk = self.qs_prod, 'qsprod'
                    S.op('dve', (lambda e, qst=qst, idx=idx, hh=hh, qch=qch, pr=pr, NS=NS: e.tensor_tensor(
                        out=pr[:, :, :], in0=qst[:, idx, hh, :, :],
                        in1=self.gwc[:, qch, 0:3].unsqueeze(1).to_broadcast([128, NS, 3]), op=ALU.mult)),
                         reads=[('qs_st', par), 'const'], writes=[prk])
                    S.op('dve', (lambda e, dst=dst, pr=pr, W=W: e.tensor_reduce(out=dst[:, 0:W], in_=pr[:, :, :], axis=AX.X,
                                                                               op=ALU.add)),
                         reads=[prk], writes=[dkey])
                    S.op('dve', (lambda e, dst=dst, psap=psap, qch=qch, W=W: e.scalar_tensor_tensor(
                        out=dst[:, 0:W], in0=psap[:, 0:W], scalar=self.gwc[:, qch, 3:4], in1=dst[:, 0:W],
                        op0=ALU.mult, op1=ALU.add)), reads=[pskey, dkey, 'const'], writes=[dkey])
                S.op('act', (lambda e, dst=dst, W=W: e.activation(out=dst[:, 0:W], in_=dst[:, 0:W], func=AF.Silu)),
                     reads=[dkey], writes=[dkey])
                if nm in ('q', 'k'):
                    rn, rnk = self.tmp('rn')
                    self.colsum_sq(lambda k, dst=dst, dkey=dkey, W=W: (dst[:, 0:W], dkey), 1, W, 1.0, L2_EPS, rn, rnk)
                    sc = (128.0 ** -0.5) if nm == 'q' else 1.0
                    S.op('dve', (lambda e, dst=dst, rn=rn, W=W, sc=sc: e.scalar_tensor_tensor(
                        out=dst[:, 0:W], in0=dst[:, 0:W], scalar=sc, in1=rn[:, 0:W], op0=ALU.mult, op1=ALU.mult)),
                         reads=[dkey, rnk], writes=[dkey])

        if sample and not S.dry:
            for idx in range(3):
                ch0 = idx * H + 2 * j
                S.dma('pool', f'qsout{par}', (lambda e, qnew=qnew, idx=idx, ch0=ch0: e.dma_start(
                    out=self.d_qkv_s[:, ch0:ch0 + nh, :, :], in_=qnew[:, idx, 0:nh, :, :])),
                      reads=[('qs_new', par)], writes=[('d_qkv_s', idx, j)])

    def gdn_prompt_head(self, hd, hh, W):
        c = self.c
        S = self.S
        H = c.H
        NCH = W // C64
        NW = NCH * C64
        T = self.Tb[hh]
        qn, kn, vv = self.hq[hh], self.hk[hh], self.hv[hh]
        kq, kk, kv_ = ('hq', hh), ('hk', hh), ('hv', hh)
        ident = self.ident_f
        knb, qnb = T.knb, T.qnb
        S.op('act', (lambda e: e.activation(out=knb[:, 0:NW], in_=kn[:, 0:NW], func=AF.Copy)), reads=[kk], writes=['knb'])
        gcol = self.g_g[0:C64, 0:NCH, hd]
        bcol = self.g_beta[0:C64, 0:NCH, hd]
        pG, pGk = self.ps('aux')
        S.op('pe', (lambda e: e.matmul(pG[0:C64, 0:NCH], lhsT=self.tri_f[0:C64, 0:C64], rhs=gcol, start=True, stop=True)),
             reads=['g_g', 'const'], writes=[pGk])
        S.op('pe', (lambda e: e.matmul(pG[:, 64:64 + NCH], lhsT=self.ones_f[0:C64, :], rhs=gcol, start=True, stop=True)),
             reads=['g_g', 'const', pGk], writes=[pGk])
        Gc = T.t_G
        S.op('dve', (lambda e: e.tensor_copy(out=Gc[:, 0:NCH], in_=pG[0:C64, 0:NCH])), reads=[pGk], writes=['t_G'])
        S.op('act', (lambda e: e.activation(out=T.t_alast[:, 0:NCH], in_=pG[:, 64:64 + NCH], func=AF.Exp)),
             reads=[pGk], writes=['t_alast'])
        S.op('dve', (lambda e: e.tensor_tensor(out=T.t_kdsc[:, 0:NCH], in0=pG[0:C64, 64:64 + NCH], in1=Gc[:, 0:NCH],
                                               op=ALU.subtract)), reads=[pGk, 't_G'], writes=['t_kdsc'])
        S.op('act', (lambda e: e.activation(out=T.t_kdsc[:, 0:NCH], in_=T.t_kdsc[:, 0:NCH], func=AF.Exp)),
             reads=['t_kdsc'], writes=['t_kdsc'])
        S.op('act', (lambda e: e.activation(out=T.t_eG[:, 0:NCH], in_=Gc[:, 0:NCH], func=AF.Exp)), reads=['t_G'],
             writes=['t_eG'])
        S.op('dve', (lambda e: e.scalar_tensor_tensor(out=T.t_nbe[:, 0:NCH], in0=T.t_eG[:, 0:NCH], scalar=-1.0, in1=bcol,
                                                      op0=ALU.mult, op1=ALU.mult)), reads=['t_eG', 'g_beta'], writes=['t_nbe'])
        gt, gtk = T.t_X, 't_X'
        S.op('dve', (lambda e: e.tensor_tensor(out=gt[:, 0:NCH, :], in0=gcol.unsqueeze(2).to_broadcast([C64, NCH, C64]),
                                               in1=self.tri_f[0:C64, 0:C64].unsqueeze(1).to_broadcast([C64, NCH, C64]),
                                               op=ALU.mult)), reads=['g_g', 'const'], writes=[gtk])
        pR, pRk = self.ps('aux')
        S.op('pe', (lambda e: e.matmul(pR[:, 0:NW], lhsT=self.ones_f[0:C64, :], rhs=gt[:, 0:NCH, :].rearrange("p c x -> p (c x)"),
                                       start=True, stop=True)), reads=[gtk, 'const'], writes=[pRk])
        S.op('act', (lambda e: e.activation(out=T.t_eGrow[:, 0:NW], in_=pR[:, 0:NW], func=AF.Exp)), reads=[pRk],
             writes=['t_eGrow'])
        S.op('dve', (lambda e: e.tensor_tensor(out=qnb[:, 0:NW], in0=qn[:, 0:NW], in1=T.t_eGrow[:, 0:NW], op=ALU.mult)),
             reads=[kq, 't_eGrow'], writes=['qnb'])
        X, Xk = T.t_X, 't_X'
        S.op('dve', (lambda e: e.tensor_tensor(out=X[:, 0:NCH, :], in0=pR[0:C64, 0:NW].rearrange("p (c x) -> p c x", c=NCH),
                                               in1=Gc[:, 0:NCH].unsqueeze(2).to_broadcast([C64, NCH, C64]), op=ALU.subtract)),
             reads=[pRk, 't_G'], writes=[Xk])
        D, DT = T.t_D, T.t_DT
        S.op('dve', (lambda e: e.scalar_tensor_tensor(out=D[:, 0:NCH, :], in0=X[:, 0:NCH, :], scalar=-1.0,
                                                      in1=self.m_up.unsqueeze(1).to_broadcast([C64, NCH, C64]),
                                                      op0=ALU.mult, op1=ALU.subtract)), reads=[Xk, 'const'], writes=['t_D'])
        S.op('act', (lambda e: e.activation(out=D[:, 0:NCH, :], in_=D[:, 0:NCH, :], func=AF.Exp)), reads=['t_D'], writes=['t_D'])
        S.op('dve', (lambda e: e.tensor_tensor(out=DT[:, 0:NCH, :], in0=X[:, 0:NCH, :],
                                               in1=self.m_lo.unsqueeze(1).to_broadcast([C64, NCH, C64]), op=ALU.subtract)),
             reads=[Xk, 'const'], writes=['t_DT'])
        S.op('act', (lambda e: e.activation(out=DT[:, 0:NCH, :], in_=DT[:, 0:NCH, :], func=AF.Exp)), reads=['t_DT'],
             writes=['t_DT'])
        S.op('dve', (lambda e: e.tensor_tensor(out=D[:, 0:NCH, :], in0=D[:, 0:NCH, :],
                                               in1=self.m_strict.unsqueeze(1).to_broadcast([C64, NCH, C64]), op=ALU.mult)),
             reads=['t_D', 'const'], writes=['t_D'])
        S.op('dve', (lambda e: e.scalar_tensor_tensor(out=D[:, 0:NCH, :], in0=D[:, 0:NCH, :], scalar=-1.0,
                                                      in1=bcol.unsqueeze(2).to_broadcast([C64, NCH, C64]),
                                                      op0=ALU.mult, op1=ALU.mult)), reads=['t_D', 'g_beta'], writes=['t_D'])
        pA, pAk = self.ps('aux')
        pQ, pQk = self.ps('aux')

        def fa(e):
            ins = None
            for ci in range(NCH):
                sl = slice(ci * C64, (ci + 1) * C64)
                ins = e.matmul(pA[0:C64, sl], lhsT=knb[:, sl], rhs=knb[:, sl], start=True, stop=True)
            return ins
        S.op('pe', fa, reads=['knb'], writes=[pAk])
        S.op('act', (lambda e: e.activation(out=T.qrb[:, 0:NW], in_=qn[:, 0:NW], func=AF.Copy)), reads=[kq], writes=['qrb'])

        def fq(e):
            ins = None
            for ci in range(NCH):
                sl = slice(ci * C64, (ci + 1) * C64)
                ins = e.matmul(pQ[0:C64, sl], lhsT=knb[:, sl], rhs=T.qrb[:, sl], start=True, stop=True)
            return ins
        S.op('pe', fq, reads=['knb', 'qrb'], writes=[pQk])
        Nm, NmT = T.t_N, T.t_NT
        S.op('dve', (lambda e: e.tensor_tensor(out=Nm[0][:, 0:NCH, :], in0=pA[0:C64, 0:NW].rearrange("p (c x) -> p c x", c=NCH),
                                               in1=D[:, 0:NCH, :], op=ALU.mult)), reads=[pAk, 't_D'], writes=[('t_N', 0)])
        S.op('dve', (lambda e: e.tensor_tensor(out=T.t_attT[:, 0:NCH, :],
                                               in0=pQ[0:C64, 0:NW].rearrange("p (c x) -> p c x", c=NCH),
                                               in1=DT[:, 0:NCH, :], op=ALU.mult)), reads=[pQk, 't_DT'], writes=['t_attT'])
        pT_, pTk = self.ps('aux')

        def ft(e):
            ins = None
            for ci in range(NCH):
                sl = slice(ci * C64, (ci + 1) * C64)
                ins = e.transpose(pT_[0:C64, sl], Nm[0][:, ci, :], ident[0:C64, 0:C64])
            return ins
        S.op('pe', ft, reads=[('t_N', 0), 'const'], writes=[pTk])
        S.op('act', (lambda e: e.activation(out=NmT[0][:, 0:NCH, :], in_=pT_[0:C64, 0:NW].rearrange("p (c x) -> p c x", c=NCH),
                                            func=AF.Copy)), reads=[pTk], writes=[('t_NT', 0)])
        PT = T.t_PT
        S.op('dve', (lambda e: e.tensor_tensor(out=PT[0][:, 0:NCH, :], in0=NmT[0][:, 0:NCH, :],
                                               in1=ident[0:C64, 0:C64].unsqueeze(1).to_broadcast([C64, NCH, C64]), op=ALU.add)),
             reads=[('t_NT', 0), 'const'], writes=[('t_PT', 0)])
        cur = 0
        for lvl in range(1, 6):
            nxt = 1 - cur
            pM, pMk = self.ps('aux')

            def fm(e, cur=cur, pM=pM):
                ins = None
                for ci in range(NCH):
                    sl = slice(ci * C64, (ci + 1) * C64)
                    ins = e.matmul(pM[0:C64, sl], lhsT=NmT[cur][:, ci, :], rhs=Nm[cur][:, ci, :], start=True, stop=True)
                return ins
            S.op('pe', fm, reads=[('t_N', cur), ('t_NT', cur)], writes=[pMk])
            S.op('act', (lambda e, nxt=nxt, pM=pM: e.activation(out=Nm[nxt][:, 0:NCH, :],
                                                               in_=pM[0:C64, 0:NW].rearrange("p (c x) -> p c x", c=NCH),
                                                               func=AF.Copy)), reads=[pMk], writes=[('t_N', nxt)])
            if lvl < 5:
                pMT, pMTk = self.ps('aux')

                def fmt(e, cur=cur, pMT=pMT):
                    ins = None
                    for ci in range(NCH):
                        sl = slice(ci * C64, (ci + 1) * C64)
                        ins = e.matmul(pMT[0:C64, sl], lhsT=Nm[cur][:, ci, :], rhs=NmT[cur][:, ci, :], start=True, stop=True)
                    return ins
                S.op('pe', fmt, reads=[('t_N', cur), ('t_NT', cur)], writes=[pMTk])
                S.op('dve', (lambda e, nxt=nxt, pMT=pMT: e.tensor_copy(out=NmT[nxt][:, 0:NCH, :],
                                                                      in_=pMT[0:C64, 0:NW].rearrange("p (c x) -> p c x", c=NCH))),
                     reads=[pMTk], writes=[('t_NT', nxt)])
            pP, pPk = self.ps('aux')

            def fp(e, nxt=nxt, cur=cur, pP=pP):
                ins = None
                for ci in range(NCH):
                    sl = slice(ci * C64, (ci + 1) * C64)
                    ins = e.matmul(pP[0:C64, sl], lhsT=Nm[nxt][:, ci, :], rhs=PT[0][:, ci, :], start=True, stop=True)
                return ins
            S.op('pe', fp, reads=[('t_N', nxt), ('t_PT', 0)], writes=[pPk])
            S.op('dve', (lambda e, nxt=nxt, cur=cur, pP=pP: e.tensor_tensor(
                out=PT[0][:, 0:NCH, :], in0=pP[0:C64, 0:NW].rearrange("p (c x) -> p c x", c=NCH), in1=PT[0][:, 0:NCH, :],
                op=ALU.add)), reads=[pPk, ('t_PT', 0)], writes=[('t_PT', 0)])
            cur = nxt
        PTf, PTk = PT[0], ('t_PT', 0)
        for ci in range(NCH):
            sl = slice(ci * C64, (ci + 1) * C64)
            pk, pkk = self.ps('aux')
            S.op('pe', (lambda e, pk=pk, sl=sl: e.transpose(pk[0:C64, 0:128], kn[:, sl], ident[:, :])), reads=[kk, 'const'],
                 writes=[pkk])
            S.op('pe', (lambda e, pk=pk, sl=sl: e.transpose(pk[0:C64, 128:256], vv[:, sl], ident[:, :])),
                 reads=[kv_, 'const', pkk], writes=[pkk])
            S.op('dve', (lambda e, pk=pk, ci=ci: e.tensor_scalar(out=T.t_kd[:, ci, :], in0=pk[0:C64, 0:128],
                                                               scalar1=T.t_kdsc[:, ci:ci + 1], scalar2=None, op0=ALU.mult)),
                 reads=[pkk, 't_kdsc'], writes=[('t_kd', ci)])
            S.op('dve', (lambda e, pk=pk, ci=ci: e.tensor_scalar(out=T.t_bv[:, ci, :], in0=pk[0:C64, 128:256],
                                                               scalar1=self.g_beta[0:C64, ci, hd:hd + 1], scalar2=None,
                                                               op0=ALU.mult)),
                 reads=[pkk, 'g_beta'], writes=[('t_bv', ci)])
        Sf = self.Sst[:, hd, :]
        Sk = ('S', hd)
        Sb = T.Sbf
        S.op('act', (lambda e: e.activation(out=Sb[:, :], in_=Sf, func=AF.Copy)), reads=[Sk], writes=['Sbf'])
        for ci in range(NCH):
            sl = slice(ci * C64, (ci + 1) * C64)
            p1, p1k = self.ps('aux')
            S.op('pe', (lambda e, p1=p1, sl=sl: e.matmul(p1[0:C64, 0:128], lhsT=knb[:, sl], rhs=Sb[:, :], start=True, stop=True)),
                 reads=['knb', 'Sbf'], writes=[p1k])
            S.op('dve', (lambda e, p1=p1, ci=ci: e.scalar_tensor_tensor(out=T.t_R[:, :], in0=p1[0:C64, 0:128],
                                                                      scalar=T.t_nbe[:, ci:ci + 1], in1=T.t_bv[:, ci, :],
                                                                      op0=ALU.mult, op1=ALU.add)),
                 reads=[p1k, 't_nbe', ('t_bv', ci)], writes=['t_R'])
            S.op('pe', (lambda e, p1=p1, ci=ci: e.matmul(p1[0:C64, 128:256], lhsT=PTf[:, ci, :], rhs=T.t_R[:, :],
                                                       start=True, stop=True)), reads=[PTk, 't_R', p1k], writes=[p1k])
            S.op('act', (lambda e, p1=p1: e.activation(out=T.t_ub[:, :], in_=p1[0:C64, 128:256], func=AF.Copy)),
                 reads=[p1k], writes=['t_ub'])
            p2, p2k = self.ps('aux')

            def fo(e, p2=p2, sl=sl, ci=ci):
                e.matmul(p2[0:C64, 0:128], lhsT=qnb[:, sl], rhs=Sb[:, :], start=True, stop=False)
                return e.matmul(p2[0:C64, 0:128], lhsT=T.t_attT[:, ci, :], rhs=T.t_ub[:, :], start=False, stop=True)
            S.op('pe', fo, reads=['qnb', 'Sbf', 't_attT', 't_ub'], writes=[p2k])
            S.op('act', (lambda e, p2=p2, ci=ci: e.activation(out=T.t_o[:, ci, :], in_=p2[0:C64, 0:128], func=AF.Copy)),
                 reads=[p2k], writes=[('t_o', ci)])
            p3, p3k = self.ps('aux')
            S.op('pe', (lambda e, p3=p3, ci=ci: e.matmul(p3[:, 0:128], lhsT=T.t_kd[:, ci, :], rhs=T.t_ub[:, :],
                                                       start=True, stop=True)), reads=[('t_kd', ci), 't_ub'], writes=[p3k])
            S.op('dve', (lambda e, p3=p3, ci=ci: e.scalar_tensor_tensor(out=Sf, in0=Sf, scalar=T.t_alast[:, ci:ci + 1],
                                                                      in1=p3[:, 0:128], op0=ALU.mult, op1=ALU.add)),
                 reads=[p3k, 't_alast', Sk], writes=[Sk])
            if ci < NCH - 1:
                S.op('act', (lambda e: e.activation(out=Sb[:, :], in_=Sf, func=AF.Copy)), reads=[Sk], writes=['Sbf'])
        sq = T.t_bv
        sqkeys = [('t_bv', ci) for ci in range(NCH)]
        S.op('dve', (lambda e: e.tensor_tensor(out=sq[:, 0:NCH, :], in0=T.t_o[:, 0:NCH, :], in1=T.t_o[:, 0:NCH, :],
                                               op=ALU.mult)), reads=[('t_o', ci) for ci in range(NCH)], writes=sqkeys)
        S.op('dve', (lambda e: e.tensor_reduce(out=T.t_oss[:, 0:NCH], in_=sq[:, 0:NCH, :], axis=AX.X, op=ALU.add)),
             reads=sqkeys, writes=['t_oss'])
        S.op('dve', (lambda e: e.tensor_scalar(out=T.t_oss[:, 0:NCH], in0=T.t_oss[:, 0:NCH], scalar1=1.0 / 128,
                                               scalar2=RMS_EPS, op0=ALU.mult, op1=ALU.add)), reads=['t_oss'], writes=['t_oss'])
        self.rsqrt_(T.t_oss[:, 0:NCH], 't_oss')
        S.op('dve', (lambda e: e.tensor_tensor(out=sq[:, 0:NCH, :], in0=T.t_o[:, 0:NCH, :],
                                               in1=T.t_oss[:, 0:NCH].unsqueeze(2).to_broadcast([C64, NCH, 128]), op=ALU.mult)),
             reads=[('t_o', ci) for ci in range(NCH)] + ['t_oss'] + sqkeys, writes=sqkeys)
        pO, pOk = self.ps('aux')

        def fto(e):
            ins = None
            for ci in range(NCH):
                ins = e.transpose(pO[:, ci * C64:(ci + 1) * C64], sq[:, ci, :], ident[0:C64, 0:C64])
            return ins
        S.op('pe', fto, reads=sqkeys + ['const'], writes=[pOk])
        S.op('dve', (lambda e: e.scalar_tensor_tensor(out=self.scr[:, hd, 0:NW], in0=pO[:, 0:NW], scalar=self.gon[:, 0:1],
                                                      in1=self.hz[hh][:, 0:NW], op0=ALU.mult, op1=ALU.mult)),
             reads=[pOk, ('hz', hh), 'const'], writes=[('scr', hd)])

    def gdn_sample_states(self, W):
        c = self.c
        S = self.S
        H, NS = c.H, c.NS
        ident = self.ident_f
        S.op('act', (lambda e: e.activation(out=self.s_a[0:NS, :], in_=self.g_g[0:NS, 0, :], func=AF.Exp)),
             reads=['g_g'], writes=['s_a'])
        for b in range(NS):
            Sb_, Sbk = self.s_S, 's_S'
            S.dma('pool', 'sin', (lambda e, Sb_=Sb_, b=b: e.dma_start(out=Sb_[:, :, :], in_=self.d_gs_in[b])), writes=[Sbk])
            S.op('dve', (lambda e, b=b: e.tensor_scalar(out=self.s_bd[0:NS, 0, :], in0=self.s_a[0:NS, :],
                                                       scalar1=ident[0:NS, b:b + 1], scalar2=None, op0=ALU.mult)),
                 reads=['s_a', 'const'], writes=['s_bd'])
            S.op('dve', (lambda e, b=b: e.tensor_scalar(out=self.s_bd[0:NS, 1, :], in0=self.g_beta[0:NS, 0, :],
                                                       scalar1=ident[0:NS, b:b + 1], scalar2=None, op0=ALU.mult)),
                 reads=['g_beta', 'const', 's_bd'], writes=['s_bd'])
            pb, pbk = self.ps('aux')
            S.op('pe', (lambda e, pb=pb: e.matmul(pb[:, 0:2 * H], lhsT=self.ones_f[0:NS, :],
                                                  rhs=self.s_bd[0:NS, :, :].rearrange("p a h -> p (a h)"),
                                                  start=True, stop=True)), reads=['s_bd', 'const'], writes=[pbk])
            S.op('act', (lambda e, pb=pb: e.activation(out=self.s_ab[:, :, :],
                                                       in_=pb[:, 0:2 * H].rearrange("p (a h) -> p a h", a=2), func=AF.Copy)),
                 reads=[pbk], writes=['s_ab'])
            pk, pkk = self.ps('aux')

            def fks(e, Sb_=Sb_, pk=pk, b=b):
                ins = None
                for hd in range(H):
                    ins = e.matmul(pk[:, hd:hd + 1], lhsT=Sb_[:, hd, :], rhs=self.s_k[:, hd, b:b + 1], start=True, stop=True)
                return ins
            S.op('pe', fks, reads=[Sbk, 's_k'], writes=[pkk])
            r, rk = self.s_r, 's_r'
            S.op('dve', (lambda e, pk=pk, b=b: e.tensor_tensor(out=r[:, :], in0=pk[:, 0:H], in1=self.s_ab[:, 0, :], op=ALU.mult)),
                 reads=[pkk, 's_ab'], writes=[rk])
            S.op('dve', (lambda e, b=b: e.tensor_tensor(out=r[:, :], in0=self.s_v[:, :, b], in1=r[:, :], op=ALU.subtract)),
                 reads=[rk, 's_v'], writes=[rk])
            S.op('dve', (lambda e, b=b: e.tensor_tensor(out=r[:, :], in0=r[:, :], in1=self.s_ab[:, 1, :], op=ALU.mult)),
                 reads=[rk, 's_ab'], writes=[rk])
            pt, ptk = self.ps('aux')
            S.op('pe', (lambda e, pt=pt, b=b: e.transpose(pt[0:H, 0:128], self.s_k[:, :, b], ident[:, :])), reads=['s_k', 'const'],
                 writes=[ptk])
            S.op('pe', (lambda e, pt=pt: e.transpose(pt[0:H, 128:256], r[:, :], ident[:, :])), reads=[rk, 'const', ptk],
                 writes=[ptk])
            S.op('act', (lambda e, pt=pt: e.activation(out=self.s_rows[0:H, :], in_=pt[0:H, 0:256], func=AF.Copy)), reads=[ptk],
                 writes=['s_rows'])
            for g0 in range(0, H, 2):
                ng = min(2, H - g0)
                po, pok = self.ps('aux')
                rb, rbk = self.s_rbd[(g0 // 2) % 2], ('s_rbd', (g0 // 2) % 2)
                S.op('dve', (lambda e, rb=rb, g0=g0, ng=ng: e.tensor_tensor(
                    out=rb[0:H, 0:ng, :], in0=self.s_rows[0:H, 128:256].unsqueeze(1).to_broadcast([H, ng, 128]),
                    in1=ident[0:H, g0:g0 + ng].unsqueeze(2).to_broadcast([H, ng, 128]), op=ALU.mult)),
                     reads=['s_rows', 'const'], writes=[rbk])
                S.op('pe', (lambda e, po=po, rb=rb, ng=ng: e.matmul(
                    po[:, 0:ng * 128], lhsT=self.s_rows[0:H, 0:128],
                    rhs=rb[0:H, 0:ng, :].rearrange("p h d -> p (h d)"), start=True, stop=True)),
                     reads=['s_rows', rbk], writes=[pok])
                S.op('dve', (lambda e, Sb_=Sb_, b=b, g0=g0, ng=ng: e.tensor_tensor(
                    out=Sb_[:, g0:g0 + ng, :], in0=Sb_[:, g0:g0 + ng, :],
                    in1=self.s_ab[:, 0, g0:g0 + ng].unsqueeze(2).to_broadcast([128, ng, 128]), op=ALU.mult)),
                     reads=[Sbk, 's_ab'], writes=[Sbk])
                S.op('dve', (lambda e, Sb_=Sb_, po=po, g0=g0, ng=ng: e.tensor_tensor(
                    out=Sb_[:, g0:g0 + ng, :], in0=Sb_[:, g0:g0 + ng, :],
                    in1=po[:, 0:ng * 128].rearrange("p (h d) -> p h d", h=ng), op=ALU.add)),
                     reads=[Sbk, pok], writes=[Sbk])
            pq, pqk = self.ps('aux')

            def foq(e, Sb_=Sb_, pq=pq, b=b):
                ins = None
                for hd in range(H):
                    ins = e.matmul(pq[:, hd:hd + 1], lhsT=Sb_[:, hd, :], rhs=self.s_q[:, hd, b:b + 1], start=True, stop=True)
                return ins
            S.op('pe', foq, reads=[Sbk, 's_q'], writes=[pqk])
            S.op('act', (lambda e, pq=pq, b=b: e.activation(out=self.s_o[:, :, b], in_=pq[:, 0:H], func=AF.Copy)), reads=[pqk],
                 writes=['s_o'])
            S.dma('pool', 'sout', (lambda e, Sb_=Sb_, b=b: e.dma_start(out=self.d_gdn_s[b], in_=Sb_[:, :, :])),
                  reads=[Sbk], writes=[('d_gdn_s', b)])
        HB = H * NS
        of = self.s_o[:, :, :].rearrange("p h b -> p (h b)")
        hstep = max(1, 256 // NS)
        for h0 in range(0, H, hstep):
            nh = min(hstep, H - h0)
            o0, n = h0 * NS, nh * NS
            rn, rnk = self.tmp('rn')
            self.colsum_sq(lambda k, o0=o0, n=n: (of[:, o0:o0 + n], 's_o'), 1, n, 1.0 / 128, RMS_EPS, rn, rnk)
            t, tk = self.tmp('on')
            S.op('dve', (lambda e, t=t, rn=rn, o0=o0, n=n: e.scalar_tensor_tensor(out=t[:, 0:n], in0=of[:, o0:o0 + n],
                                                                               scalar=self.gon[:, 0:1], in1=rn[:, 0:n],
                                                                               op0=ALU.mult, op1=ALU.mult)),
                 reads=['s_o', rnk, 'const'], writes=[tk])
            S.op('dve', (lambda e, t=t, n=n, h0=h0, nh=nh: e.tensor_tensor(
                out=self.scr[:, h0:h0 + nh, 0:NS], in0=t[:, 0:n].rearrange("p (h b) -> p h b", h=nh),
                in1=self.s_z[:, h0:h0 + nh, :], op=ALU.mult)),
                 reads=[tk, 's_z'], writes=[('scr', hx) for hx in range(h0, h0 + nh)])

    def gdn(self, W, sample, t0):
        c = self.c
        S = self.S
        H, NS = c.H, c.NS
        h = self.h
        self.gdn_gates(W, 1 if sample else W // C64)
        for j in range((H + 1) // 2):
            self.gdn_proj_head(2 * j, W, None, sample)
            if S.dry:
                continue
            if not sample:
                caps = []
                for hh in range(min(2, H - 2 * j)):
                    S.cap = []
                    S.ns = hh
                    self.aux_set = (2 * hh, 2 * hh + 1)
                    self.gdn_prompt_head(2 * j + hh, hh, W)
                    caps.append(S.cap)
                    S.cap = None
                    S.ns = None
                    self.aux_set = None
                S.replay_zipped(caps)
                continue
            for hh in range(min(2, H - 2 * j)):
                hd = 2 * j + hh
                if True:
                    for nm, src, dst in (('q', self.hq, self.s_q), ('k', self.hk, self.s_k), ('v', self.hv, self.s_v),
                                         ('z', self.hz, self.s_z)):
                        S.op('act', (lambda e, src=src, dst=dst, hh=hh, hd=hd: e.activation(out=dst[:, hd, :], in_=src[hh][:, 0:NS],
                                                                                            func=AF.Copy)),
                             reads=[('h' + nm, hh)], writes=['s_' + nm])
        if sample and not S.dry:
            self.gdn_sample_states(W)

        def cb(mi, psap, pskey):
            S.op('dve', (lambda e, mi=mi, psap=psap, W=W: e.tensor_tensor(out=h[:, mi, 0:W], in0=h[:, mi, 0:W], in1=psap[:, 0:W],
                                                                        op=ALU.add)),
                 reads=[pskey, ('h', mi)], writes=[('h', mi)])
        self.linear(self.w['gdn_w_out'][0], c.KC, 0, c.D, self.scr, lambda k: ('scr', k), W, cb, 'gout')

    def tile(self, ti, sample):
        c = self.c
        S = self.S
        W = c.NS if sample else c.TT
        t0 = 0 if sample else ti * c.TT
        h = self.h
        if not S.dry:
            src = self.d_xsT if sample else self.d_xpT[:, :, t0:t0 + W]
            S.dma('pool', 'xin', (lambda e, src=src, W=W: e.dma_start(out=h[:, :, 0:W], in_=src)),
                  writes=[('h', k) for k in range(c.KC)])
        for layer in range(2):
            self.rmsnorm(self.gmix, layer, W)
            if layer == 0:
                self.conformer(W, sample, t0)
            else:
                self.gdn(W, sample, t0)
            self.rmsnorm(self.gffn, layer, W)
            self.ffn(layer, W)
            self.rmsnorm(self.gple, layer, W)
            self.ple(layer, W, sample, t0)
        if S.dry:
            return
        self.colsum_sq(lambda k: (h[:, k, 0:W], ('h', k)), c.KC, W, 1.0 / c.D, RMS_EPS, self.rstd, 'rstd')
        for k in range(c.KC):
            t, tk = self.tmp('y')
            S.op('dve', (lambda e, t=t, k=k, W=W: e.scalar_tensor_tensor(out=t[:, 0:W], in0=h[:, k, 0:W],
                                                                       scalar=self.gfin[:, 0, k:k + 1], in1=self.rstd[:, 0:W],
                                                                       op0=ALU.mult, op1=ALU.mult)),
                 reads=[('h', k), 'rstd', 'const'], writes=[tk])
            dst = self.d_ysT[:, k, :] if sample else self.d_ypT[:, k, t0:t0 + W]
            S.dma('pool', f'yout{tk[1]}', (lambda e, t=t, dst=dst, W=W: e.dma_start(out=dst, in_=t[:, 0:W])),
                  reads=[tk], writes=[('d_y', sample, ti, k)])

    def build(self):
        c = self.c
        nc = self.nc
        D, F, H, KC, QC, NS, TT, SEQ = c.D, c.F, c.H, c.KC, c.QC, c.NS, c.TT, c.SEQ
        self.d_xpT = self.din("xpT", [128, KC, SEQ])
        self.d_xsT = self.din("xsT", [128, KC, NS])
        self.d_ppT = [self.din(f"ppT{l}", [128, c.PC, SEQ]) for l in range(2)]
        self.d_psT = [self.din(f"psT{l}", [128, c.PC, NS]) for l in range(2)]
        self.d_cs_in = self.din("cs_in", [128, KC, NS, CW - 1])
        self.d_qs_in = self.din("qs_in", [128, QC, NS, SC - 1])
        self.d_gs_in = self.din("gs_in", [NS, 128, H, 128])
        d_vec = {n: self.din(n, s) for n, s in dict(
            gmix=[128, 2, KC], gffn=[128, 2, KC], gple=[128, 2, KC], gfin=[128, 1, KC],
            cwdw=[128, KC, CW], cbdw=[128, KC], clng=[128, KC], clnb=[128, KC], gwc=[128, QC, SC],
            alog=[C64, H], dtb=[C64, H], gon=[128, 1],
            c_ident=[128, 128], c_tri=[C64, C64], c_mup=[C64, C64], c_mlo=[C64, C64], c_mstrict=[C64, C64]).items()}
        self.w = {}
        for n, s in dict(conf_w_pw1=[1, D, 2 * D], conf_w_pw2=[1, D, D], gdn_w_in=[1, D, c.PROJ], gdn_w_out=[1, D, D],
                         ffn_w_gate=[2, D, F], ffn_w_up=[2, D, F], ffn_w_down=[2, F, D], ple_w_gate=[2, D, D],
                         ple_w_proj=[2, c.PLE, D]).items():
            ap = self.din(n, s)
            self.w[n] = [ap[l] for l in range(s[0])]
        self.d_ypT = self.dout("ypT", [128, KC, SEQ])
        self.d_ysT = self.dout("ysT", [128, KC, NS])
        self.d_conf_p = self.dout("conf_p", [128, KC, CW - 1])
        self.d_qkv_p = self.dout("qkv_p", [128, QC, SC - 1])
        self.d_gdn_p = self.dout("gdn_p", [128, H, 128])
        self.d_conf_s = self.dout("conf_s", [128, KC, NS, CW - 1])
        self.d_qkv_s = self.dout("qkv_s", [128, QC, NS, SC - 1])
        self.d_gdn_s = self.dout("gdn_s", [NS, 128, H, 128])

        NCH = c.NCH
        with contextlib.ExitStack() as st:
            self.st = st
            S = self.S = Sched(nc, st)
            self.h = self.sb("h", [128, KC, TT])
            self.u = self.sb("u", [128, KC, TT], BF16)
            self.wb = [self.sb(f"wb{i}", [128, 32, 256], BF16) for i in range(c.NWB)]
            self.psb = [st.enter_context(nc.psum_tensor(f"psb{i}", [128, 512], F32)) for i in range(8)]
            self.tmps = [self.sb(f"tmp{i}", [128, 256]) for i in range(3)]
            self.c_a = [self.sb(f"c_a{i}", [128, 256]) for i in range(2)]
            self.c_b = [self.sb(f"c_b{i}", [128, 256]) for i in range(2)]
            self.rstd = self.sb("rstd", [128, TT])
            self.mean = self.sb("mean", [128, TT])
            self.pT = self.sb("pT", [128, c.PC, TT], BF16)
            self.scr = self.sb("scr", [128, max(c.FGMAX, KC, H), TT], BF16)
            self.wba = self.sb("wba", [128, KC, 2 * H], BF16)
            self.g_beta = self.sb("g_beta", [C64, NCH, H])
            self.g_g = self.sb("g_g", [C64, NCH, H])
            self.hq = [self.sb(f"hq{i}", [128, TT]) for i in range(2)]
            self.hk = [self.sb(f"hk{i}", [128, TT]) for i in range(2)]
            self.hv = [self.sb(f"hv{i}", [128, TT]) for i in range(2)]
            self.hz = [self.sb(f"hz{i}", [128, TT]) for i in range(2)]
            self.ones_f = self.sb("ones_f", [128, 128])
            self.ident_f = self.sb("ident_f", [128, 128])
            self.tri_f = self.sb("tri_f", [C64, C64])
            self.m_up_t = self.sb("m_up", [C64, C64])
            self.m_lo_t = self.sb("m_lo", [C64, C64])
            self.m_strict_t = self.sb("m_strict", [C64, C64])
            self.m_up, self.m_lo, self.m_strict = self.m_up_t[:, :], self.m_lo_t[:, :], self.m_strict_t[:, :]
            self.gmix = self.sb("gmix", [128, 2, KC])
            self.gffn = self.sb("gffn", [128, 2, KC])
            self.gple = self.sb("gple", [128, 2, KC])
            self.gfin = self.sb("gfin", [128, 1, KC])
            self.cwdw = self.sb("cwdw", [128, KC, CW])
            self.cbdw = self.sb("cbdw", [128, KC])
            self.clng = self.sb("clng", [128, KC])
            self.clnb = self.sb("clnb", [128, KC])
            self.gwc = self.sb("gwc", [128, QC, SC])
            self.nA = self.sb("nA", [C64, H])
            self.dtb = self.sb("dtb", [C64, H])
            self.gon = self.sb("gon", [128, 1])
            self.ps_lin_i = self.ps_aux_i = self.tmp_i = 0

            with contextlib.ExitStack() as pst:
                self.st = pst
                self.scr_glu = self.sb("glub", [128, 4, CW - 1 + TT])
                self.chalo = self.sb("chalo", [128, KC, CW - 1])
                self.qhalo = self.sb("qhalo", [128, QC, SC - 1])
                self.Sst = self.sb("Sst", [128, H, 128])
                self.cvb = [[self.sb(f"cvb{a}{b}", [128, SC - 1 + TT]) for b in range(2)] for a in range(2)]
                class _NS:
                    pass
                self.Tb = []
                for p_ in range(2):
                    T = _NS()
                    sfx = f"_{p_}"
                    T.knb = self.sb("knb" + sfx, [128, TT], BF16)
                    T.qnb = self.sb("qnb" + sfx, [128, TT], BF16)
                    T.qrb = self.sb("qrb" + sfx, [128, TT], BF16)
                    T.t_G = self.sb("t_G" + sfx, [C64, NCH])
                    T.t_alast = self.sb("t_alast" + sfx, [128, NCH])
                    T.t_kdsc = self.sb("t_kdsc" + sfx, [C64, NCH])
                    T.t_eG = self.sb("t_eG" + sfx, [C64, NCH])
                    T.t_nbe = self.sb("t_nbe" + sfx, [C64, NCH])
                    T.t_eGrow = self.sb("t_eGrow" + sfx, [128, TT])
                    T.t_X = self.sb("t_X" + sfx, [C64, NCH, C64])
                    T.t_D = self.sb("t_D" + sfx, [C64, NCH, C64])
                    T.t_DT = self.sb("t_DT" + sfx, [C64, NCH, C64])
                    T.t_attT = self.sb("t_attT" + sfx, [C64, NCH, C64], BF16)
                    T.t_N = [self.sb(f"t_N{i}" + sfx, [C64, NCH, C64]) for i in range(2)]
                    T.t_NT = [self.sb(f"t_NT{i}" + sfx, [C64, NCH, C64]) for i in range(2)]
                    T.t_PT = [self.sb(f"t_PT{i}" + sfx, [C64, NCH, C64]) for i in range(1)]
                    T.t_kd = self.sb("t_kd" + sfx, [C64, NCH, 128], BF16)
                    T.t_bv = self.sb("t_bv" + sfx, [C64, NCH, 128])
                    T.t_R = self.sb("t_R" + sfx, [C64, 128])
                    T.t_ub = self.sb("t_ub" + sfx, [C64, 128], BF16)
                    T.t_o = self.sb("t_o" + sfx, [C64, NCH, 128])
                    T.t_oss = self.sb("t_oss" + sfx, [C64, NCH])
                    T.Sbf = self.sb("Sbf" + sfx, [128, 128], BF16)
                    self.Tb.append(T)

                self.plan = []
                S.dry = True
                self.tile(0, False)
                S.dry = False
                NP = len(self.plan)
                ntiles = c.NT + (1 if NS > 0 else 0)
                self.wtotal = NP * ntiles
                self.wpos = 0
                self.wissued = 0
                self.wcache = []
                for i0 in range(0, NP, 64):
                    n = min(64, NP - i0)
                    wc = nc.dram_tensor(f"wcache{i0 // 64}", [n, 128, 32 * 256], BF16, kind="Internal").ap()
                    self.wcache.extend(wc[i] for i in range(n))

                S.op('dve', lambda e: e.memset(self.ones_f[:, :], 1.0), writes=['const'])
                lst = [(self.ident_f, 'c_ident'), (self.tri_f, 'c_tri'), (self.m_up_t, 'c_mup'),
                       (self.m_lo_t, 'c_mlo'), (self.m_strict_t, 'c_mstrict'), (self.gmix, 'gmix'),
                       (self.gffn, 'gffn'), (self.gple, 'gple'), (self.gfin, 'gfin'), (self.cwdw, 'cwdw'),
                       (self.cbdw, 'cbdw'), (self.clng, 'clng'), (self.clnb, 'clnb'), (self.gwc, 'gwc'),
                       (self.nA, 'alog'), (self.dtb, 'dtb'), (self.gon, 'gon')]
                for i, (t, n) in enumerate(lst):
                    S.dma('sp', f'cst{i % 4}', (lambda e, t=t, n=n: e.dma_start(out=t[:], in_=d_vec[n])), writes=[('cst', i)])
                S.op('act', lambda e: e.activation(out=self.nA[:, :], in_=self.nA[:, :], func=AF.Exp),
                     reads=[('cst', i) for i in range(len(lst))], writes=['const'])
                S.op('dve', lambda e: e.tensor_scalar(out=self.nA[:, :], in0=self.nA[:, :], scalar1=-1.0, scalar2=None,
                                                      op0=ALU.mult), reads=['const'], writes=['const'])
                S.op('dve', lambda e: e.memset(self.chalo[:, :, :], 0.0), writes=[('chalo', k) for k in range(KC)])
                S.op('dve', lambda e: e.memset(self.qhalo[:, :, :], 0.0), writes=[('qhalo', k) for k in range(QC)])
                S.op('dve', lambda e: e.memset(self.Sst[:, :, :], 0.0), writes=[('S', k) for k in range(H)])

                for ti in range(c.NT):
                    self.tile(ti, False)
                S.dma('pool', 'po0', (lambda e: e.dma_start(out=self.d_conf_p, in_=self.chalo[:, :, :])),
                      reads=[('chalo', k) for k in range(KC)], writes=['d_conf_p'])
                S.dma('pool', 'po1', (lambda e: e.dma_start(out=self.d_qkv_p, in_=self.qhalo[:, :, :])),
                      reads=[('qhalo', k) for k in range(QC)], writes=['d_qkv_p'])
                S.dma('pool', 'po2', (lambda e: e.dma_start(out=self.d_gdn_p, in_=self.Sst[:, :, :])),
                      reads=[('S', k) for k in range(H)], writes=['d_gdn_p'])
                for e_ in Sched.ENG:
                    S.wait_all(e_)
                S.emit()

            if NS > 0:
                with contextlib.ExitStack() as sst:
                    self.st = sst
                    self.cs_st = [self.sb(f"cs_st{i}", [128, NS, CW - 1]) for i in range(1)]
                    self.cs_new = [self.sb(f"cs_new{i}", [128, NS, CW - 1]) for i in range(1)]
                    self.cs_prod = [self.sb(f"cs_prod{i}", [128, NS, CW - 1]) for i in range(1)]
                    self.c_c = [self.sb(f"c_c{i}", [128, NS]) for i in range(1)]
                    self.qs_st = [self.sb(f"qs_st{i}", [128, 3, 2, NS, SC - 1]) for i in range(1)]
                    self.qs_new = [self.sb(f"qs_new{i}", [128, 3, 2, NS, SC - 1]) for i in range(1)]
                    self.qs_prod = self.sb("qs_prod", [128, NS, SC - 1])
                    self.s_a = self.sb("s_a", [NS, H])
                    self.s_bd = self.sb("s_bd", [NS, 2, H])
                    self.s_ab = self.sb("s_ab", [128, 2, H])
                    self.s_S = self.sb("s_S", [128, H, 128])
                    self.s_q = self.sb("s_q", [128, H, NS])
                    self.s_k = self.sb("s_k", [128, H, NS])
                    self.s_v = self.sb("s_v", [128, H, NS])
                    self.s_z = self.sb("s_z", [128, H, NS])
                    self.s_o = self.sb("s_o", [128, H, NS])
                    self.s_r = self.sb("s_r", [128, H])
                    self.s_rows = self.sb("s_rows", [H, 256])
                    self.s_rbd = [self.sb(f"s_rbd{i}", [H, 2, 128]) for i in range(2)]
                    self.tile(0, True)
                    assert self.wpos == self.wtotal, (self.wpos, self.wtotal)
                    S.wait_all('sp')
                    S.emit()
        return nc


def _fm(x):
    T, C = x.shape
    return np.ascontiguousarray(x.reshape(T, C // 128, 128).transpose(2, 1, 0))


def _fm_inv(y):
    p, kc, T = y.shape
    return np.ascontiguousarray(y.transpose(2, 1, 0).reshape(T, kc * 128))


def _vec(v):
    lead = v.shape[:-1]
    C = v.shape[-1]
    r = v.reshape(lead + (C // 128, 128))
    return np.ascontiguousarray(np.moveaxis(r, -1, 0))


def make_consts():
    p = np.arange(C64)[:, None]
    x = np.arange(C64)[None, :]
    return dict(
        c_ident=np.eye(128, dtype=np.float32),
        c_tri=(p <= x).astype(np.float32),
        c_mup=np.where(x > p, NEG, 0.0).astype(np.float32),
        c_mlo=np.where(x < p, NEG, 0.0).astype(np.float32),
        c_mstrict=(x < p).astype(np.float32),
    )


def run(cfg, inp, nseq_cores=None):
    c = cfg
    NC = c.NCORES
    B = inp['x_prompt'].shape[0]
    NS = c.NS
    prog = Prog(c)
    nc = prog.build()
    f = lambda a: np.ascontiguousarray(np.asarray(a, dtype=np.float32))
    shared = dict(
        gmix=_vec(f(inp['g_mix'])), gffn=_vec(f(inp['g_ffn'])), gple=_vec(f(inp['g_ple'])),
        gfin=_vec(f(inp['g_final'])[None]),
        cwdw=np.ascontiguousarray(_vec(f(inp['conf_w_dw'][0])).transpose(0, 2, 1)),
        cbdw=_vec(f(inp['conf_b_dw'][0])), clng=_vec(f(inp['conf_ln_g'][0])), clnb=_vec(f(inp['conf_ln_b'][0])),
        gwc=np.ascontiguousarray(_vec(f(inp['gdn_w_conv'][0])).transpose(0, 2, 1)),
        alog=np.ascontiguousarray(np.broadcast_to(f(inp['gdn_a_log'][0])[None, :], (C64, c.H))),
        dtb=np.ascontiguousarray(np.broadcast_to(f(inp['gdn_dt_bias'][0])[None, :], (C64, c.H))),
        gon=np.ascontiguousarray(f(inp['gdn_g_onorm'][0])[:, None]),
    )
    shared.update(make_consts())
    for n in ('conf_w_pw1', 'conf_w_pw2', 'gdn_w_in', 'gdn_w_out', 'ffn_w_gate', 'ffn_w_up', 'ffn_w_down', 'ple_w_gate',
              'ple_w_proj'):
        shared[n] = f(inp[n])
    xp, xs = f(inp['x_prompt']), f(inp['x_sample'])
    pp, psm = f(inp['p_prompt']), f(inp['p_sample'])
    scc, scq, sg = f(inp['state_conv_conformer']), f(inp['state_conv_qkv']), f(inp['state_gdn'])
    in_maps = []
    ACT = c.ACTIVE
    assert len(ACT) >= B and NS * len(ACT) == xs.shape[0]
    zero_map = None
    for ci in range(NC):
        if ci not in ACT:
            if zero_map is None:
                zero_map = {k: np.zeros_like(v) for k, v in in_maps[0].items()}
            in_maps.append(zero_map)
            continue
        a = ACT.index(ci)
        sq = a % B
        sl = slice(a * NS, (a + 1) * NS)
        m = dict(shared)
        m['xpT'] = _fm(xp[sq])
        m['xsT'] = _fm(xs[sl, 0])
        for l in range(2):
            m[f'ppT{l}'] = _fm(pp[l, sq])
            m[f'psT{l}'] = _fm(psm[l, sl, 0])
        m['cs_in'] = np.ascontiguousarray(scc[0, sl].reshape(NS, CW - 1, c.KC, 128).transpose(3, 2, 0, 1))
        m['qs_in'] = np.ascontiguousarray(scq[0, sl].reshape(NS, SC - 1, c.QC, 128).transpose(3, 2, 0, 1))
        m['gs_in'] = np.ascontiguousarray(sg[0, sl].transpose(0, 2, 1, 3))
        in_maps.append(m)
    res = run_bass_kernel_spmd(nc, in_maps, core_ids=list(range(NC)))
    R = [res.results[ci] for ci in c.ACTIVE]
    NC = len(c.ACTIVE)
    D = c.D
    y_p = np.stack([_fm_inv(R[b]['ypT']) for b in range(B)])
    y_s = np.concatenate([_fm_inv(R[ci]['ysT']) for ci in range(NC)])[:, None, :]
    conf_p = np.stack([_fm_inv(R[b]['conf_p']) for b in range(B)])[None]
    qkv_p = np.stack([_fm_inv(R[b]['qkv_p']) for b in range(B)])[None]
    gdn_p = np.stack([np.ascontiguousarray(R[b]['gdn_p'].transpose(1, 0, 2)) for b in range(B)])[None]
    conf_s = np.concatenate([np.ascontiguousarray(R[ci]['conf_s'].transpose(2, 3, 1, 0)).reshape(NS, CW - 1, D)
                             for ci in range(NC)])[None]
    qkv_s = np.concatenate([np.ascontiguousarray(R[ci]['qkv_s'].transpose(2, 3, 1, 0)).reshape(NS, SC - 1, c.QKV)
                            for ci in range(NC)])[None]
    gdn_s = np.concatenate([np.ascontiguousarray(R[ci]['gdn_s'].transpose(0, 2, 1, 3)) for ci in range(NC)])[None]
    return (y_p, y_s, conf_p, qkv_p, gdn_p, conf_s, qkv_s, gdn_s)


def kernel(**inputs):
    cfg = Cfg(NS=32, ACTIVE=(0, 1, 4, 5))
    return run(cfg, inputs)
```
